# Optimizing a Trainium2 kernel written in Bass

```python
import jax, jax.numpy as jnp
from jax import lax
import numpy as np

D_MODEL = 1024
BATCH = 8
SEQ = 4096
DEPTH = 4

GRID_W = 64
CTX_LEN = 256
EPS = 1e-6

FNET_GROUPS = 4
FNET_GROUP_DIM = 128
FNET_WIDTH = FNET_GROUPS * FNET_GROUP_DIM
HG_HEADS = 4
HG_DK = 128
HG_DV = 128
HG_KW = HG_HEADS * HG_DK
HG_VW = HG_HEADS * HG_DV
HG_CHUNK = 64
AB_SPLITS = [FNET_WIDTH, FNET_WIDTH + HG_KW, FNET_WIDTH + 2 * HG_KW, FNET_WIDTH + 3 * HG_KW,
             FNET_WIDTH + 3 * HG_KW + HG_VW]
AB_IN = FNET_WIDTH + 3 * HG_KW + 2 * HG_VW
AB_OUT = FNET_WIDTH + HG_VW
ATT_HEADS = 8
ATT_KV_HEADS = 2
ATT_GROUP = ATT_HEADS // ATT_KV_HEADS
HEAD_DIM = 128
ATT_WIDTH = ATT_HEADS * HEAD_DIM
ATT_KV_WIDTH = ATT_KV_HEADS * HEAD_DIM
ATT_IN = ATT_WIDTH + 2 * ATT_KV_WIDTH
Q_BLOCK = 128
ROPE_THETA = 10000.0
D_FF = 2816
CONV_K = 3

N_EVEN = (DEPTH + 1) // 2
N_ODD = DEPTH // 2

kernel_name = "hybrid_fourier_hgrn2_gqa_prefix_dit"


def rms_norm(x, gain=None):
    x32 = x.astype(jnp.float32)
    y = x32 * lax.rsqrt(jnp.mean(x32 * x32, axis=-1, keepdims=True) + EPS)
    if gain is not None:
        y = y * gain.astype(jnp.float32)
    return y.astype(x.dtype)


def modulate(x, shift, scale):
    return rms_norm(x) * (1 + scale) + shift


def modulation(cond, w_mod, b_mod):
    m = jax.nn.silu(cond) @ w_mod + b_mod
    return jnp.split(m, 6, axis=-1)


def axial_rope(x):
    L = x.shape[1]
    t = jnp.arange(L)
    row = (t // GRID_W).astype(jnp.float32)
    col = (t % GRID_W).astype(jnp.float32)
    n_freq = HEAD_DIM // 4
    freqs = ROPE_THETA ** (-jnp.arange(n_freq, dtype=jnp.float32) / n_freq)
    ang = jnp.concatenate([row[:, None] * freqs, col[:, None] * freqs], axis=-1)
    cos = jnp.cos(ang)[None, :, None, :]
    sin = jnp.sin(ang)[None, :, None, :]
    xp = x.astype(jnp.float32).reshape(*x.shape[:-1], HEAD_DIM // 2, 2)
    x0, x1 = xp[..., 0], xp[..., 1]
    out = jnp.stack([x0 * cos - x1 * sin, x0 * sin + x1 * cos], axis=-1).reshape(x.shape)
    return out.astype(x.dtype)


def block_attention(q, k, v):
    B, Lq = q.shape[:2]
    nb = Lq // Q_BLOCK
    qb = q.reshape(B, nb, Q_BLOCK, ATT_KV_HEADS, ATT_GROUP, HEAD_DIM).transpose(1, 0, 2, 3, 4, 5)
    scale = HEAD_DIM ** -0.5

    def one_block(qi):
        s = jnp.einsum('bqhgd,bkhd->bhgqk', qi, k, preferred_element_type=jnp.float32) * scale
        p = jax.nn.softmax(s, axis=-1)
        return jnp.einsum('bhgqk,bkhd->bqhgd', p.astype(v.dtype), v)

    o = lax.map(one_block, qb)
    return o.transpose(1, 0, 2, 3, 4, 5).reshape(B, Lq, ATT_WIDTH)


def gla_chunk_scan(q, k, v, log_f, s0):
    B, L, H, _ = q.shape
    DV = v.shape[-1]
    n = L // HG_CHUNK

    def to_chunks(t):
        return t.reshape(B, n, HG_CHUNK, H, t.shape[-1]).transpose(1, 0, 3, 2, 4)

    lower = jnp.tril(jnp.ones((HG_CHUNK, HG_CHUNK), dtype=bool))[None, None, :, :, None]

    def step(S, inp):
        qc, kc, vc, lfc = inp
        b = jnp.cumsum(lfc, axis=2)
        inter = jnp.einsum('bhtk,bhkv->bhtv', qc * jnp.exp(b), S)
        rel = jnp.where(lower, b[:, :, :, None, :] - b[:, :, None, :, :], -jnp.inf)
        att = jnp.einsum('bhtk,bhtsk,bhsk->bhts', qc, jnp.exp(rel), kc)
        o = inter + jnp.einsum('bhts,bhsv->bhtv', att, vc)
        b_last = b[:, :, -1:, :]
        S = jnp.exp(b_last[:, :, 0, :, None]) * S + jnp.einsum('bhsk,bhsv->bhkv', kc * jnp.exp(b_last - b), vc)
        return S, o

    S, o = lax.scan(step, s0, (to_chunks(q), to_chunks(k), to_chunks(v), to_chunks(log_f)))
    return o.transpose(1, 0, 3, 2, 4).reshape(B, L, H, DV), S


def fourier_mix(a):
    B, L, _ = a.shape
    ag = a.astype(jnp.float32).reshape(B, L, FNET_GROUPS, FNET_GROUP_DIM)
    y = jnp.fft.fft2(ag, axes=(1, 3), norm='ortho').real
    return y.reshape(B, L, FNET_WIDTH).astype(a.dtype)


def fourier_hgrn_mixer(h_lat, h_ctx, w_in, w_out, lb, gn_gain, need_ctx):
    B = h_lat.shape[0]

    def parts(h):
        p = h @ w_in
        a, q, zf0, zf1, i, g = jnp.split(p, AB_SPLITS, axis=-1)
        heads = lambda t, d: t.reshape(*t.shape[:2], HG_HEADS, d).astype(jnp.float32)
        return a, heads(jax.nn.silu(q), HG_DK), (heads(zf0, HG_DK), heads(zf1, HG_DK)), heads(i, HG_DV), g

    a_l, q_l, zf_l, v_l, g_l = parts(h_lat)
    a_c, q_c, zf_c, v_c, g_c = parts(h_ctx)

    def forget(z, lb_d):
        f = lb_d + (1.0 - lb_d) * jax.nn.sigmoid(z)
        return jnp.log(f), 1.0 - f

    outs_l, outs_c = [], []
    for d in range(2):
        lb_d = lb[d].reshape(HG_HEADS, HG_DK).astype(jnp.float32)
        lf_l, k_l = forget(zf_l[d], lb_d)
        lf_c, k_c = forget(zf_c[d], lb_d)
        seq_c = [q_c, k_c, v_c, lf_c]
        seq_l = [q_l, k_l, v_l, lf_l]
        if d == 1:
            seq_c = [jnp.flip(t, axis=1) for t in seq_c]
            seq_l = [jnp.flip(t, axis=1) for t in seq_l]
        s0 = jnp.zeros((B, HG_HEADS, HG_DK, HG_DV), jnp.float32)
        oc, s_ctx = gla_chunk_scan(*seq_c, s0)
        ol, _ = gla_chunk_scan(*seq_l, s_ctx)
        if d == 1:
            oc, ol = jnp.flip(oc, axis=1), jnp.flip(ol, axis=1)
        outs_c.append(oc)
        outs_l.append(ol)

    def gated_out(o, g):
        o = rms_norm(o, gn_gain.reshape(HG_HEADS, HG_DV))
        return (o.reshape(*o.shape[:2], HG_VW) * jax.nn.silu(g.astype(jnp.float32))).astype(g.dtype)

    y_lat = jnp.concatenate([fourier_mix(a_l), gated_out(outs_l[0] + outs_l[1], g_l)], axis=-1) @ w_out
    y_ctx = None
    if need_ctx:
        y_ctx = jnp.concatenate([fourier_mix(a_c), gated_out(outs_c[0] + outs_c[1], g_c)], axis=-1) @ w_out
    return y_lat, y_ctx


def attention_mixer(h_lat, h_ctx, w_qkv, qn_g, kn_g, w_out, need_ctx):
    def qkv(h, rope):
        B, L, _ = h.shape
        p = h @ w_qkv
        q, k, v = jnp.split(p, [ATT_WIDTH, ATT_WIDTH + ATT_KV_WIDTH], axis=-1)
        q = rms_norm(q.reshape(B, L, ATT_HEADS, HEAD_DIM), qn_g)
        k = rms_norm(k.reshape(B, L, ATT_KV_HEADS, HEAD_DIM), kn_g)
        v = v.reshape(B, L, ATT_KV_HEADS, HEAD_DIM)
        if rope:
            q, k = axial_rope(q), axial_rope(k)
        return q, k, v

    q_l, k_l, v_l = qkv(h_lat, True)
    q_c, k_c, v_c = qkv(h_ctx, False)
    o_l = block_attention(q_l, jnp.concatenate([k_l, k_c], axis=1), jnp.concatenate([v_l, v_c], axis=1))
    y_lat = o_l @ w_out
    y_ctx = None
    if need_ctx:
        y_ctx = block_attention(q_c, k_c, v_c) @ w_out
    return y_lat, y_ctx


def conv_ffn(h, w_up, conv_w, conv_b, w_down, on_grid):
    B, L, _ = h.shape
    gate, val = jnp.split(h @ w_up, [D_FF], axis=-1)
    w = conv_w.astype(h.dtype)
    if on_grid:
        rows = L // GRID_W
        g2 = gate.reshape(B, rows, GRID_W, D_FF)
        conv = lax.conv_general_dilated(g2, w[:, :, None, :], (1, 1), 'SAME',
                                        dimension_numbers=('NHWC', 'HWIO', 'NHWC'),
                                        feature_group_count=D_FF).reshape(B, L, D_FF)
    else:
        conv = lax.conv_general_dilated(gate, w[CONV_K // 2][:, None, :], (1,), 'SAME',
                                        dimension_numbers=('NWC', 'WIO', 'NWC'),
                                        feature_group_count=D_FF)
    act = jax.nn.silu(conv + conv_b.astype(h.dtype)) * val
    return act @ w_down


def setup_inputs(seed: int = 0) -> dict:
    key = jax.random.key(seed)
    ks = jax.random.split(key, 20)
    nrm = lambda k, shape, scale: jax.random.normal(k, shape, jnp.float32) * scale
    return {
        "x": nrm(ks[0], (BATCH, SEQ, D_MODEL), 1.0),
        "c": nrm(ks[1], (BATCH, D_MODEL), 1.0),
        "ctx": nrm(ks[2], (BATCH, CTX_LEN, D_MODEL), 1.0),
        "c_ctx": nrm(ks[3], (D_MODEL,), 1.0),
        "w_mod": nrm(ks[4], (DEPTH, D_MODEL, 6 * D_MODEL), D_MODEL ** -0.5),
        "b_mod": nrm(ks[5], (DEPTH, 6 * D_MODEL), 0.01),
        "w_in_ab": nrm(ks[6], (N_EVEN, D_MODEL, AB_IN), D_MODEL ** -0.5),
        "w_out_ab": nrm(ks[7], (N_EVEN, AB_OUT, D_MODEL), AB_OUT ** -0.5),
        "hg_lb_logits": nrm(ks[8], (N_EVEN, 2, HG_KW), 0.5),
        "hg_norm_g": 1.0 + nrm(ks[9], (N_EVEN, HG_VW), 0.02),
        "w_qkv": nrm(ks[10], (N_ODD, D_MODEL, ATT_IN), D_MODEL ** -0.5),
        "q_norm_g": 1.0 + nrm(ks[11], (N_ODD, HEAD_DIM), 0.02),
        "k_norm_g": 1.0 + nrm(ks[12], (N_ODD, HEAD_DIM), 0.02),
        "w_out_att": nrm(ks[13], (N_ODD, ATT_WIDTH, D_MODEL), ATT_WIDTH ** -0.5),
        "w_up": nrm(ks[14], (DEPTH, D_MODEL, 2 * D_FF), D_MODEL ** -0.5),
        "conv_w": nrm(ks[15], (DEPTH, CONV_K, CONV_K, D_FF), 1.0 / CONV_K),
        "conv_b": nrm(ks[16], (DEPTH, D_FF), 0.01),
        "w_down": nrm(ks[17], (DEPTH, D_FF, D_MODEL), D_FF ** -0.5),
        "final_norm_g": 1.0 + nrm(ks[18], (D_MODEL,), 0.02),
    }


def reference(x, c, ctx, c_ctx, w_mod, b_mod, w_in_ab, w_out_ab, hg_lb_logits, hg_norm_g,
              w_qkv, q_norm_g, k_norm_g, w_out_att, w_up, conv_w, conv_b, w_down, final_norm_g):
    lbp = jax.nn.softmax(hg_lb_logits.astype(jnp.float32), axis=0)
    lower_bounds = jnp.cumsum(lbp, axis=0) - lbp[0]

    for layer in range(DEPTH):
        last = layer == DEPTH - 1
        sh1, sc1, g1, sh2, sc2, g2 = [t[:, None, :] for t in modulation(c, w_mod[layer], b_mod[layer])]
        ch1, cs1, cg1, ch2, cs2, cg2 = modulation(c_ctx, w_mod[layer], b_mod[layer])
        h_lat = modulate(x, sh1, sc1)
        h_ctx = modulate(ctx, ch1, cs1)
        if layer % 2 == 0:
            e = layer // 2
            y_lat, y_ctx = fourier_hgrn_mixer(h_lat, h_ctx, w_in_ab[e], w_out_ab[e], lower_bounds[e],
                                              hg_norm_g[e], not last)
        else:
            o = layer // 2
            y_lat, y_ctx = attention_mixer(h_lat, h_ctx, w_qkv[o], q_norm_g[o], k_norm_g[o],
                                           w_out_att[o], not last)
        x = x + g1 * y_lat
        x = x + g2 * conv_ffn(modulate(x, sh2, sc2), w_up[layer], conv_w[layer], conv_b[layer],
                              w_down[layer], True)
        if not last:
            ctx = ctx + cg1 * y_ctx
            ctx = ctx + cg2 * conv_ffn(modulate(ctx, ch2, cs2), w_up[layer], conv_w[layer], conv_b[layer],
                                       w_down[layer], False)
    return rms_norm(x, final_norm_g)
```

```python
import numpy as np
import concourse.bass as bass
import concourse.mybir as mybir

F32 = mybir.dt.float32
BF16 = mybir.dt.bfloat16
I32 = mybir.dt.int32
U8 = mybir.dt.uint8
AF = mybir.ActivationFunctionType
ALU = mybir.AluOpType

SEM_LIMIT = 30000


class Res:
    __slots__ = ("name", "w", "r")

    def __init__(self, name):
        self.name = name
        self.w = None
        self.r = {}


class Eng:
    def __init__(self, name, hw):
        self.name = name
        self.hw = hw
        self.ops = []
        self.sem = None
        self.count = 0
        self.seen = {}
        self.pending = []
        self.nsem = 0


class Sched:
    def __init__(self, nc, same_eng_sync=True):
        self.nc = nc
        self.same_eng_sync = same_eng_sync
        self.engs = {
            "pe": Eng("pe", nc.tensor),
            "act": Eng("act", nc.scalar),
            "dve": Eng("dve", nc.vector),
            "pool": Eng("pool", nc.gpsimd),
            "sp": Eng("sp", nc.sync),
        }
        self.semid = 0
        self.dma_slots = {}
        self.dma_rr = {}
        self.nres = 0
        self.barrier_exempt = set()

    def res(self, name=None):
        self.nres += 1
        return Res(name or f"r{self.nres}")

    def _newsem(self, tag):
        self.semid += 1
        s = self.nc.alloc_semaphore(name=f"s{self.semid}_{tag}")
        return (self.semid, s)

    def _deps(self, e, reads, writes):
        deps = []
        for r in reads:
            if r.w is not None:
                deps.append(r.w)
        for w in writes:
            if w.w is not None:
                deps.append(w.w)
            for ev in w.r.values():
                deps.append(ev)
        waits = []
        for ev in deps:
            key, sem, val, en = ev
            if en == e.name and (e.name == "pe" or not self.same_eng_sync):
                continue
            if e.seen.get(key, 0) >= val:
                continue
            e.seen[key] = val
            waits.append((sem, val))
        return waits

    def _mark(self, ev, reads, writes, en):
        for r in reads:
            r.r[en] = ev
        for w in writes:
            w.w = ev
            w.r = {}

    def op(self, en, fn, reads=(), writes=(), inc=True):
        e = self.engs[en]
        waits = self._deps(e, reads, writes)
        if inc:
            if e.sem is None or e.count >= SEM_LIMIT:
                e.sem = self._newsem(en)
                e.count = 0
            e.count += 1
            ev = (e.sem[0], e.sem[1], e.count, en)
            for (res, mode) in e.pending:
                if mode == "r":
                    res.r[en] = ev
                else:
                    res.w = ev
                    res.r = {}
            e.pending = []
            self._mark(ev, reads, writes, en)
            e.ops.append((waits, fn, (e.sem[1], 1)))
        else:
            for r in reads:
                e.pending.append((r, "r"))
            for w in writes:
                e.pending.append((w, "w"))
            e.ops.append((waits, fn, None))

    def dma(self, q, out, in_, reads=(), writes=(), nslots=8, **kw):
        e = self.engs[q]
        waits = self._deps(e, reads, writes)
        slots = self.dma_slots.setdefault(q, [])
        if len(slots) < nslots:
            slots.append([self._newsem("dma" + q), 0])
            si = len(slots) - 1
        else:
            si = self.dma_rr.get(q, 0) % nslots
        self.dma_rr[q] = si + 1
        slot = slots[si]
        if 16 * (slot[1] + 1) > SEM_LIMIT:
            slot[0] = self._newsem("dma" + q)
            slot[1] = 0
        key, sem = slot[0]
        if slot[1] > 0 and e.seen.get(key, 0) < 16 * slot[1]:
            e.seen[key] = 16 * slot[1]
            waits.append((sem, 16 * slot[1]))
        slot[1] += 1
        ev = (key, sem, 16 * slot[1], "dma")
        self._mark(ev, reads, writes, "dma%d_%s" % (si, q))

        def fn(hw, out=out, in_=in_, kw=kw):
            return hw.dma_start(out=out, in_=in_, **kw)
        e.ops.append((waits, fn, (sem, 16)))
        return ev

    def barrier(self):
        evs = []
        for e in self.engs.values():
            assert not e.pending
            if e.sem is not None and e.count > 0:
                evs.append((e.sem[0], e.sem[1], e.count))
        for q, slots in self.dma_slots.items():
            if q in self.barrier_exempt:
                continue
            for slot in slots:
                if slot[1] > 0:
                    evs.append((slot[0][0], slot[0][1], 16 * slot[1]))
        for e in self.engs.values():
            waits = []
            for key, sem, val in evs:
                if e.seen.get(key, 0) >= val:
                    continue
                if e.sem is not None and key == e.sem[0]:
                    continue
                e.seen[key] = val
                waits.append((sem, val))
            e.ops.append((waits, None, None))

    def wait_all(self, en, resources):
        e = self.engs[en]
        waits = self._deps(e, list(resources), [])
        e.ops.append((waits, None, None))

    def emit(self):
        nc = self.nc
        for e in self.engs.values():
            assert not e.pending, f"engine {e.name} has pending non-inc ops at end"
        with nc.Block() as block:
            def run(e, hw):
                for waits, fn, inc in e.ops:
                    for (sem, val) in waits:
                        hw.wait_ge(sem, val)
                    if fn is None:
                        continue
                    ins = fn(hw)
                    if inc is not None:
                        ins.then_inc(inc[0], inc[1])

            @block.tensor
            def _(hw):
                run(self.engs["pe"], hw)

            @block.scalar
            def _(hw):
                run(self.engs["act"], hw)

            @block.vector
            def _(hw):
                run(self.engs["dve"], hw)

            @block.gpsimd
            def _(hw):
                run(self.engs["pool"], hw)

            @block.sync
            def _(hw):
                run(self.engs["sp"], hw)

    def stats(self):
        return {k: len(v.ops) for k, v in self.engs.items()}
from contextlib import ExitStack
import ml_dtypes
from concourse.bass_utils import run_bass_kernel_spmd

D = 1024
L = 4096
CT = 256
T = L + CT
DEPTH = 4
DFF = 2816
NFF = DFF // 128
EPS = 1e-6
GRID = 64
CH = 32
NCH = T // CH


class TL:
    def __init__(self, t, r):
        self.t = t
        self.r = r

    def __getitem__(self, k):
        return self.t[k]


class KB:
    def __init__(self, debug_outs=()):
        self.nc = bass.Bass("TRN2", target_bir_lowering=False)
        self.S = Sched(self.nc)
        self.dram = {}
        self.dres = {}
        self.debug_outs = set(debug_outs)
        self.nt = 0
        self.psum = []
        self.psi = 0
        self.wres = {}

    def din(self, name, shape, dt=F32):
        t = self.nc.dram_tensor(name, list(shape), dt, kind="ExternalInput")
        self.dram[name] = t
        return t

    def dscratch(self, name, shape, dt, out=False):
        kind = "ExternalOutput" if (out or name in self.debug_outs) else "Internal"
        t = self.nc.dram_tensor(name, list(shape), dt, kind=kind)
        self.dram[name] = t
        return t

    def dr(self, name, idx=0):
        k = (name, idx)
        if k not in self.dres:
            self.dres[k] = self.S.res("%s_%s" % (name, idx))
        return self.dres[k]

    def sb(self, name, shape, dt):
        self.nt += 1
        t = self.nc.alloc_sbuf_tensor("%s_%d" % (name, self.nt), list(shape), dt)
        return TL(t, self.S.res(name))

    def ring(self, name, n, shape, dt):
        return [self.sb("%s%d" % (name, i), shape, dt) for i in range(n)]

    def init_psum(self):
        for i in range(8):
            t = self.nc.alloc_psum_tensor("ps%d" % i, [128, 512], F32)
            self.psum.append(TL(t, self.S.res("ps%d" % i)))

    def bank(self):
        b = self.psum[self.psi % 8]
        self.psi += 1
        return b

    def mm(self, out, lhsT, rhs, start, stop, reads, writes, inc=None, **kw):
        if inc is None:
            inc = stop
        self.S.op("pe", lambda hw: hw.matmul(out, lhsT, rhs, start=start, stop=stop, **kw), reads, writes, inc)

    def tr(self, out, in_, ident, reads, writes, inc=True):
        self.S.op("pe", lambda hw: hw.transpose(out, in_, ident), reads, writes, inc)

    def act(self, out, in_, func, reads, writes, bias=None, scale=None, accum_out=None):
        kw = {}
        if bias is not None:
            kw["bias"] = bias
        if scale is not None:
            kw["scale"] = scale
        if accum_out is not None:
            kw["accum_out"] = accum_out
        self.S.op("act", lambda hw: hw.activation(out, in_, func, **kw), reads, writes)

    def tt(self, en, out, in0, in1, op, reads, writes):
        self.S.op(en, lambda hw: hw.tensor_tensor(out, in0, in1, op), reads, writes)

    def ts(self, en, out, in0, s1, op0, reads, writes, s2=None, op1=None):
        if op1 is None:
            self.S.op(en, lambda hw: hw.tensor_scalar(out, in0, s1, None, op0), reads, writes)
        else:
            self.S.op(en, lambda hw: hw.tensor_scalar(out, in0, s1, s2, op0, op1), reads, writes)

    def stt(self, out, in0, scalar, in1, op0, op1, reads, writes):
        self.S.op("dve", lambda hw: hw.scalar_tensor_tensor(out, in0, scalar, in1, op0, op1), reads, writes)

    def copy(self, en, out, in_, reads, writes):
        if en == "act":
            self.S.op(en, lambda hw: hw.copy(out, in_), reads, writes)
        else:
            self.S.op(en, lambda hw: hw.tensor_copy(out, in_), reads, writes)

    def memset(self, en, ap, val, writes):
        self.S.op(en, lambda hw: hw.memset(ap, val), (), writes)

    def load(self, out, in_, reads, writes, q="sp", **kw):
        return self.S.dma(q, out, in_, reads, writes, **kw)

    def setup_consts(self):
        nc = self.nc
        self.init_psum()
        self.c_ident = self.din("c_ident", [128, 128], BF16)
        self.ident = self.sb("ident", [128, 128], BF16)
        self.load(self.ident[:], self.c_ident.ap(), [], [self.ident.r])
        self.ones = self.sb("ones", [128, 128], BF16)
        self.memset("pool", self.ones[:], 1.0, [self.ones.r])

    def rms_rstd(self, xt, n, rstd, sq, width=D):
        nch = width // 128
        self.act(sq[:, 0:nch, 0:n], xt[:, 0:nch, 0:n], AF.Square, [xt.r], [sq.r])
        for n0 in range(0, n, 512):
            n1 = min(n, n0 + 512)
            bk = self.bank()
            for c in range(nch):
                self.mm(bk[:, 0:n1 - n0], self.ones[:], sq[:, c, n0:n1], c == 0, c == nch - 1,
                        [self.ones.r, sq.r], [bk.r])
            self.act(rstd[:, n0:n1], bk[:, 0:n1 - n0], AF.Ln, [bk.r], [rstd.r], bias=self.epsb[:, 0:1], scale=1.0 / width)
            self.act(rstd[:, n0:n1], rstd[:, n0:n1], AF.Exp, [rstd.r], [rstd.r], scale=-0.5)

    def xview(self, name):
        return self.dram[name].ap().rearrange("(c p) n -> p c n", p=128)

    def mod_setup(self):
        cT = self.din("cT", [128, 8, 2])
        bmodT = self.din("bmodT", [128, DEPTH, 48])
        self.wmod = self.din("w_mod", [DEPTH, D, 6 * D])
        self.epsb = self.sb("epsb", [128, 1], F32)
        self.memset("pool", self.epsb[:], EPS, [self.epsb.r])
        self.modsb = self.sb("modsb", [128, DEPTH, 48, 2], F32)
        self.modr = [self.S.res("mod%d" % l) for l in range(DEPTH)]
        csb = self.sb("csb", [128, 8, 2], F32)
        self.ssb = self.sb("ssb", [128, 8, 2], F32)
        self.bsb = self.sb("bsb", [128, DEPTH, 48], F32)
        self.identf = self.sb("identf", [2, 2], F32)
        self.load(csb[:], cT.ap(), [], [csb.r])
        self.load(self.bsb[:], bmodT.ap(), [], [self.bsb.r])
        self.act(self.ssb[:], csb[:], AF.Silu, [csb.r], [self.ssb.r])
        self.copy("dve", self.identf[:], self.ident[0:2, 0:2], [self.ident.r], [self.identf.r])

    def mod_tasks(self, l, bank_fn=None):
        bank_fn = bank_fn or self.bank
        NW = 3
        wring = self.pring("wmod", NW, [128, 8, 512], F32)
        mr = self.psb("mrow", [2, 6 * D], F32)
        wv = self.wmod.ap()[l].rearrange("(k p) n -> p k n", p=128)
        ssb, bsb = self.ssb, self.bsb
        loaded = set()

        def ld(pi):
            if pi < 12 and pi not in loaded:
                loaded.add(pi)
                w = wring[pi % NW]
                self.load(w[:], wv[:, :, pi * 512:(pi + 1) * 512], [], [w.r])

        def piece(pi):
            def f():
                for a_ in range(NW):
                    ld(pi + a_)
                w = wring[pi % NW]
                bk = bank_fn()
                for k in range(8):
                    self.mm(bk[0:2, 0:512], ssb[:, k, :], w[:, k, :], k == 0, k == 7, [w.r, ssb.r], [bk.r])
                self.copy("act", mr[0:2, pi * 512:(pi + 1) * 512], bk[0:2, 0:512], [bk.r], [mr.r])
            return f

        def fin():
            bk = bank_fn()
            for j in range(48):
                self.tr(bk[:, 2 * j:2 * j + 2], mr[0:2, j * 128:(j + 1) * 128], self.identf[0:2, 0:2], [mr.r, self.identf.r], [bk.r],
                        inc=(j == 47))
            self.tt("dve", self.modsb[:, l], bk[:, 0:96].rearrange("p (j t) -> p j t", t=2),
                    bsb[:, l, :].unsqueeze(2).broadcast_to([128, 48, 2]), ALU.add, [bk.r, bsb.r], [self.modr[l]])
            for sp in (1, 4):
                sl = self.modsb[:, l, sp * 8:(sp + 1) * 8, :]
                self.ts("dve", sl, sl, 1.0, ALU.add, [self.modr[l]], [self.modr[l]])
        def pre():
            for a_ in range(NW - 1):
                ld(a_)
        return [pre] + [piece(pi) for pi in range(12)] + [fin]

    def phase_mod0(self):
        self.begin_phase()
        for t in self.mod_tasks(0):
            t()
        self.end_phase()

    def mod(self, l, split, c, col):
        return self.modsb[:, l, split * 8 + c, col:col + 1]

    def phase_final(self, xa):
        gT = self.din("fngT", [128, 8])
        gsb = self.sb("fng", [128, 8], F32)
        self.load(gsb[:], gT.ap(), [], [gsb.r])
        outT = self.dscratch("outT", [D, L], F32, out=True)
        xv = self.xview(xa)
        ov = self.xview("outT")
        xr = self.ring("fx", 2, [128, 8, 512], F32)
        sq = self.ring("fsq", 2, [128, 8, 512], BF16)
        rs = self.ring("frs", 2, [128, 512], F32)
        yr = self.ring("fy", 2, [128, 8, 512], F32)
        for i in range(L // 512):
            x = xr[i % 2]
            self.load(x[:], xv[:, :, i * 512:(i + 1) * 512], [self.dr(xa, i)], [x.r])
            self.rms_rstd(x, 512, rs[i % 2], sq[i % 2])
            r = rs[i % 2]
            y = yr[i % 2]
            for c in range(8):
                if c < 5:
                    self.stt(y[:, c, :], x[:, c, :], gsb[:, c:c + 1], r[:], ALU.mult, ALU.mult, [x.r, r.r, gsb.r], [y.r])
                else:
                    self.tt("pool", y[:, c, :], x[:, c, :], r[:], ALU.mult, [x.r, r.r], [y.r])
                    self.act(y[:, c, :], y[:, c, :], AF.Identity, [y.r, gsb.r], [y.r], scale=gsb[:, c:c + 1])
            self.load(ov[:, :, i * 512:(i + 1) * 512], y[:], [y.r], [self.dr("outT", i)])
        self.out_res = [self.dr("outT", i) for i in range(L // 512)]

    def begin_phase(self):
        self.pstack = ExitStack()

    def psb(self, name, shape, dt):
        self.nt += 1
        t = self.pstack.enter_context(self.nc.sbuf_tensor("%s_%d" % (name, self.nt), list(shape), dt))
        return TL(t, self.S.res(name))

    def pring(self, name, n, shape, dt):
        return [self.psb("%s%d" % (name, i), shape, dt) for i in range(n)]

    def end_phase(self):
        self.S.barrier()
        self.pstack.close()

    def phase_wcast(self, specs, order):
        self.wcast_setup(specs)
        self.wcast_issue(order)

    def wcast_setup(self, specs):
        self.S.barrier_exempt.add("pool")
        self.wc = {}
        for name, shape in specs:
            src = self.din(name, shape)
            dst = self.dscratch("wb_" + name, shape, BF16)
            per = int(np.prod(shape[1:]))
            assert per % 1024 == 0
            self.wc[name] = (src, dst, per // 1024)

    def wcast_tasks(self, order):
        tasks = []
        for name, l in order:
            src, dst, rows = self.wc[name]
            sv = src.ap()[l].rearrange("a b -> (a b)").rearrange("(r n) -> r n", n=1024)
            dv = dst.ap()[l].rearrange("a b -> (a b)").rearrange("(r n) -> r n", n=1024)
            rl = []
            self.wres[(name, l)] = rl
            for r0 in range(0, rows, 1024):
                r1 = min(rows, r0 + 1024)
                rr = self.S.res("wb")
                rl.append(rr)
                tasks.append(lambda dv=dv, sv=sv, r0=r0, r1=r1, rr=rr: self.load(dv[r0:r1, :], sv[r0:r1, :], [], [rr], q="pool"))
        return tasks

    def wcast_issue(self, order):
        for t in self.wcast_tasks(order):
            t()

    def wb(self, name, l):
        return self.dram["wb_" + name].ap()[l], self.wres[(name, l)]

    def norm_mod(self, xt, n0, n1, rstd, h, l, sh_split, sc_split, col):
        n = n1 - n0
        self.act(h[:, :, n0:n1], xt[:, :, n0:n1], AF.Square, [xt.r], [h.r])
        for a in range(n0, n1, 512):
            b_ = min(n1, a + 512)
            bk = self.bank()
            for c in range(8):
                self.mm(bk[:, 0:b_ - a], self.ones[:], h[:, c, a:b_], c == 0, c == 7, [self.ones.r, h.r], [bk.r])
            self.act(rstd[:, a:b_], bk[:, 0:b_ - a], AF.Ln, [bk.r], [rstd.r], bias=self.epsb[:, 0:1], scale=1.0 / D)
            self.act(rstd[:, a:b_], rstd[:, a:b_], AF.Exp, [rstd.r], [rstd.r], scale=-0.5)
        for c in range(8):
            self.stt(h[:, c, n0:n1], xt[:, c, n0:n1], self.mod(l, sc_split, c, col), rstd[:, n0:n1], ALU.mult, ALU.mult,
                     [xt.r, rstd.r, self.modr[l]], [h.r])
            self.ts("pool", h[:, c, n0:n1], h[:, c, n0:n1], self.mod(l, sh_split, c, col), ALU.add, [h.r, self.modr[l]], [h.r],
                    s2=1.0, op1=ALU.mult)

    def phase_out(self, l, wname, widx, xin, xout):
        self.begin_phase()
        tasks = []
        wsrc, wres = self.wb(wname, widx)
        W = self.psb("wout", [128, 8, D], BF16)
        self.load(W[:], wsrc.rearrange("(k p) n -> p k n", p=128), wres, [W.r])
        mv = self.xview("mixT")
        xv = self.xview(xin)
        ov = self.xview(xout)
        mr = self.pring("omix", 2, [128, 8, 512], BF16)
        xr = self.pring("ox", 2, [128, 8, 512], F32)
        for i in range(9):
            n = 512 if i < 8 else CT
            col = 0 if i < 8 else 1
            t0 = i * 512
            m = mr[i % 2]
            x = xr[i % 2]
            self.load(m[:, :, 0:n], mv[:, :, t0:t0 + n], [self.dr("mixT", i)], [m.r])
            self.load(x[:, :, 0:n], xv[:, :, t0:t0 + n], [self.dr(xin, i)], [x.r])
            for c in range(8):
                bk = self.bank()
                for k in range(8):
                    self.mm(bk[:, 0:n], W[:, k, c * 128:(c + 1) * 128], m[:, k, 0:n], k == 0, k == 7, [W.r, m.r], [bk.r])
                self.stt(x[:, c, 0:n], bk[:, 0:n], self.mod(l, 2, c, col), x[:, c, 0:n], ALU.mult, ALU.add,
                         [bk.r, x.r, self.modr[l]], [x.r])
            self.load(ov[:, :, t0:t0 + n], x[:, :, 0:n], [x.r], [self.dr(xout, i)])
            for _ in range(2):
                if tasks:
                    tasks.pop(0)()
        while tasks:
            tasks.pop(0)()
        self.end_phase()

    def phase_ffn(self, l, xin, xout, bg=None):
        self.begin_phase()
        bg = list(bg or [])
        wd_src, wd_res = self.wb("w_down", l)
        wu_src, wu_res = self.wb("w_up", l)
        wuv = wu_src.rearrange("(k p) (g n) -> p k g n", p=128, g=2)
        wdv = wd_src.rearrange("(c p) n -> p c n", p=128)
        cw = self.psb("convw", [128, NFF, 9], F32)
        cb = self.psb("convb", [128, NFF], F32)
        self.load(cw[:], self.dram["convwT"].ap()[l], [], [cw.r])
        self.load(cb[:], self.dram["convbT"].ap()[l], [], [cb.r])
        xt = self.psb("fx", [128, 8, 1152], F32)
        h = self.psb("fh", [128, 8, 1152], BF16)
        rstd = self.psb("frstd", [128, 1152], F32)
        actT = self.psb("factT", [128, NFF, 1024], BF16)
        wur = self.pring("fwu", 3, [128, 8, 2, 256], BF16)
        Wd = self.psb("wdown", [128, NFF, D], BF16)
        self.load(Wd[:], wdv, wd_res, [Wd.r])
        Gr = self.pring("fG", 2, [128, 18 * 66], BF16)
        sr = self.pring("fsilu", 2, [128, 512], F32)
        vr = self.pring("fval", 2, [128, 1024], BF16)
        dgr = self.pring("fdiag", 2, [128, 9, 128], BF16)
        xrr = self.pring("fxr", 4, [128, 512], F32)
        for G in Gr:
            self.memset("pool", G[:], 0.0, [G.r])
        xv = self.xview(xin)
        ov = self.xview(xout)

        def geom(band):
            if band < 4:
                lo = 64 if band > 0 else 0
                hi = 64 if band < 3 else 0
                return dict(ctx=False, t0=band * 1024, lo=lo, hi=hi, ncol=1152, cen0=64, ncen=1024, col=0)
            return dict(ctx=True, t0=L, lo=0, hi=0, ncol=CT, cen0=0, ncen=CT, col=1)

        def prologue(band):
            g = geom(band)
            if not g["ctx"]:
                t0, lo, hi = g["t0"], g["lo"], g["hi"]
                rd = [self.dr(xin, i) for i in range(max(0, 2 * band - 1), min(8, 2 * band + 3))]
                if lo == 0:
                    self.memset("pool", xt[:, :, 0:64], 0.0, [xt.r])
                if hi == 0:
                    self.memset("pool", xt[:, :, 1088:1152], 0.0, [xt.r])
                self.load(xt[:, :, 64 - lo:64 + 1024 + hi], xv[:, :, t0 - lo:t0 + 1024 + hi], rd, [xt.r])
            else:
                self.load(xt[:, :, 0:CT], xv[:, :, L:L + CT], [self.dr(xin, 8)], [xt.r])
            self.norm_mod(xt, 0, g["ncol"], rstd, h, l, 3, 4, g["col"])

        wu_it = [0]
        wu_of = {}

        def load_wu(band, c):
            if c < NFF and (band, c) not in wu_of:
                wu = wur[wu_it[0] % 3]
                wu_it[0] += 1
                for g_ in range(2):
                    self.load(wu[:, :, g_, :], wuv[:, :, g_, c * 128:(c + 2) * 128], wu_res, [wu.r])
                wu_of[(band, c)] = wu
                wu_of[(band, c + 1)] = wu

        prologue(0)
        load_wu(0, 0)
        wd_it = 0
        xr_it = 0
        for band in range(5):
            g = geom(band)
            ctxb, lo, hi, t0 = g["ctx"], g["lo"], g["hi"], g["t0"]
            cen0, ncen, col = g["cen0"], g["ncen"], g["col"]
            nob = 2 if not ctxb else 1
            held = {}

            def st0(c, k):
                if bg:
                    bg.pop(0)()
                if c % 2 == 0:
                    load_wu(band, c + 2)
                wu = wu_of[(band, c)]
                wo = (c % 2) * 128
                G = Gr[c % 2]
                dg = dgr[c % 2]
                val = vr[c % 2]
                self.tt("pool", dg[:], self.ident[:].unsqueeze(1).broadcast_to([128, 9, 128]),
                        cw[:, c, :].unsqueeze(2).broadcast_to([128, 9, 128]), ALU.mult, [self.ident.r, cw.r], [dg.r])
                if not ctxb:
                    G3 = G[:].rearrange("p (r w) -> p r w", w=66)
                    for j in range(3):
                        bk = self.bank()
                        for kx in range(8):
                            self.mm(bk[:, 0:384], wu[:, kx, 0, wo:wo + 128], h[:, kx, j * 384:(j + 1) * 384], kx == 0, kx == 7,
                                    [wu.r, h.r], [bk.r])
                        ra, rb = 6 * j, 6 * j + 6
                        pa = 0
                        if j == 0 and lo == 0:
                            ra, pa = 1, 64
                        if j == 2 and hi == 0:
                            rb = 17
                        self.copy("act", G3[:, ra:rb, 1:65], bk[:, pa:pa + (rb - ra) * 64].rearrange("p (r w) -> p r w", w=64),
                                  [bk.r], [G.r])
                    if lo == 0:
                        self.memset("pool", G3[:, 0:1, :], 0.0, [G.r])
                    if hi == 0:
                        self.memset("pool", G3[:, 17:18, :], 0.0, [G.r])
                    for ob in range(2):
                        vb = self.bank()
                        for kx in range(8):
                            self.mm(vb[:, 0:512], wu[:, kx, 1, wo:wo + 128], h[:, kx, 64 + ob * 512:64 + (ob + 1) * 512], kx == 0, kx == 7,
                                    [wu.r, h.r], [vb.r])
                        self.copy("act" if ob == 0 else "dve", val[:, ob * 512:(ob + 1) * 512], vb[:, 0:512], [vb.r], [val.r])
                else:
                    bk = self.bank()
                    for kx in range(8):
                        self.mm(bk[:, 0:CT], wu[:, kx, 0, wo:wo + 128], h[:, kx, 0:CT], kx == 0, kx == 7, [wu.r, h.r], [bk.r])
                    self.memset("pool", G[:, 0:CT + 2], 0.0, [G.r])
                    self.copy("act", G[:, 1:CT + 1], bk[:, 0:CT], [bk.r], [G.r])
                    vb = self.bank()
                    for kx in range(8):
                        self.mm(vb[:, 0:CT], wu[:, kx, 1, wo:wo + 128], h[:, kx, 0:CT], kx == 0, kx == 7, [wu.r, h.r], [vb.r])
                    self.copy("act", val[:, 0:CT], vb[:, 0:CT], [vb.r], [val.r])

            def st1(c, k):
                G = Gr[c % 2]
                dg = dgr[c % 2]
                val = vr[c % 2]
                if not ctxb:
                    G3 = G[:].rearrange("p (r w) -> p r w", w=66)
                    for ob in range(2):
                        cbk = self.bank()
                        for tap in range(9):
                            dr_, dc_ = tap // 3 - 1, tap % 3 - 1
                            rhs = G3[:, 8 * ob + 1 + dr_:8 * ob + 9 + dr_, 1 + dc_:65 + dc_]
                            self.mm(cbk[:, 0:512].rearrange("p (r w) -> p r w", w=64), dg[:, tap, :], rhs, tap == 0, tap == 8,
                                    [dg.r, G.r], [cbk.r])
                        s_ = sr[ob]
                        self.act(s_[:], cbk[:, 0:512], AF.Silu, [cbk.r, cb.r], [s_.r], bias=cb[:, c:c + 1])
                        self.tt("pool" if ob == 0 else "dve", actT[:, c, ob * 512:(ob + 1) * 512], s_[:], val[:, ob * 512:(ob + 1) * 512],
                                ALU.mult, [s_.r, val.r], [actT.r])
                else:
                    cbk = self.bank()
                    for kx in range(3):
                        self.mm(cbk[:, 0:CT], dg[:, 3 + kx, :], G[:, kx:kx + CT], kx == 0, kx == 2, [dg.r, G.r], [cbk.r])
                    s_ = sr[0]
                    self.act(s_[:, 0:CT], cbk[:, 0:CT], AF.Silu, [cbk.r, cb.r], [s_.r], bias=cb[:, c:c + 1])
                    self.tt("dve", actT[:, c, 0:CT], s_[:, 0:CT], val[:, 0:CT], ALU.mult, [s_.r, val.r], [actT.r])
                    self.memset("pool", G[:, 0:CT + 2], 0.0, [G.r])

            self.pipe(list(range(NFF)), [st0, st1])
            if band + 1 < 5:
                load_wu(band + 1, 0)

            blocks = list(range(0, ncen, 512))
            for bi, a_ in enumerate(blocks):
                nb = min(512, ncen - a_)
                for dc in range(8):
                    xr = xrr[xr_it % 4]
                    xr_it += 1
                    self.load(xr[:, 0:nb], xv[:, dc, t0 + a_:t0 + a_ + nb], [self.dr(xin, (t0 + a_) // 512)], [xr.r])
                    bk = self.bank()
                    for c in range(NFF):
                        self.mm(bk[:, 0:nb], Wd[:, c, dc * 128:(dc + 1) * 128], actT[:, c, a_:a_ + nb], c == 0, c == NFF - 1, [Wd.r, actT.r], [bk.r])
                    self.stt(xr[:, 0:nb], bk[:, 0:nb], self.mod(l, 5, dc, col), xr[:, 0:nb], ALU.mult, ALU.add,
                             [bk.r, xr.r, self.modr[l]], [xr.r])
                    self.load(ov[:, dc, t0 + a_:t0 + a_ + nb], xr[:, 0:nb], [xr.r], [self.dr(xout, (t0 + a_) // 512)])
                if bi == 0 and band + 1 < 5:
                    prologue(band + 1)
        while bg:
            bg.pop(0)()
        self.end_phase()

    def head_rstd(self, src_ap, src_res, n, sq, rstd, width=128):
        self.act(sq[:, 0:n], src_ap, AF.Square, src_res, [sq.r])
        bk = self.bank()
        self.mm(bk[:, 0:n], self.ones[:], sq[:, 0:n], True, True, [self.ones.r, sq.r], [bk.r])
        self.act(rstd[:, 0:n], bk[:, 0:n], AF.Ln, [bk.r], [rstd.r], bias=self.epsb[:, 0:1], scale=1.0 / width)
        self.act(rstd[:, 0:n], rstd[:, 0:n], AF.Exp, [rstd.r], [rstd.r], scale=-0.5)

    def phase_qkv(self, l, o, xin):
        self.begin_phase()
        wsrc, wres = self.wb("w_qkv", o)
        W = self.psb("wqkv", [128, 8, 1536], BF16)
        self.load(W[:], wsrc.rearrange("(k p) n -> p k n", p=128), wres, [W.r])
        gq = self.psb("gq", [128, 2], F32)
        self.load(gq[:], self.dram["qkgT"].ap()[o], [], [gq.r])
        perm = self.psb("perm", [128, 128], BF16)
        self.load(perm[:], self.dram["c_perm"].ap(), [], [perm.r])
        xv = self.xview(xin)
        qv = self.dram["qT"].ap().rearrange("(c p) n -> p c n", p=128)
        vv = self.dram["vtok"].ap()
        xts = self.pring("qx", 2, [128, 8, 512], F32)
        hs_ = self.pring("qh", 2, [128, 8, 512], BF16)
        rstds = self.pring("qrstd", 2, [128, 512], F32)
        css = self.pring("qcos", 2, [128, 512], F32)
        sns = self.pring("qsin", 2, [128, 512], F32)
        NR = 4
        raw = self.pring("qraw", NR, [128, 512], F32)
        sq = self.pring("qsq", NR, [128, 512], BF16)
        hr = self.pring("qhr", NR, [128, 512], F32)
        qn = self.pring("qn", NR, [128, 512], BF16)
        t1 = self.pring("qt1", NR, [128, 512], F32)
        qo = self.pring("qo", NR, [128, 512], BF16)
        vs = self.pring("qvs", 2, [128, 256], BF16)

        def prologue(i):
            n = 512 if i < 8 else CT
            col = 0 if i < 8 else 1
            t0 = i * 512
            xt, h, rstd = xts[i % 2], hs_[i % 2], rstds[i % 2]
            self.load(xt[:, :, 0:n], xv[:, :, t0:t0 + n], [self.dr(xin, i)], [xt.r])
            if i < 8:
                self.load(css[i % 2][:], self.dram["cosT"].ap()[:, t0:t0 + 512], [], [css[i % 2].r])
                self.load(sns[i % 2][:], self.dram["sinT"].ap()[:, t0:t0 + 512], [], [sns[i % 2].r])
            self.norm_mod(xt, 0, n, rstd, h, l, 0, 1, col)
        prologue(0)
        NR = 4
        bkq = {}
        for i in range(9):
            n = 512 if i < 8 else CT
            t0 = i * 512
            h, cs, sn = hs_[i % 2], css[i % 2], sns[i % 2]
            items = list(range(10)) + ["v%d" % s_ for s_ in range(n // 128)]

            def q0(hd, k):
                j = k % NR
                if isinstance(hd, str):
                    s_ = int(hd[1:])
                    bk = self.bank()
                    for kx in range(8):
                        self.mm(bk[:, 0:256], h[:, kx, s_ * 128:(s_ + 1) * 128], W[:, kx, 1280:1536], kx == 0, kx == 7, [h.r, W.r], [bk.r])
                    bkq[(i, hd)] = bk
                    return
                bk = self.bank()
                for kx in range(8):
                    self.mm(bk[:, 0:n], W[:, kx, hd * 128:(hd + 1) * 128], h[:, kx, 0:n], kx == 0, kx == 7, [W.r, h.r], [bk.r])
                self.copy("act", raw[j][:, 0:n], bk[:, 0:n], [bk.r], [raw[j].r])
                self.act(sq[j][:, 0:n], bk[:, 0:n], AF.Square, [bk.r], [sq[j].r])

            def q1(hd, k):
                j = k % NR
                if isinstance(hd, str):
                    s_ = int(hd[1:])
                    bk = bkq.pop((i, hd))
                    v_ = vs[s_ % 2]
                    self.copy("act", v_[:], bk[:, 0:256], [bk.r], [v_.r])
                    self.load(vv[t0 + s_ * 128:t0 + (s_ + 1) * 128, :], v_[:], [v_.r], [self.dr("vtok", i)])
                    return
                bk = self.bank()
                self.mm(bk[:, 0:n], self.ones[:], sq[j][:, 0:n], True, True, [self.ones.r, sq[j].r], [bk.r])
                self.act(hr[j][:, 0:n], bk[:, 0:n], AF.Ln, [bk.r], [hr[j].r], bias=self.epsb[:, 0:1], scale=1.0 / 128)
                self.act(hr[j][:, 0:n], hr[j][:, 0:n], AF.Exp, [hr[j].r], [hr[j].r], scale=-0.5)
                gcol = gq[:, 0:1] if hd < 8 else gq[:, 1:2]
                dst = qn[j] if i < 8 else qo[j]
                self.stt(dst[:, 0:n], raw[j][:, 0:n], gcol, hr[j][:, 0:n], ALU.mult, ALU.mult, [raw[j].r, hr[j].r, gq.r], [dst.r])
                if i == 8:
                    self.load(qv[:, hd, t0:t0 + n], qo[j][:, 0:n], [qo[j].r], [self.dr("qT", i)])

            def q2(hd, k):
                j = k % NR
                if isinstance(hd, str) or i == 8:
                    return
                pb = self.bank()
                self.mm(pb[:, 0:n], perm[:], qn[j][:, 0:n], True, True, [perm.r, qn[j].r], [pb.r])
                self.tt("pool", t1[j][:, 0:n], qn[j][:, 0:n], cs[:, 0:n], ALU.mult, [qn[j].r, cs.r], [t1[j].r])
                self.tt("dve", raw[j][:, 0:n], pb[:, 0:n], sn[:, 0:n], ALU.mult, [pb.r, sn.r], [raw[j].r])
                self.tt("pool", qo[j][:, 0:n], t1[j][:, 0:n], raw[j][:, 0:n], ALU.add, [t1[j].r, raw[j].r], [qo[j].r])
                self.load(qv[:, hd, t0:t0 + n], qo[j][:, 0:n], [qo[j].r], [self.dr("qT", i)])
            self.pipe(items, [q0, q1, q2])
            if i + 1 < 9:
                prologue(i + 1)
        self.end_phase()

    def phase_attn(self, mod_next=None):
        self.begin_phase()
        tasks = self.mod_tasks(mod_next, lambda: self.psum[7]) if mod_next is not None and mod_next < DEPTH else []
        qv = self.dram["qT"].ap().rearrange("(c p) n -> p c n", p=128)
        mv = self.xview("mixT")
        KT = self.psb("aKT", [128, T], BF16)
        V = self.psb("aV", [128, 34, 132], BF16)
        self.memset("pool", V[:], 0.0, [V.r])
        self.memset("pool", V[:, :, 128:129], 1.0, [V.r])
        qr = self.pring("aQ", 2, [128, 512], BF16)
        LA = 2
        pr = self.pring("aP", LA + 1, [128, 512], BF16)
        rinv = self.pring("arinv", 2, [128, 4], F32)
        otok = self.pring("aOt", 2, [128, 4, 128], BF16)
        ob = self.pring("aO", 2, [128, 512], BF16)
        scale = 128 ** -0.5
        allq = [self.dr("qT", i) for i in range(9)]
        allv = [self.dr("vtok", i) for i in range(9)]
        it = 0
        pi = 0
        tbk = self.psum[7]
        tbv = tbk[:].bitcast(BF16)
        for g in range(2):
            self.load(KT[:], qv[:, 8 + g, :], allq, [KT.r])
            self.load(V[:, :, 0:128], self.dram["vtok"].ap().rearrange("(c p) d -> p c d", p=128)[:, :, g * 128:(g + 1) * 128], allv, [V.r])
            for hq in range(4):
                hd = 4 * g + hq
                for i in range(9):
                    n = 512 if i < 8 else CT
                    nsub = n // 128
                    t0 = i * 512
                    kcs = list(range(34)) if i < 8 else [32, 33]
                    q = qr[it % 2]
                    obk = [self.psum[2 * (it % 2)], self.psum[2 * (it % 2) + 1]]
                    ri, ot, o_ = rinv[it % 2], otok[it % 2], ob[it % 2]
                    it += 1
                    self.load(q[:, 0:n], qv[:, hd, t0:t0 + n], [self.dr("qT", i)], [q.r])
                    sbanks = {}
                    if tasks and it % 4 == 2:
                        tasks.pop(0)()

                    def issue_s(kc):
                        nonlocal pi
                        b_ = self.psum[4 + pi % (LA + 1)]
                        p_ = pr[pi % (LA + 1)]
                        pi += 1
                        self.mm(b_[:, 0:n], KT[:, kc * 128:(kc + 1) * 128], q[:, 0:n], True, True, [KT.r, q.r], [b_.r])
                        self.act(p_[:, 0:n], b_[:, 0:n], AF.Exp, [b_.r], [p_.r], scale=scale)
                        sbanks[kc] = p_
                    for kc in kcs[0:LA]:
                        issue_s(kc)
                    for idx, kc in enumerate(kcs):
                        if idx + LA < len(kcs):
                            issue_s(kcs[idx + LA])
                        p_ = sbanks.pop(kc)
                        first, last = idx == 0, idx == len(kcs) - 1
                        for s_ in range(nsub):
                            bk = obk[s_ // 2]
                            off = (s_ % 2) * 132
                            self.mm(bk[:, off:off + 129], p_[:, s_ * 128:(s_ + 1) * 128], V[:, kc, 0:129], first and s_ % 2 == 0, last,
                                    [p_.r, V.r], [bk.r], inc=(s_ == nsub - 1), skip_group_check=True)
                    for s_ in range(nsub):
                        bk = obk[s_ // 2]
                        off = (s_ % 2) * 132
                        self.S.op("dve", lambda hw, ri=ri, bk=bk, off=off, s_=s_: hw.reciprocal(ri[:, s_:s_ + 1], bk[:, off + 128:off + 129]),
                                  [bk.r], [ri.r])
                        self.ts("dve", ot[:, s_, :], bk[:, off:off + 128], ri[:, s_:s_ + 1], ALU.mult, [bk.r, ri.r], [ot.r])
                    for s_ in range(nsub):
                        self.tr(tbv[:, s_ * 128:(s_ + 1) * 128], ot[:, s_, :], self.ident[:], [ot.r, self.ident.r], [tbk.r], inc=(s_ == nsub - 1))
                    self.copy("act", o_[:, 0:n], tbv[:, 0:n], [tbk.r], [o_.r])
                    self.load(mv[:, hd, t0:t0 + n], o_[:, 0:n], [o_.r], [self.dr("mixT", i)])
        while tasks:
            tasks.pop(0)()
        self.end_phase()

    def pipe(self, items, stages):
        n, ns = len(items), len(stages)
        for step in range(n + ns - 1):
            for j in range(ns - 1, -1, -1):
                k = step - j
                if 0 <= k < n:
                    stages[j](items[k], k)

    def phase_ab_in(self, l, e, xin):
        self.begin_phase()
        wsrc, wres = self.wb("w_in_ab", e)
        W = self.psb("win", [128, 8, 3072], BF16)
        self.load(W[:], wsrc.rearrange("(k p) n -> p k n", p=128), wres, [W.r])
        CS = self.psb("ccs", [128, 256], BF16)
        self.load(CS[:], self.dram["c_cs"].ap(), [], [CS.r])
        cmask = self.psb("cmask", [128, 2, 128], I32)
        self.load(cmask[:], self.dram["c_mask"].ap(), [], [cmask.r])
        smask = self.psb("smask", [128, 512], F32)
        self.load(smask[:], self.dram["c_smask"].ap(), [], [smask.r])
        lg = self.psb("lg", [128, 2, 2, 4], F32)
        self.load(lg[:], self.dram["hglT"].ap(), [], [lg.r])
        lb = self.psb("lb", [128, 2, 4], F32)
        oml = self.psb("oml", [128, 2, 4], F32)
        lbm1 = self.psb("lbm1", [128, 2, 4], F32)
        if e == 0:
            self.memset("pool", lb[:], 0.0, [lb.r])
        else:
            self.tt("dve", lb[:], lg[:, 1], lg[:, 0], ALU.subtract, [lg.r], [lb.r])
            self.act(lb[:], lb[:], AF.Sigmoid, [lb.r], [lb.r])
        self.ts("dve", oml[:], lb[:], -1.0, ALU.mult, [lb.r], [oml.r], s2=1.0, op1=ALU.add)
        self.ts("dve", lbm1[:], lb[:], -1.0, ALU.add, [lb.r], [lbm1.r])
        xv = self.xview(xin)
        xt = self.psb("ax", [128, 8, 512], F32)
        hs_ = self.pring("ah", 2, [128, 8, 512], BF16)
        rstd = self.psb("arstd", [128, 512], F32)
        aT = self.pring("aT", 2, [128, 512], BF16)
        pq = self.pring("apq", 2, [128, 2, 256], BF16)
        gs = self.pring("ags", 2, [128, 512], BF16)
        vsb = self.psb("avs", [128, 4, 512], BF16)
        qs = self.psb("aqs", [128, 4, 512], F32)
        sig8 = self.psb("asig", [128, 8, 512], F32)
        R = 2
        lf = self.pring("alf", R, [128, 512], F32)
        kk = self.pring("akk", R + 1, [128, 512], F32)
        bb = self.pring("ab", R, [128, 512], F32)
        bm = self.pring("abm", R, [128, 512], F32)
        E1 = self.pring("aE1", R, [128, 512], F32)
        E2 = self.pring("aE2", R, [128, 512], F32)
        qt = [self.psb("aqt%d" % d, [128, 4, 512], BF16) for d in range(2)]
        kt = [self.psb("akt%d" % d, [128, 4, 512], BF16) for d in range(2)]
        kh = self.pring("akh", 2, [128, 512], BF16)
        khs = self.pring("akhs", 2, [128, 4, 128], BF16)
        dsb = self.pring("adsb", 2, [128, 512 // CH], F32)
        qib = self.pring("aqib", 2, [128, 512], BF16)
        attT = [self.pring("aatt%d" % d, 2, [128, 128], BF16) for d in range(2)]
        oi = self.psb("aoi", [128, 512], F32)
        for d in range(2):
            for a_ in attT[d]:
                self.memset("pool", a_[:], 0.0, [a_.r])
        PQv = self.dram["PQ"].ap()
        nl_, ncx_ = L // CH, CT // CH

        def prologue(i):
            n = 512 if i < 8 else CT
            self.load(xt[:, :, 0:n], xv[:, :, i * 512:i * 512 + n], [self.dr(xin, i)], [xt.r])
            self.norm_mod(xt, 0, n, rstd, hs_[i % 2], l, 0, 1, 0 if i < 8 else 1)
        prologue(0)
        ai = [0]
        for i in range(9):
            n = 512 if i < 8 else CT
            t0 = i * 512
            nsub = n // 128
            nch = n // CH
            h = hs_[i % 2]
            banks = {}

            def proj_to(key, c0, M=None):
                bk = self.bank()
                for k in range(8):
                    self.mm(bk[:, 0:n], W[:, k, c0:c0 + 128], h[:, k, 0:n], k == 0, k == 7, [W.r, h.r], [bk.r])
                banks[key] = bk

            items = [("a", g) for g in range(4)] + [("g", hd) for hd in range(4)] + [("v", s_) for s_ in range(nsub)] + \
                    [("q", hd) for hd in range(4)] + [("z", j) for j in range(8)]

            def p1s0(it, k):
                kind, j = it
                if kind == "a":
                    proj_to(it, j * 128)
                elif kind == "g":
                    proj_to(it, 2560 + j * 128)
                elif kind == "q":
                    proj_to(it, 512 + j * 128)
                elif kind == "z":
                    proj_to(it, 1024 + j * 128)
                else:
                    bk = self.bank()
                    for kx in range(8):
                        self.mm(bk[:, 0:512], h[:, kx, j * 128:(j + 1) * 128], W[:, kx, 2048:2560], kx == 0, kx == 7, [h.r, W.r], [bk.r])
                    banks[it] = bk

            def p1s1(it, k):
                kind, j = it
                bk = banks.pop(it)
                if kind == "a":
                    self.copy("dve", aT[j % 2][:, 0:n], bk[:, 0:n], [bk.r], [aT[j % 2].r])
                elif kind == "g":
                    self.act(sig8[:, j, 0:n], bk[:, 0:n], AF.Sigmoid, [bk.r], [sig8.r])
                    g_ = gs[j % 2]
                    self.tt("dve", g_[:, 0:n], bk[:, 0:n], sig8[:, j, 0:n], ALU.mult, [bk.r, sig8.r], [g_.r])
                    self.load(self.dram["gT"].ap()[j * 128:(j + 1) * 128, t0:t0 + n], g_[:, 0:n], [g_.r], [self.dr("gT", i)])
                elif kind == "q":
                    self.act(sig8[:, 4 + j, 0:n], bk[:, 0:n], AF.Sigmoid, [bk.r], [sig8.r])
                    self.tt("dve", qs[:, j, 0:n], bk[:, 0:n], sig8[:, 4 + j, 0:n], ALU.mult, [bk.r, sig8.r], [qs.r])
                elif kind == "z":
                    self.act(sig8[:, j, 0:n], bk[:, 0:n], AF.Sigmoid, [bk.r], [sig8.r])
                else:
                    self.copy("act", vsb[:, j, :], bk[:, 0:512], [bk.r], [vsb.r])
                    if j == nsub - 1:
                        self.load(self.dram["vtok2"].ap()[t0:t0 + n, :].rearrange("(s p) d -> p s d", p=128), vsb[:, 0:nsub, :], [vsb.r],
                                  [self.dr("vtok2", i)])

            def p1s2(it, k):
                kind, g = it
                if kind != "a":
                    return
                for s2 in range(0, nsub, 2):
                    b2 = self.bank()
                    ns2 = min(2, nsub - s2)
                    for u in range(ns2):
                        self.mm(b2[:, u * 256:(u + 1) * 256], aT[g % 2][:, (s2 + u) * 128:(s2 + u + 1) * 128], CS[:], True, True,
                                [aT[g % 2].r, CS.r], [b2.r], inc=(u == ns2 - 1))
                    p_ = pq[(s2 // 2) % 2]
                    self.copy("act", p_[:, 0:ns2, :], b2[:, 0:ns2 * 256].rearrange("p (u n) -> p u n", n=256), [b2.r], [p_.r])
                    self.load(PQv[t0 + s2 * 128:t0 + (s2 + ns2) * 128, g, :].rearrange("(u p) n -> p u n", p=128), p_[:, 0:ns2, :],
                              [p_.r], [self.dr("PQ", i)])
            self.pipe(items, [p1s0, p1s1, p1s2])

            if i + 1 < 9:
                prologue(i + 1)

            items2 = [(d, hd) for d in range(2) for hd in range(4)]

            def b3of(t):
                return t[:, 0:n].rearrange("p (c t) -> p c t", t=CH)

            def s1(it, k):
                d, hd = it
                j = d * 4 + hd
                r = k % R
                self.ts("dve", lf[r][:, 0:n], sig8[:, j, 0:n], oml[:, d, hd:hd + 1], ALU.mult, [sig8.r, oml.r, lb.r], [lf[r].r],
                        s2=lb[:, d, hd:hd + 1], op1=ALU.add)
                self.act(lf[r][:, 0:n], lf[r][:, 0:n], AF.Ln, [lf[r].r], [lf[r].r])
                kr = kk[k % (R + 1)]
                self.ts("pool", kr[:, 0:n], sig8[:, j, 0:n], lbm1[:, d, hd:hd + 1], ALU.mult, [sig8.r, lbm1.r, oml.r], [kr.r],
                        s2=oml[:, d, hd:hd + 1], op1=ALU.add)

            def s2(it, k):
                d, hd = it
                r = k % R
                b = bb[r]
                if d == 0:
                    bo_, lo_ = b[:, 0:n], lf[r][:, 0:n]
                else:
                    bo_, lo_ = b[:, 0:n][:, ::-1], lf[r][:, 0:n][:, ::-1]
                self.S.op("dve", lambda hw, bo_=bo_, lo_=lo_, sm_=smask[:, 0:n]: hw.tensor_tensor_scan(bo_, sm_, lo_, 0.0, ALU.mult, ALU.add),
                          [smask.r, lf[r].r], [b.r])
                b3 = b3of(b)
                self.tt("dve", b3of(bm[r]), b3, b3[:, :, CH // 2:CH // 2 + 1].broadcast_to([128, nch, CH]), ALU.subtract, [b.r], [bm[r].r])
                self.act(E1[r][:, 0:n], bm[r][:, 0:n], AF.Exp, [bm[r].r], [E1[r].r])
                self.act(E2[r][:, 0:n], bm[r][:, 0:n], AF.Exp, [bm[r].r], [E2[r].r], scale=-1.0)

            def s3(it, k):
                d, hd = it
                r = k % R
                b = bb[r]
                kr = kk[k % (R + 1)]
                li = CH - 1 if d == 0 else 0
                b3 = b3of(b)
                self.tt("pool", qt[d][:, hd, 0:n], qs[:, hd, 0:n], E1[r][:, 0:n], ALU.mult, [qs.r, E1[r].r], [qt[d].r])
                self.tt("pool", kt[d][:, hd, 0:n], kr[:, 0:n], E2[r][:, 0:n], ALU.mult, [kr.r, E2[r].r], [kt[d].r])
                self.tt("dve", b3of(bm[r]), b3[:, :, li:li + 1].broadcast_to([128, nch, CH]), b3, ALU.subtract, [b.r], [bm[r].r])
                self.act(E1[r][:, 0:n], bm[r][:, 0:n], AF.Exp, [bm[r].r], [E1[r].r])
                self.act(E2[r][:, 0:n], b[:, 0:n], AF.Exp, [b.r], [E2[r].r])
                ds_ = dsb[k % 2]
                c0_ = t0 // CH
                if d == 0:
                    p0_ = (ncx_ + c0_) if i < 8 else 0
                    self.act(ds_[:, 0:nch], b3[:, :, li], AF.Exp, [b.r], [ds_.r])
                else:
                    p0_ = (ncx_ + nl_ - c0_ - nch) if i < 8 else 0
                    self.act(ds_[:, 0:nch][:, ::-1], b3[:, :, li], AF.Exp, [b.r], [ds_.r])
                self.load(self.dram["dT"].ap()[d, hd * 128:(hd + 1) * 128, p0_:p0_ + nch], ds_[:, 0:nch], [ds_.r], [self.dr("dT", i)])
                self.load(self.dram["qtT"].ap()[d, hd * 128:(hd + 1) * 128, t0:t0 + n], qt[d][:, hd, 0:n], [qt[d].r], [self.dr("qtT", i)])

            def s4(it, k):
                d, hd = it
                r = k % R
                kr = kk[k % (R + 1)]
                kh_ = kh[k % 2]
                qi_ = qib[k % 2]
                self.tt("pool", kh_[:, 0:n], kr[:, 0:n], E1[r][:, 0:n], ALU.mult, [kr.r, E1[r].r], [kh_.r])
                self.tt("pool", qi_[:, 0:n], qs[:, hd, 0:n], E2[r][:, 0:n], ALU.mult, [qs.r, E2[r].r], [qi_.r])
                self.load(self.dram["qiT"].ap()[d, hd * 128:(hd + 1) * 128, t0:t0 + n], qi_[:, 0:n], [qi_.r], [self.dr("qiT", i)])
                tb = self.bank()
                tbv = tb[:].bitcast(BF16)
                for s_ in range(nsub):
                    self.tr(tbv[:, s_ * 128:(s_ + 1) * 128], kh_[:, s_ * 128:(s_ + 1) * 128], self.ident[:], [kh_.r, self.ident.r], [tb.r],
                            inc=(s_ == nsub - 1))
                k_ = khs[k % 2]
                self.copy("act", k_[:, 0:nsub, :], tbv[:, 0:nsub * 128].rearrange("p (s k) -> p s k", k=128), [tb.r], [k_.r])
                self.load(self.dram["khat"].ap()[d, t0:t0 + n, hd * 128:(hd + 1) * 128].rearrange("(s p) k -> p s k", p=128), k_[:, 0:nsub, :],
                          [k_.r], [self.dr("khat", i)])
            self.pipe(items2, [s1, s2, s3, s4])

            items3 = [(hd, s_) for hd in range(4) for s_ in range(nsub)]
            abk = {}

            def i0(it, k):
                hd, s_ = it
                sl = slice(s_ * 128, (s_ + 1) * 128)
                for d in range(2):
                    ab_ = self.psum[2 + self.psi % 6]
                    self.psi += 1
                    self.mm(ab_[:, 0:128], kt[d][:, hd, sl], qt[d][:, hd, sl], True, True, [kt[d].r, qt[d].r], [ab_.r])
                    abk[(it, d)] = ab_

            def i1(it, k):
                hd, s_ = it
                sl = slice(s_ * 128, (s_ + 1) * 128)
                obk = self.psum[hd % 2]
                ats = []
                for d in range(2):
                    ab_ = abk.pop((it, d))
                    a_ = attT[d][ai[0] % 2]
                    self.S.op("dve", lambda hw, a_=a_, ab_=ab_, d=d: hw.copy_predicated(a_[:], cmask[:, d, :], ab_[:, 0:128]),
                              [cmask.r, ab_.r], [a_.r])
                    ats.append(a_)
                ai[0] += 1
                self.mm(obk[:, sl], vsb[:, s_, hd * 128:(hd + 1) * 128], ats[0][:], True, False, [vsb.r, ats[0].r], [obk.r], inc=False)
                self.mm(obk[:, sl], vsb[:, s_, hd * 128:(hd + 1) * 128], ats[1][:], False, True, [vsb.r, ats[1].r], [obk.r], inc=True)
                if s_ == nsub - 1:
                    self.copy("act", oi[:, 0:n], obk[:, 0:n], [obk.r], [oi.r])
                    self.load(self.dram["ointra"].ap()[hd * 128:(hd + 1) * 128, t0:t0 + n], oi[:, 0:n], [oi.r], [self.dr("ointra", i)])
            self.pipe(items3, [i0, i1])
        self.end_phase()

    def phase_fourier(self, mod_next=None):
        self.begin_phase()
        fj = [0]
        tasks = self.mod_tasks(mod_next, lambda: self.psum[4 if fj[0] % 2 == 0 else 0]) if mod_next is not None and mod_next < DEPTH else []
        PQ = self.psb("fPQ", [128, 34, 4, 256], BF16)
        allpq = [self.dr("PQ", i) for i in range(9)]
        pv = self.dram["PQ"].ap().rearrange("(c p) g n -> p c (g n)", p=128)
        for c0 in range(0, 34, 2):
            self.load(PQ[:, c0:c0 + 2].rearrange("p c g n -> p c (g n)"), pv[:, c0:c0 + 2, :], allpq, [PQ.r])
        cr = self.pring("fC", 3, [128, 2, 4, 512], BF16)
        yo = self.pring("fy", 2, [128, 512], BF16)
        dft = self.dram["c_dft"].ap()
        mv = self.xview("mixT")
        ci = 0
        for j in range(8):
            banks = [self.psum[g] for g in range(4)] if j % 2 == 0 else [self.psum[4 + g] for g in range(4)]
            for tq in range(8):
                if tasks and (j * 8 + tq) % 4 == 1:
                    fj[0] = j
                    tasks.pop(0)()
                c_ = cr[ci % 3]
                ci += 1
                for z in range(2):
                    self.load(c_[:, z], dft[z, tq * 512:(tq + 1) * 512, j * 512:(j + 1) * 512].rearrange("(t p) n -> p t n", p=128),
                              [], [c_.r])
                for t4 in range(4):
                    tc = tq * 4 + t4
                    for g in range(4):
                        self.mm(banks[g][:, 0:512], PQ[:, tc, g, 0:128], c_[:, 0, t4, :], tc == 0, False, [PQ.r, c_.r], [banks[g].r], inc=False)
                        self.mm(banks[g][:, 0:512], PQ[:, tc, g, 128:256], c_[:, 1, t4, :], False, tc == 31, [PQ.r, c_.r], [banks[g].r],
                                inc=(tc == 31 or (t4 == 3 and g == 3)))
            for g in range(4):
                y = yo[g % 2]
                self.copy("act" if g % 2 == 0 else "dve", y[:], banks[g][:, 0:512], [banks[g].r], [y.r])
                self.load(mv[:, g, j * 512:(j + 1) * 512], y[:], [y.r], [self.dr("mixT", j)])
        while tasks:
            tasks.pop(0)()
        c2 = self.psb("fC2", [128, 2, 2, 256], BF16)
        for z in range(2):
            self.load(c2[:, z], self.dram["c_dft256"].ap()[z].rearrange("(c p) n -> p c n", p=128), [], [c2.r])
        for g in range(4):
            bk = self.bank()
            for tc in range(2):
                self.mm(bk[:, 0:256], PQ[:, 32 + tc, g, 0:128], c2[:, 0, tc, :], tc == 0, False, [PQ.r, c2.r], [bk.r], inc=False)
                self.mm(bk[:, 0:256], PQ[:, 32 + tc, g, 128:256], c2[:, 1, tc, :], False, tc == 1, [PQ.r, c2.r], [bk.r], inc=(tc == 1))
            y = yo[g % 2]
            self.copy("act", y[:, 0:256], bk[:, 0:256], [bk.r], [y.r])
            self.load(mv[:, g, L:L + CT], y[:, 0:256], [y.r], [self.dr("mixT", 8)])
        self.end_phase()

    def phase_hgrn_scan(self, e):
        self.begin_phase()
        nl, ncx = L // CH, CT // CH
        PC = 8
        NP = NCH // PC
        gn = self.psb("hgn", [128, 4], F32)
        self.load(gn[:], self.dram["gnT"].ap()[e], [], [gn.r])
        oacc = self.psb("hoacc", [128, 4, T], F32)
        S32 = self.psb("hS32", [128, 8, 128], F32)
        Sbf = self.pring("hSbf", 8, [128, 128], BF16)
        DD = self.psb("hDD", [128, 8, NCH], F32)
        KHr = self.pring("hKH", 2, [CH, 8, PC, 128], BF16)
        VHr = self.pring("hVH", 2, [CH, 8, PC, 128], BF16)
        QIr = self.pring("hQI", 2, [128, 8, PC * CH], BF16)
        sq = self.psb("hsq", [128, 512], BF16)
        rstd = self.psb("hrstd", [128, 512], F32)
        tmp = self.psb("htmp", [128, 512], F32)
        gsb = self.pring("hgs", 2, [128, 512], BF16)
        ob = self.pring("hob", 2, [128, 512], BF16)
        mv = self.xview("mixT")
        al = lambda nm: [self.dr(nm, i) for i in range(9)]
        S32r = [self.S.res("S32_%d" % c_) for c_ in range(8)]
        oar = [self.S.res("oacc_%d" % c_) for c_ in range(4)]
        self.memset("pool", S32[:], 0.0, S32r)
        for ch in range(8):
            self.memset("pool", Sbf[ch][:], 0.0, [Sbf[ch].r])
        for hd in range(4):
            self.load(oacc[:, hd, :], self.dram["ointra"].ap()[hd * 128:(hd + 1) * 128, :], al("ointra"), [oar[hd]])
        for ch in range(8):
            d, hd = ch // 4, ch % 4
            self.load(DD[:, ch, :], self.dram["dT"].ap()[d, hd * 128:(hd + 1) * 128, :], al("dT"), [DD.r])
        kv = self.dram["khat"].ap()
        vv = self.dram["vtok2"].ap()
        qiv = self.dram["qiT"].ap()

        def chunk_range(d, j):
            if j == 0:
                return nl
            return (j - 1) * PC if d == 0 else nl - j * PC

        def load_piece(j):
            kh, vh, qi = KHr[j % 2], VHr[j % 2], QIr[j % 2]
            for ch in range(8):
                d, hd = ch // 4, ch % 4
                c0 = chunk_range(d, j)
                ts_ = slice(c0 * CH, (c0 + PC) * CH)
                hs = slice(hd * 128, (hd + 1) * 128)
                self.load(kh[:, ch], kv[d, ts_, hs].rearrange("(c p) k -> p c k", p=CH), al("khat"), [kh.r])
                self.load(vh[:, ch], vv[ts_, hs].rearrange("(c p) k -> p c k", p=CH), al("vtok2"), [vh.r])
                self.load(qi[:, ch, :], qiv[d, hs, ts_], al("qiT"), [qi.r])

        def bidx(d, q):
            return q if d == 0 else PC - 1 - q

        def emit_U(p):
            j, q = p // PC, p % PC
            kh, vh = KHr[j % 2], VHr[j % 2]
            for half in range(2):
                ub = self.psum[4 + 2 * (p % 2) + half]
                for c4 in range(4):
                    ch = half * 4 + c4
                    ix = bidx(ch // 4, q)
                    self.mm(ub[:, c4 * 128:(c4 + 1) * 128], kh[:, ch, ix, :], vh[:, ch, ix, :], True, True, [kh.r, vh.r], [ub.r],
                            inc=(c4 == 3))

        load_piece(0)
        emit_U(0)
        for p in range(NCH):
            j, q = p // PC, p % PC
            if q == 0 and j + 1 < NP:
                load_piece(j + 1)
            if p + 1 < NCH:
                emit_U(p + 1)
            qi = QIr[j % 2]
            for ch in range(8):
                d = ch // 4
                ix = bidx(d, q)
                ib = self.psum[ch // 2]
                col = (ch % 2) * 256 + ix * CH
                self.mm(ib[:, col:col + CH], Sbf[ch][:], qi[:, ch, ix * CH:(ix + 1) * CH], True, True, [Sbf[ch].r, qi.r], [ib.r],
                        inc=(ch % 2 == 1))
            if q == PC - 1:
                for ch in range(8):
                    d, hd = ch // 4, ch % 4
                    c0 = chunk_range(d, j)
                    ib = self.psum[ch // 2]
                    cb_ = (ch % 2) * 256
                    dst = oacc[:, hd, c0 * CH:(c0 + PC) * CH]
                    self.tt("dve", dst, dst, ib[:, cb_:cb_ + PC * CH], ALU.add, [oar[hd], ib.r], [oar[hd]])
            for ch in range(8):
                ub = self.psum[4 + 2 * (p % 2) + ch // 4]
                c4 = ch % 4
                self.stt(S32[:, ch, :], S32[:, ch, :], DD[:, ch, p:p + 1], ub[:, c4 * 128:(c4 + 1) * 128], ALU.mult, ALU.add,
                         [S32r[ch], DD.r, ub.r], [S32r[ch]])
            if p + 1 < NCH:
                for ch in range(8):
                    self.copy("act", Sbf[ch][:], S32[:, ch, :], [S32r[ch]], [Sbf[ch].r])
        for hd in range(4):
            hs = slice(hd * 128, (hd + 1) * 128)
            for i in range(9):
                n = 512 if i < 8 else CT
                t0 = i * 512
                g_ = gsb[i % 2]
                o_ = ob[i % 2]
                self.load(g_[:, 0:n], self.dram["gT"].ap()[hs, t0:t0 + n], al("gT"), [g_.r])
                self.head_rstd(oacc[:, hd, t0:t0 + n], [oar[hd]], n, sq, rstd)
                self.stt(tmp[:, 0:n], oacc[:, hd, t0:t0 + n], gn[:, hd:hd + 1], rstd[:, 0:n], ALU.mult, ALU.mult, [oar[hd], gn.r, rstd.r], [tmp.r])
                self.tt("pool", o_[:, 0:n], tmp[:, 0:n], g_[:, 0:n], ALU.mult, [tmp.r, g_.r], [o_.r])
                self.load(mv[:, 4 + hd, t0:t0 + n], o_[:, 0:n], [o_.r], [self.dr("mixT", i)])
        self.end_phase()

    def finish(self):
        self.S.wait_all("sp", self.out_res)
        self.S.emit()
        return self.nc


def bf(a):
    return np.asarray(a, dtype=np.float32).astype(ml_dtypes.bfloat16)


def host_consts():
    c = {}
    c["c_ident"] = bf(np.eye(128))
    return c


def prep_core(b, inp, consts):
    f = lambda a: np.ascontiguousarray(np.asarray(a, dtype=np.float32))
    m = dict(consts)
    m["xT"] = f(np.concatenate([inp["x"][b].T, inp["ctx"][b].T], axis=1))
    cc = np.stack([inp["c"][b], inp["c_ctx"]], axis=1)
    m["cT"] = f(cc.reshape(8, 128, 2).transpose(1, 0, 2))
    m["bmodT"] = f(inp["b_mod"].reshape(DEPTH, 48, 128).transpose(2, 0, 1))
    m["w_mod"] = f(inp["w_mod"])
    m["fngT"] = f(inp["final_norm_g"].reshape(8, 128).T)
    return m


def prep_weights(inp):
    f = lambda a: np.ascontiguousarray(np.asarray(a, dtype=np.float32))
    m = {}
    for k in ("w_mod", "w_in_ab", "w_out_ab", "w_qkv", "w_out_att", "w_up", "w_down"):
        m[k] = f(inp[k])
    m["bmodT"] = f(inp["b_mod"].reshape(DEPTH, 48, 128).transpose(2, 0, 1))
    m["fngT"] = f(inp["final_norm_g"].reshape(8, 128).T)
    cwt = np.asarray(inp["conv_w"]).reshape(DEPTH, 9, NFF, 128).transpose(0, 3, 2, 1)
    m["convwT"] = f(cwt)
    m["convbT"] = f(np.asarray(inp["conv_b"]).reshape(DEPTH, NFF, 128).transpose(0, 2, 1))
    return m


def rope_tables():
    t = np.arange(L)
    row = (t // GRID).astype(np.float32)
    colp = (t % GRID).astype(np.float32)
    nf = 32
    freqs = (10000.0 ** (-np.arange(nf, dtype=np.float32) / nf)).astype(np.float32)
    ang = np.concatenate([row[:, None] * freqs, colp[:, None] * freqs], axis=-1)
    cos = np.repeat(np.cos(ang), 2, axis=1).T
    sin = np.repeat(np.sin(ang), 2, axis=1).T
    perm = np.zeros((128, 128), np.float32)
    for dp in range(128):
        if dp % 2 == 0:
            perm[dp + 1, dp] = -1.0
        else:
            perm[dp - 1, dp] = 1.0
    return np.ascontiguousarray(cos, np.float32), np.ascontiguousarray(sin, np.float32), bf(perm)


def ab_consts():
    c = {}
    ch = np.arange(128)
    ang = 2 * np.pi * np.outer(ch, ch) / 128.0
    c["c_cs"] = bf(np.concatenate([np.cos(ang), -np.sin(ang)], axis=1) / np.sqrt(128.0))
    t = np.arange(L, dtype=np.int64)
    m = np.outer(t, t) % L
    a = (2 * np.pi / L) * m
    c["c_dft"] = np.stack([bf(np.cos(a) / 64.0), bf(np.sin(a) / 64.0)])
    t2 = np.arange(CT, dtype=np.int64)
    a2 = (2 * np.pi / CT) * (np.outer(t2, t2) % CT)
    c["c_dft256"] = np.stack([bf(np.cos(a2) / 16.0), bf(np.sin(a2) / 16.0)])
    s_ = np.arange(128)[:, None]
    t_ = np.arange(128)[None, :]
    same = (s_ // CH) == (t_ // CH)
    mk = np.stack([(same & (s_ <= t_)), (same & (s_ >= t_))], axis=1).astype(np.int32)
    c["c_mask"] = np.ascontiguousarray(mk)
    sm = np.ones((128, 512), np.float32)
    sm[:, ::CH] = 0.0
    c["c_smask"] = sm
    return c


def build_full():
    K = KB()
    K.setup_consts()
    K.din('xT', [D, T])
    K.din('convwT', [DEPTH, 128, NFF, 9]); K.din('convbT', [DEPTH, 128, NFF])
    K.din('qkgT', [2, 128, 2]); K.din('c_perm', [128, 128], BF16); K.din('cosT', [128, L]); K.din('sinT', [128, L])
    K.din('c_cs', [128, 256], BF16); K.din('c_dft', [2, L, L], BF16); K.din('c_dft256', [2, CT, CT], BF16)
    K.din('c_mask', [128, 2, 128], I32); K.din('c_smask', [128, 512]); K.din('hglT', [128, 2, 2, 4]); K.din('gnT', [2, 128, 4])
    K.dscratch('xa', [D, T], F32); K.dscratch('xb', [D, T], F32); K.dscratch('mixT', [D, T], BF16)
    K.dscratch('qT', [1280, T], BF16); K.dscratch('vtok', [T, 256], BF16)
    K.dscratch('PQ', [T, 4, 256], BF16); K.dscratch('gT', [512, T], BF16); K.dscratch('vtok2', [T, 512], BF16)
    K.dscratch('qtT', [2, 512, T], BF16); K.dscratch('khat', [2, T, 512], BF16)
    K.dscratch('dT', [2, 512, NCH], F32); K.dscratch('qiT', [2, 512, T], BF16); K.dscratch('ointra', [512, T], F32)
    def worder(l):
        o = [('w_in_ab', l // 2), ('w_out_ab', l // 2)] if l % 2 == 0 else [('w_qkv', l // 2), ('w_out_att', l // 2)]
        return o + [('w_up', l), ('w_down', l)]
    K.wcast_setup([('w_in_ab', [2, D, 3072]), ('w_out_ab', [2, D, D]), ('w_up', [DEPTH, D, 2 * DFF]), ('w_down', [DEPTH, DFF, D]),
                   ('w_qkv', [2, D, 1536]), ('w_out_att', [2, D, D])])
    K.wcast_issue(worder(0))
    K.mod_setup()
    K.phase_mod0()
    for l in range(DEPTH):
        xin = 'xT' if l == 0 else 'xa'
        if l % 2 == 0:
            K.phase_ab_in(l, l // 2, xin)
            K.phase_fourier(l + 1)
            K.phase_hgrn_scan(l // 2)
            K.phase_out(l, 'w_out_ab', l // 2, xin, 'xb')
        else:
            K.phase_qkv(l, l // 2, xin)
            K.phase_attn(l + 1)
            K.phase_out(l, 'w_out_att', l // 2, xin, 'xb')
        K.phase_ffn(l, 'xb', 'xa', bg=(K.wcast_tasks(worder(l + 1)) if l + 1 < DEPTH else None))
    K.phase_final('xa')
    nc = K.finish()
    return nc, K


def kernel(**inputs):
    inp = {k: np.asarray(v) for k, v in inputs.items()}
    nc, K = build_full()
    shared = dict(host_consts())
    shared.update(prep_weights(inp))
    shared.update(ab_consts())
    cos, sin, perm = rope_tables()
    shared['cosT'] = cos; shared['sinT'] = sin; shared['c_perm'] = perm
    shared['qkgT'] = np.ascontiguousarray(np.stack([inp['q_norm_g'], inp['k_norm_g']], axis=2).astype(np.float32))
    shared['hglT'] = np.ascontiguousarray(inp['hg_lb_logits'].reshape(2, 2, 4, 128).transpose(3, 0, 1, 2).astype(np.float32))
    shared['gnT'] = np.ascontiguousarray(inp['hg_norm_g'].reshape(2, 4, 128).transpose(0, 2, 1).astype(np.float32))
    in_maps = []
    for b in range(8):
        m = dict(shared)
        m.update(prep_core(b, inp, {}))
        in_maps.append({k: v for k, v in m.items() if k in K.dram})
    res = run_bass_kernel_spmd(nc, in_maps, core_ids=list(range(8)))
    out = np.stack([np.ascontiguousarray(r['outT'].T) for r in res.results], axis=0)
    return out.astype(np.float32)
```

```python
import numpy as np
import concourse.bass as bass
import concourse.mybir as mybir

F32 = mybir.dt.float32
BF16 = mybir.dt.bfloat16
I32 = mybir.dt.int32
U8 = mybir.dt.uint8
AF = mybir.ActivationFunctionType
ALU = mybir.AluOpType

SEM_LIMIT = 30000


class Res:
    __slots__ = ("name", "w", "r")

    def __init__(self, name):
        self.name = name
        self.w = None
        self.r = {}


class Eng:
    def __init__(self, name, hw):
        self.name = name
        self.hw = hw
        self.ops = []
        self.sem = None
        self.count = 0
        self.seen = {}
        self.pending = []
        self.nsem = 0


class Sched:
    def __init__(self, nc, same_eng_sync=True):
        self.nc = nc
        self.same_eng_sync = same_eng_sync
        self.engs = {
            "pe": Eng("pe", nc.tensor),
            "act": Eng("act", nc.scalar),
            "dve": Eng("dve", nc.vector),
            "pool": Eng("pool", nc.gpsimd),
            "sp": Eng("sp", nc.sync),
        }
        self.semid = 0
        self.dma_slots = {}
        self.dma_rr = {}
        self.nres = 0
        self.barrier_exempt = set()

    def res(self, name=None):
        self.nres += 1
        return Res(name or f"r{self.nres}")

    def _newsem(self, tag):
        self.semid += 1
        s = self.nc.alloc_semaphore(name=f"s{self.semid}_{tag}")
        return (self.semid, s)

    def _deps(self, e, reads, writes):
        deps = []
        for r in reads:
            if r.w is not None:
                deps.append(r.w)
        for w in writes:
            if w.w is not None:
                deps.append(w.w)
            for ev in w.r.values():
                deps.append(ev)
        waits = []
        for ev in deps:
            key, sem, val, en = ev
            if en == e.name and (e.name == "pe" or not self.same_eng_sync):
                continue
            if e.seen.get(key, 0) >= val:
                continue
            e.seen[key] = val
            waits.append((sem, val))
        return waits

    def _mark(self, ev, reads, writes, en):
        for r in reads:
            r.r[en] = ev
        for w in writes:
            w.w = ev
            w.r = {}

    def op(self, en, fn, reads=(), writes=(), inc=True):
        e = self.engs[en]
        waits = self._deps(e, reads, writes)
        if inc:
            if e.sem is None or e.count >= SEM_LIMIT:
                e.sem = self._newsem(en)
                e.count = 0
            e.count += 1
            ev = (e.sem[0], e.sem[1], e.count, en)
            for (res, mode) in e.pending:
                if mode == "r":
                    res.r[en] = ev
                else:
                    res.w = ev
                    res.r = {}
            e.pending = []
            self._mark(ev, reads, writes, en)
            e.ops.append((waits, fn, (e.sem[1], 1)))
        else:
            for r in reads:
                e.pending.append((r, "r"))
            for w in writes:
                e.pending.append((w, "w"))
            e.ops.append((waits, fn, None))

    def dma(self, q, out, in_, reads=(), writes=(), nslots=8, **kw):
        e = self.engs[q]
        waits = self._deps(e, reads, writes)
        slots = self.dma_slots.setdefault(q, [])
        if len(slots) < nslots:
            slots.append([self._newsem("dma" + q), 0])
            si = len(slots) - 1
        else:
            si = self.dma_rr.get(q, 0) % nslots
        self.dma_rr[q] = si + 1
        slot = slots[si]
        if 16 * (slot[1] + 1) > SEM_LIMIT:
            slot[0] = self._newsem("dma" + q)
            slot[1] = 0
        key, sem = slot[0]
        if slot[1] > 0 and e.seen.get(key, 0) < 16 * slot[1]:
            e.seen[key] = 16 * slot[1]
            waits.append((sem, 16 * slot[1]))
        slot[1] += 1
        ev = (key, sem, 16 * slot[1], "dma")
        self._mark(ev, reads, writes, "dma%d_%s" % (si, q))

        def fn(hw, out=out, in_=in_, kw=kw):
            return hw.dma_start(out=out, in_=in_, **kw)
        e.ops.append((waits, fn, (sem, 16)))
        return ev

    def barrier(self):
        evs = []
        for e in self.engs.values():
            assert not e.pending
            if e.sem is not None and e.count > 0:
                evs.append((e.sem[0], e.sem[1], e.count))
        for q, slots in self.dma_slots.items():
            if q in self.barrier_exempt:
                continue
            for slot in slots:
                if slot[1] > 0:
                    evs.append((slot[0][0], slot[0][1], 16 * slot[1]))
        for e in self.engs.values():
            waits = []
            for key, sem, val in evs:
                if e.seen.get(key, 0) >= val:
                    continue
                if e.sem is not None and key == e.sem[0]:
                    continue
                e.seen[key] = val
                waits.append((sem, val))
            e.ops.append((waits, None, None))

    def wait_all(self, en, resources):
        e = self.engs[en]
        waits = self._deps(e, list(resources), [])
        e.ops.append((waits, None, None))

    def emit(self):
        nc = self.nc
        for e in self.engs.values():
            assert not e.pending, f"engine {e.name} has pending non-inc ops at end"
        with nc.Block() as block:
            def run(e, hw):
                for waits, fn, inc in e.ops:
                    for (sem, val) in waits:
                        hw.wait_ge(sem, val)
                    if fn is None:
                        continue
                    ins = fn(hw)
                    if inc is not None:
                        ins.then_inc(inc[0], inc[1])

            @block.tensor
            def _(hw):
                run(self.engs["pe"], hw)

            @block.scalar
            def _(hw):
                run(self.engs["act"], hw)

            @block.vector
            def _(hw):
                run(self.engs["dve"], hw)

            @block.gpsimd
            def _(hw):
                run(self.engs["pool"], hw)

            @block.sync
            def _(hw):
                run(self.engs["sp"], hw)

    def stats(self):
        return {k: len(v.ops) for k, v in self.engs.items()}
from contextlib import ExitStack
import ml_dtypes
from concourse.bass_utils import run_bass_kernel_spmd

D = 1024
L = 4096
CT = 256
T = L + CT
DEPTH = 4
DFF = 2816
NFF = DFF // 128
EPS = 1e-6
GRID = 64
CH = 32
NCH = T // CH


class TL:
    def __init__(self, t, r):
        self.t = t
        self.r = r

    def __getitem__(self, k):
        return self.t[k]


class KB:
    def __init__(self, debug_outs=()):
        self.nc = bass.Bass("TRN2", target_bir_lowering=False)
        self.S = Sched(self.nc)
        self.dram = {}
        self.dres = {}
        self.debug_outs = set(debug_outs)
        self.nt = 0
        self.psum = []
        self.psi = 0
        self.wres = {}

    def din(self, name, shape, dt=F32):
        t = self.nc.dram_tensor(name, list(shape), dt, kind="ExternalInput")
        self.dram[name] = t
        return t

    def dscratch(self, name, shape, dt, out=False):
        kind = "ExternalOutput" if (out or name in self.debug_outs) else "Internal"
        t = self.nc.dram_tensor(name, list(shape), dt, kind=kind)
        self.dram[name] = t
        return t

    def dr(self, name, idx=0):
        k = (name, idx)
        if k not in self.dres:
            self.dres[k] = self.S.res("%s_%s" % (name, idx))
        return self.dres[k]

    def sb(self, name, shape, dt):
        self.nt += 1
        t = self.nc.alloc_sbuf_tensor("%s_%d" % (name, self.nt), list(shape), dt)
        return TL(t, self.S.res(name))

    def ring(self, name, n, shape, dt):
        return [self.sb("%s%d" % (name, i), shape, dt) for i in range(n)]

    def init_psum(self):
        for i in range(8):
            t = self.nc.alloc_psum_tensor("ps%d" % i, [128, 512], F32)
            self.psum.append(TL(t, self.S.res("ps%d" % i)))

    def bank(self):
        b = self.psum[self.psi % 8]
        self.psi += 1
        return b

    def mm(self, out, lhsT, rhs, start, stop, reads, writes, inc=None, **kw):
        if inc is None:
            inc = stop
        self.S.op("pe", lambda hw: hw.matmul(out, lhsT, rhs, start=start, stop=stop, **kw), reads, writes, inc)

    def tr(self, out, in_, ident, reads, writes, inc=True):
        self.S.op("pe", lambda hw: hw.transpose(out, in_, ident), reads, writes, inc)

    def act(self, out, in_, func, reads, writes, bias=None, scale=None, accum_out=None):
        kw = {}
        if bias is not None:
            kw["bias"] = bias
        if scale is not None:
            kw["scale"] = scale
        if accum_out is not None:
            kw["accum_out"] = accum_out
        self.S.op("act", lambda hw: hw.activation(out, in_, func, **kw), reads, writes)

    def tt(self, en, out, in0, in1, op, reads, writes):
        self.S.op(en, lambda hw: hw.tensor_tensor(out, in0, in1, op), reads, writes)

    def ts(self, en, out, in0, s1, op0, reads, writes, s2=None, op1=None):
        if op1 is None:
            self.S.op(en, lambda hw: hw.tensor_scalar(out, in0, s1, None, op0), reads, writes)
        else:
            self.S.op(en, lambda hw: hw.tensor_scalar(out, in0, s1, s2, op0, op1), reads, writes)

    def stt(self, out, in0, scalar, in1, op0, op1, reads, writes):
        self.S.op("dve", lambda hw: hw.scalar_tensor_tensor(out, in0, scalar, in1, op0, op1), reads, writes)

    def copy(self, en, out, in_, reads, writes):
        if en == "act":
            self.S.op(en, lambda hw: hw.copy(out, in_), reads, writes)
        else:
            self.S.op(en, lambda hw: hw.tensor_copy(out, in_), reads, writes)

    def memset(self, en, ap, val, writes):
        self.S.op(en, lambda hw: hw.memset(ap, val), (), writes)

    def load(self, out, in_, reads, writes, q="sp", **kw):
        return self.S.dma(q, out, in_, reads, writes, **kw)

    def setup_consts(self):
        nc = self.nc
        self.init_psum()
        self.c_ident = self.din("c_ident", [128, 128], BF16)
        self.ident = self.sb("ident", [128, 128], BF16)
        self.load(self.ident[:], self.c_ident.ap(), [], [self.ident.r])
        self.ones = self.sb("ones", [128, 128], BF16)
        self.memset("pool", self.ones[:], 1.0, [self.ones.r])

    def rms_rstd(self, xt, n, rstd, sq, width=D):
        nch = width // 128
        self.act(sq[:, 0:nch, 0:n], xt[:, 0:nch, 0:n], AF.Square, [xt.r], [sq.r])
        for n0 in range(0, n, 512):
            n1 = min(n, n0 + 512)
            bk = self.bank()
            for c in range(nch):
                self.mm(bk[:, 0:n1 - n0], self.ones[:], sq[:, c, n0:n1], c == 0, c == nch - 1,
                        [self.ones.r, sq.r], [bk.r])
            self.act(rstd[:, n0:n1], bk[:, 0:n1 - n0], AF.Ln, [bk.r], [rstd.r], bias=self.epsb[:, 0:1], scale=1.0 / width)
            self.act(rstd[:, n0:n1], rstd[:, n0:n1], AF.Exp, [rstd.r], [rstd.r], scale=-0.5)

    def xview(self, name):
        return self.dram[name].ap().rearrange("(c p) n -> p c n", p=128)

    def mod_setup(self):
        cT = self.din("cT", [128, 8, 2])
        bmodT = self.din("bmodT", [128, DEPTH, 48])
        self.wmod = self.din("w_mod", [DEPTH, D, 6 * D])
        self.epsb = self.sb("epsb", [128, 1], F32)
        self.memset("pool", self.epsb[:], EPS, [self.epsb.r])
        self.modsb = self.sb("modsb", [128, DEPTH, 48, 2], F32)
        self.modr = [self.S.res("mod%d" % l) for l in range(DEPTH)]
        csb = self.sb("csb", [128, 8, 2], F32)
        self.ssb = self.sb("ssb", [128, 8, 2], F32)
        self.bsb = self.sb("bsb", [128, DEPTH, 48], F32)
        self.identf = self.sb("identf", [2, 2], F32)
        self.load(csb[:], cT.ap(), [], [csb.r])
        self.load(self.bsb[:], bmodT.ap(), [], [self.bsb.r])
        self.act(self.ssb[:], csb[:], AF.Silu, [csb.r], [self.ssb.r])
        self.copy("dve", self.identf[:], self.ident[0:2, 0:2], [self.ident.r], [self.identf.r])

    def mod_tasks(self, l, bank_fn=None):
        bank_fn = bank_fn or self.bank
        NW = 3
        wring = self.pring("wmod", NW, [128, 8, 512], F32)
        mr = self.psb("mrow", [2, 6 * D], F32)
        wv = self.wmod.ap()[l].rearrange("(k p) n -> p k n", p=128)
        ssb, bsb = self.ssb, self.bsb
        loaded = set()

        def ld(pi):
            if pi < 12 and pi not in loaded:
                loaded.add(pi)
                w = wring[pi % NW]
                self.load(w[:], wv[:, :, pi * 512:(pi + 1) * 512], [], [w.r])

        def piece(pi):
            def f():
                for a_ in range(NW):
                    ld(pi + a_)
                w = wring[pi % NW]
                bk = bank_fn()
                for k in range(8):
                    self.mm(bk[0:2, 0:512], ssb[:, k, :], w[:, k, :], k == 0, k == 7, [w.r, ssb.r], [bk.r])
                self.copy("act", mr[0:2, pi * 512:(pi + 1) * 512], bk[0:2, 0:512], [bk.r], [mr.r])
            return f

        def fin():
            bk = bank_fn()
            for j in range(48):
                self.tr(bk[:, 2 * j:2 * j + 2], mr[0:2, j * 128:(j + 1) * 128], self.identf[0:2, 0:2], [mr.r, self.identf.r], [bk.r],
                        inc=(j == 47))
            self.tt("dve", self.modsb[:, l], bk[:, 0:96].rearrange("p (j t) -> p j t", t=2),
                    bsb[:, l, :].unsqueeze(2).broadcast_to([128, 48, 2]), ALU.add, [bk.r, bsb.r], [self.modr[l]])
            for sp in (1, 4):
                sl = self.modsb[:, l, sp * 8:(sp + 1) * 8, :]
                self.ts("dve", sl, sl, 1.0, ALU.add, [self.modr[l]], [self.modr[l]])
        def pre():
            for a_ in range(NW - 1):
                ld(a_)
        return [pre] + [piece(pi) for pi in range(12)] + [fin]

    def phase_mod0(self):
        self.begin_phase()
        for t in self.mod_tasks(0):
            t()
        self.end_phase()

    def mod(self, l, split, c, col):
        return self.modsb[:, l, split * 8 + c, col:col + 1]

    def phase_final(self, xa):
        gT = self.din("fngT", [128, 8])
        gsb = self.sb("fng", [128, 8], F32)
        self.load(gsb[:], gT.ap(), [], [gsb.r])
        outT = self.dscratch("outT", [D, L], F32, out=True)
        xv = self.xview(xa)
        ov = self.xview("outT")
        xr = self.ring("fx", 2, [128, 8, 512], F32)
        sq = self.ring("fsq", 2, [128, 8, 512], BF16)
        rs = self.ring("frs", 2, [128, 512], F32)
        yr = self.ring("fy", 2, [128, 8, 512], F32)
        for i in range(L // 512):
            x = xr[i % 2]
            self.load(x[:], xv[:, :, i * 512:(i + 1) * 512], [self.dr(xa, i)], [x.r])
            self.rms_rstd(x, 512, rs[i % 2], sq[i % 2])
            r = rs[i % 2]
            y = yr[i % 2]
            for c in range(8):
                if c < 5:
                    self.stt(y[:, c, :], x[:, c, :], gsb[:, c:c + 1], r[:], ALU.mult, ALU.mult, [x.r, r.r, gsb.r], [y.r])
                else:
                    self.tt("pool", y[:, c, :], x[:, c, :], r[:], ALU.mult, [x.r, r.r], [y.r])
                    self.act(y[:, c, :], y[:, c, :], AF.Identity, [y.r, gsb.r], [y.r], scale=gsb[:, c:c + 1])
            self.load(ov[:, :, i * 512:(i + 1) * 512], y[:], [y.r], [self.dr("outT", i)])
        self.out_res = [self.dr("outT", i) for i in range(L // 512)]

    def begin_phase(self):
        self.pstack = ExitStack()

    def psb(self, name, shape, dt):
        self.nt += 1
        t = self.pstack.enter_context(self.nc.sbuf_tensor("%s_%d" % (name, self.nt), list(shape), dt))
        return TL(t, self.S.res(name))

    def pring(self, name, n, shape, dt):
        return [self.psb("%s%d" % (name, i), shape, dt) for i in range(n)]

    def end_phase(self):
        self.S.barrier()
        self.pstack.close()

    def phase_wcast(self, specs, order):
        self.wcast_setup(specs)
        self.wcast_issue(order)

    def wcast_setup(self, specs):
        self.S.barrier_exempt.add("pool")
        self.wc = {}
        for name, shape in specs:
            src = self.din(name, shape)
            dst = self.dscratch("wb_" + name, shape, BF16)
            per = int(np.prod(shape[1:]))
            assert per % 1024 == 0
            self.wc[name] = (src, dst, per // 1024)

    def wcast_tasks(self, order):
        tasks = []
        for name, l in order:
            src, dst, rows = self.wc[name]
            sv = src.ap()[l].rearrange("a b -> (a b)").rearrange("(r n) -> r n", n=1024)
            dv = dst.ap()[l].rearrange("a b -> (a b)").rearrange("(r n) -> r n", n=1024)
            rl = []
            self.wres[(name, l)] = rl
            for r0 in range(0, rows, 1024):
                r1 = min(rows, r0 + 1024)
                rr = self.S.res("wb")
                rl.append(rr)
                tasks.append(lambda dv=dv, sv=sv, r0=r0, r1=r1, rr=rr: self.load(dv[r0:r1, :], sv[r0:r1, :], [], [rr], q="pool"))
        return tasks

    def wcast_issue(self, order):
        for t in self.wcast_tasks(order):
            t()

    def wb(self, name, l):
        return self.dram["wb_" + name].ap()[l], self.wres[(name, l)]

    def norm_mod(self, xt, n0, n1, rstd, h, l, sh_split, sc_split, col):
        n = n1 - n0
        self.act(h[:, :, n0:n1], xt[:, :, n0:n1], AF.Square, [xt.r], [h.r])
        for a in range(n0, n1, 512):
            b_ = min(n1, a + 512)
            bk = self.bank()
            for c in range(8):
                self.mm(bk[:, 0:b_ - a], self.ones[:], h[:, c, a:b_], c == 0, c == 7, [self.ones.r, h.r], [bk.r])
            self.act(rstd[:, a:b_], bk[:, 0:b_ - a], AF.Ln, [bk.r], [rstd.r], bias=self.epsb[:, 0:1], scale=1.0 / D)
            self.act(rstd[:, a:b_], rstd[:, a:b_], AF.Exp, [rstd.r], [rstd.r], scale=-0.5)
        for c in range(8):
            self.stt(h[:, c, n0:n1], xt[:, c, n0:n1], self.mod(l, sc_split, c, col), rstd[:, n0:n1], ALU.mult, ALU.mult,
                     [xt.r, rstd.r, self.modr[l]], [h.r])
            self.ts("pool", h[:, c, n0:n1], h[:, c, n0:n1], self.mod(l, sh_split, c, col), ALU.add, [h.r, self.modr[l]], [h.r],
                    s2=1.0, op1=ALU.mult)

    def phase_out(self, l, wname, widx, xin, xout):
        self.begin_phase()
        tasks = []
        wsrc, wres = self.wb(wname, widx)
        W = self.psb("wout", [128, 8, D], BF16)
        self.load(W[:], wsrc.rearrange("(k p) n -> p k n", p=128), wres, [W.r])
        mv = self.xview("mixT")
        xv = self.xview(xin)
        ov = self.xview(xout)
        mr = self.pring("omix", 2, [128, 8, 512], BF16)
        xr = self.pring("ox", 2, [128, 8, 512], F32)
        for i in range(9):
            n = 512 if i < 8 else CT
            col = 0 if i < 8 else 1
            t0 = i * 512
            m = mr[i % 2]
            x = xr[i % 2]
            self.load(m[:, :, 0:n], mv[:, :, t0:t0 + n], [self.dr("mixT", i)], [m.r])
            self.load(x[:, :, 0:n], xv[:, :, t0:t0 + n], [self.dr(xin, i)], [x.r])
            for c in range(8):
                bk = self.bank()
                for k in range(8):
                    self.mm(bk[:, 0:n], W[:, k, c * 128:(c + 1) * 128], m[:, k, 0:n], k == 0, k == 7, [W.r, m.r], [bk.r])
                self.stt(x[:, c, 0:n], bk[:, 0:n], self.mod(l, 2, c, col), x[:, c, 0:n], ALU.mult, ALU.add,
                         [bk.r, x.r, self.modr[l]], [x.r])
            self.load(ov[:, :, t0:t0 + n], x[:, :, 0:n], [x.r], [self.dr(xout, i)])
            for _ in range(2):
                if tasks:
                    tasks.pop(0)()
        while tasks:
            tasks.pop(0)()
        self.end_phase()

    def phase_ffn(self, l, xin, xout, bg=None):
        self.begin_phase()
        bg = list(bg or [])
        wd_src, wd_res = self.wb("w_down", l)
        wu_src, wu_res = self.wb("w_up", l)
        wuv = wu_src.rearrange("(k p) (g n) -> p k g n", p=128, g=2)
        wdv = wd_src.rearrange("(c p) n -> p c n", p=128)
        cw = self.psb("convw", [128, NFF, 9], F32)
        cb = self.psb("convb", [128, NFF], F32)
        self.load(cw[:], self.dram["convwT"].ap()[l], [], [cw.r])
        self.load(cb[:], self.dram["convbT"].ap()[l], [], [cb.r])
        xt = self.psb("fx", [128, 8, 1152], F32)
        h = self.psb("fh", [128, 8, 1152], BF16)
        rstd = self.psb("frstd", [128, 1152], F32)
        actT = self.psb("factT", [128, NFF, 1024], BF16)
        wur = self.pring("fwu", 3, [128, 8, 2, 256], BF16)
        Wd = self.psb("wdown", [128, NFF, D], BF16)
        self.load(Wd[:], wdv, wd_res, [Wd.r])
        Gr = self.pring("fG", 2, [128, 18 * 66], BF16)
        sr = self.pring("fsilu", 2, [128, 512], F32)
        vr = self.pring("fval", 2, [128, 1024], BF16)
        dgr = self.pring("fdiag", 2, [128, 9, 128], BF16)
        xrr = self.pring("fxr", 4, [128, 512], F32)
        for G in Gr:
            self.memset("pool", G[:], 0.0, [G.r])
        xv = self.xview(xin)
        ov = self.xview(xout)

        def geom(band):
            if band < 4:
                lo = 64 if band > 0 else 0
                hi = 64 if band < 3 else 0
                return dict(ctx=False, t0=band * 1024, lo=lo, hi=hi, ncol=1152, cen0=64, ncen=1024, col=0)
            return dict(ctx=True, t0=L, lo=0, hi=0, ncol=CT, cen0=0, ncen=CT, col=1)

        def prologue(band):
            g = geom(band)
            if not g["ctx"]:
                t0, lo, hi = g["t0"], g["lo"], g["hi"]
                rd = [self.dr(xin, i) for i in range(max(0, 2 * band - 1), min(8, 2 * band + 3))]
                if lo == 0:
                    self.memset("pool", xt[:, :, 0:64], 0.0, [xt.r])
                if hi == 0:
                    self.memset("pool", xt[:, :, 1088:1152], 0.0, [xt.r])
                self.load(xt[:, :, 64 - lo:64 + 1024 + hi], xv[:, :, t0 - lo:t0 + 1024 + hi], rd, [xt.r])
            else:
                self.load(xt[:, :, 0:CT], xv[:, :, L:L + CT], [self.dr(xin, 8)], [xt.r])
            self.norm_mod(xt, 0, g["ncol"], rstd, h, l, 3, 4, g["col"])

        wu_it = [0]
        wu_of = {}

        def load_wu(band, c):
            if c < NFF and (band, c) not in wu_of:
                wu = wur[wu_it[0] % 3]
                wu_it[0] += 1
                for g_ in range(2):
                    self.load(wu[:, :, g_, :], wuv[:, :, g_, c * 128:(c + 2) * 128], wu_res, [wu.r])
                wu_of[(band, c)] = wu
                wu_of[(band, c + 1)] = wu

        prologue(0)
        load_wu(0, 0)
        wd_it = 0
        xr_it = 0
        for band in range(5):
            g = geom(band)
            ctxb, lo, hi, t0 = g["ctx"], g["lo"], g["hi"], g["t0"]
            cen0, ncen, col = g["cen0"], g["ncen"], g["col"]
            nob = 2 if not ctxb else 1
            held = {}

            def st0(c, k):
                if bg:
                    bg.pop(0)()
                if c % 2 == 0:
                    load_wu(band, c + 2)
                wu = wu_of[(band, c)]
                wo = (c % 2) * 128
                G = Gr[c % 2]
                dg = dgr[c % 2]
                val = vr[c % 2]
                self.tt("pool", dg[:], self.ident[:].unsqueeze(1).broadcast_to([128, 9, 128]),
                        cw[:, c, :].unsqueeze(2).broadcast_to([128, 9, 128]), ALU.mult, [self.ident.r, cw.r], [dg.r])
                if not ctxb:
                    G3 = G[:].rearrange("p (r w) -> p r w", w=66)
                    for j in range(3):
                        bk = self.bank()
                        for kx in range(8):
                            self.mm(bk[:, 0:384], wu[:, kx, 0, wo:wo + 128], h[:, kx, j * 384:(j + 1) * 384], kx == 0, kx == 7,
                                    [wu.r, h.r], [bk.r])
                        ra, rb = 6 * j, 6 * j + 6
                        pa = 0
                        if j == 0 and lo == 0:
                            ra, pa = 1, 64
                        if j == 2 and hi == 0:
                            rb = 17
                        self.copy("act", G3[:, ra:rb, 1:65], bk[:, pa:pa + (rb - ra) * 64].rearrange("p (r w) -> p r w", w=64),
                                  [bk.r], [G.r])
                    if lo == 0:
                        self.memset("pool", G3[:, 0:1, :], 0.0, [G.r])
                    if hi == 0:
                        self.memset("pool", G3[:, 17:18, :], 0.0, [G.r])
                    for ob in range(2):
                        vb = self.bank()
                        for kx in range(8):
                            self.mm(vb[:, 0:512], wu[:, kx, 1, wo:wo + 128], h[:, kx, 64 + ob * 512:64 + (ob + 1) * 512], kx == 0, kx == 7,
                                    [wu.r, h.r], [vb.r])
                        self.copy("act" if ob == 0 else "dve", val[:, ob * 512:(ob + 1) * 512], vb[:, 0:512], [vb.r], [val.r])
                else:
                    bk = self.bank()
                    for kx in range(8):
                        self.mm(bk[:, 0:CT], wu[:, kx, 0, wo:wo + 128], h[:, kx, 0:CT], kx == 0, kx == 7, [wu.r, h.r], [bk.r])
                    self.memset("pool", G[:, 0:CT + 2], 0.0, [G.r])
                    self.copy("act", G[:, 1:CT + 1], bk[:, 0:CT], [bk.r], [G.r])
                    vb = self.bank()
                    for kx in range(8):
                        self.mm(vb[:, 0:CT], wu[:, kx, 1, wo:wo + 128], h[:, kx, 0:CT], kx == 0, kx == 7, [wu.r, h.r], [vb.r])
                    self.copy("act", val[:, 0:CT], vb[:, 0:CT], [vb.r], [val.r])

            def st1(c, k):
                G = Gr[c % 2]
                dg = dgr[c % 2]
                val = vr[c % 2]
                if not ctxb:
                    G3 = G[:].rearrange("p (r w) -> p r w", w=66)
                    for ob in range(2):
                        cbk = self.bank()
                        for tap in range(9):
                            dr_, dc_ = tap // 3 - 1, tap % 3 - 1
                            rhs = G3[:, 8 * ob + 1 + dr_:8 * ob + 9 + dr_, 1 + dc_:65 + dc_]
                            self.mm(cbk[:, 0:512].rearrange("p (r w) -> p r w", w=64), dg[:, tap, :], rhs, tap == 0, tap == 8,
                                    [dg.r, G.r], [cbk.r])
                        s_ = sr[ob]
                        self.act(s_[:], cbk[:, 0:512], AF.Silu, [cbk.r, cb.r], [s_.r], bias=cb[:, c:c + 1])
                        self.tt("pool" if ob == 0 else "dve", actT[:, c, ob * 512:(ob + 1) * 512], s_[:], val[:, ob * 512:(ob + 1) * 512],
                                ALU.mult, [s_.r, val.r], [actT.r])
                else:
                    cbk = self.bank()
                    for kx in range(3):
                        self.mm(cbk[:, 0:CT], dg[:, 3 + kx, :], G[:, kx:kx + CT], kx == 0, kx == 2, [dg.r, G.r], [cbk.r])
                    s_ = sr[0]
                    self.act(s_[:, 0:CT], cbk[:, 0:CT], AF.Silu, [cbk.r, cb.r], [s_.r], bias=cb[:, c:c + 1])
                    self.tt("dve", actT[:, c, 0:CT], s_[:, 0:CT], val[:, 0:CT], ALU.mult, [s_.r, val.r], [actT.r])
                    self.memset("pool", G[:, 0:CT + 2], 0.0, [G.r])

            self.pipe(list(range(NFF)), [st0, st1])
            if band + 1 < 5:
                load_wu(band + 1, 0)

            blocks = list(range(0, ncen, 512))
            for bi, a_ in enumerate(blocks):
                nb = min(512, ncen - a_)
                for dc in range(8):
                    xr = xrr[xr_it % 4]
                    xr_it += 1
                    self.load(xr[:, 0:nb], xv[:, dc, t0 + a_:t0 + a_ + nb], [self.dr(xin, (t0 + a_) // 512)], [xr.r])
                    bk = self.bank()
                    for c in range(NFF):
                        self.mm(bk[:, 0:nb], Wd[:, c, dc * 128:(dc + 1) * 128], actT[:, c, a_:a_ + nb], c == 0, c == NFF - 1, [Wd.r, actT.r], [bk.r])
                    self.stt(xr[:, 0:nb], bk[:, 0:nb], self.mod(l, 5, dc, col), xr[:, 0:nb], ALU.mult, ALU.add,
                             [bk.r, xr.r, self.modr[l]], [xr.r])
                    self.load(ov[:, dc, t0 + a_:t0 + a_ + nb], xr[:, 0:nb], [xr.r], [self.dr(xout, (t0 + a_) // 512)])
                if bi == 0 and band + 1 < 5:
                    prologue(band + 1)
        while bg:
            bg.pop(0)()
        self.end_phase()

    def head_rstd(self, src_ap, src_res, n, sq, rstd, width=128):
        self.act(sq[:, 0:n], src_ap, AF.Square, src_res, [sq.r])
        bk = self.bank()
        self.mm(bk[:, 0:n], self.ones[:], sq[:, 0:n], True, True, [self.ones.r, sq.r], [bk.r])
        self.act(rstd[:, 0:n], bk[:, 0:n], AF.Ln, [bk.r], [rstd.r], bias=self.epsb[:, 0:1], scale=1.0 / width)
        self.act(rstd[:, 0:n], rstd[:, 0:n], AF.Exp, [rstd.r], [rstd.r], scale=-0.5)

    def phase_qkv(self, l, o, xin):
        self.begin_phase()
        wsrc, wres = self.wb("w_qkv", o)
        W = self.psb("wqkv", [128, 8, 1536], BF16)
        self.load(W[:], wsrc.rearrange("(k p) n -> p k n", p=128), wres, [W.r])
        gq = self.psb("gq", [128, 2], F32)
        self.load(gq[:], self.dram["qkgT"].ap()[o], [], [gq.r])
        perm = self.psb("perm", [128, 128], BF16)
        self.load(perm[:], self.dram["c_perm"].ap(), [], [perm.r])
        xv = self.xview(xin)
        qv = self.dram["qT"].ap().rearrange("(c p) n -> p c n", p=128)
        vv = self.dram["vtok"].ap()
        xts = self.pring("qx", 2, [128, 8, 512], F32)
        hs_ = self.pring("qh", 2, [128, 8, 512], BF16)
        rstds = self.pring("qrstd", 2, [128, 512], F32)
        css = self.pring("qcos", 2, [128, 512], F32)
        sns = self.pring("qsin", 2, [128, 512], F32)
        NR = 4
        raw = self.pring("qraw", NR, [128, 512], F32)
        sq = self.pring("qsq", NR, [128, 512], BF16)
        hr = self.pring("qhr", NR, [128, 512], F32)
        qn = self.pring("qn", NR, [128, 512], BF16)
        t1 = self.pring("qt1", NR, [128, 512], F32)
        qo = self.pring("qo", NR, [128, 512], BF16)
        vs = self.pring("qvs", 2, [128, 256], BF16)

        def prologue(i):
            n = 512 if i < 8 else CT
            col = 0 if i < 8 else 1
            t0 = i * 512
            xt, h, rstd = xts[i % 2], hs_[i % 2], rstds[i % 2]
            self.load(xt[:, :, 0:n], xv[:, :, t0:t0 + n], [self.dr(xin, i)], [xt.r])
            if i < 8:
                self.load(css[i % 2][:], self.dram["cosT"].ap()[:, t0:t0 + 512], [], [css[i % 2].r])
                self.load(sns[i % 2][:], self.dram["sinT"].ap()[:, t0:t0 + 512], [], [sns[i % 2].r])
            self.norm_mod(xt, 0, n, rstd, h, l, 0, 1, col)
        prologue(0)
        NR = 4
        bkq = {}
        for i in range(9):
            n = 512 if i < 8 else CT
            t0 = i * 512
            h, cs, sn = hs_[i % 2], css[i % 2], sns[i % 2]
            items = list(range(10)) + ["v%d" % s_ for s_ in range(n // 128)]

            def q0(hd, k):
                j = k % NR
                if isinstance(hd, str):
                    s_ = int(hd[1:])
                    bk = self.bank()
                    for kx in range(8):
                        self.mm(bk[:, 0:256], h[:, kx, s_ * 128:(s_ + 1) * 128], W[:, kx, 1280:1536], kx == 0, kx == 7, [h.r, W.r], [bk.r])
                    bkq[(i, hd)] = bk
                    return
                bk = self.bank()
                for kx in range(8):
                    self.mm(bk[:, 0:n], W[:, kx, hd * 128:(hd + 1) * 128], h[:, kx, 0:n], kx == 0, kx == 7, [W.r, h.r], [bk.r])
                self.copy("act", raw[j][:, 0:n], bk[:, 0:n], [bk.r], [raw[j].r])
                self.act(sq[j][:, 0:n], bk[:, 0:n], AF.Square, [bk.r], [sq[j].r])

            def q1(hd, k):
                j = k % NR
                if isinstance(hd, str):
                    s_ = int(hd[1:])
                    bk = bkq.pop((i, hd))
                    v_ = vs[s_ % 2]
                    self.copy("act", v_[:], bk[:, 0:256], [bk.r], [v_.r])
                    self.load(vv[t0 + s_ * 128:t0 + (s_ + 1) * 128, :], v_[:], [v_.r], [self.dr("vtok", i)])
                    return
                bk = self.bank()
                self.mm(bk[:, 0:n], self.ones[:], sq[j][:, 0:n], True, True, [self.ones.r, sq[j].r], [bk.r])
                self.act(hr[j][:, 0:n], bk[:, 0:n], AF.Ln, [bk.r], [hr[j].r], bias=self.epsb[:, 0:1], scale=1.0 / 128)
                self.act(hr[j][:, 0:n], hr[j][:, 0:n], AF.Exp, [hr[j].r], [hr[j].r], scale=-0.5)
                gcol = gq[:, 0:1] if hd < 8 else gq[:, 1:2]
                dst = qn[j] if i < 8 else qo[j]
                self.stt(dst[:, 0:n], raw[j][:, 0:n], gcol, hr[j][:, 0:n], ALU.mult, ALU.mult, [raw[j].r, hr[j].r, gq.r], [dst.r])
                if i == 8:
                    self.load(qv[:, hd, t0:t0 + n], qo[j][:, 0:n], [qo[j].r], [self.dr("qT", i)])

            def q2(hd, k):
                j = k % NR
                if isinstance(hd, str) or i == 8:
                    return
                pb = self.bank()
                self.mm(pb[:, 0:n], perm[:], qn[j][:, 0:n], True, True, [perm.r, qn[j].r], [pb.r])
                self.tt("pool", t1[j][:, 0:n], qn[j][:, 0:n], cs[:, 0:n], ALU.mult, [qn[j].r, cs.r], [t1[j].r])
                self.tt("dve", raw[j][:, 0:n], pb[:, 0:n], sn[:, 0:n], ALU.mult, [pb.r, sn.r], [raw[j].r])
                self.tt("pool", qo[j][:, 0:n], t1[j][:, 0:n], raw[j][:, 0:n], ALU.add, [t1[j].r, raw[j].r], [qo[j].r])
                self.load(qv[:, hd, t0:t0 + n], qo[j][:, 0:n], [qo[j].r], [self.dr("qT", i)])
            self.pipe(items, [q0, q1, q2])
            if i + 1 < 9:
                prologue(i + 1)
        self.end_phase()

    def phase_attn(self, mod_next=None):
        self.begin_phase()
        tasks = self.mod_tasks(mod_next, lambda: self.psum[7]) if mod_next is not None and mod_next < DEPTH else []
        qv = self.dram["qT"].ap().rearrange("(c p) n -> p c n", p=128)
        mv = self.xview("mixT")
        KT = self.psb("aKT", [128, T], BF16)
        V = self.psb("aV", [128, 34, 132], BF16)
        self.memset("pool", V[:], 0.0, [V.r])
        self.memset("pool", V[:, :, 128:129], 1.0, [V.r])
        qr = self.pring("aQ", 2, [128, 512], BF16)
        LA = 2
        pr = self.pring("aP", LA + 1, [128, 512], BF16)
        rinv = self.pring("arinv", 2, [128, 4], F32)
        otok = self.pring("aOt", 2, [128, 4, 128], BF16)
        ob = self.pring("aO", 2, [128, 512], BF16)
        scale = 128 ** -0.5
        allq = [self.dr("qT", i) for i in range(9)]
        allv = [self.dr("vtok", i) for i in range(9)]
        it = 0
        pi = 0
        tbk = self.psum[7]
        tbv = tbk[:].bitcast(BF16)
        for g in range(2):
            self.load(KT[:], qv[:, 8 + g, :], allq, [KT.r])
            self.load(V[:, :, 0:128], self.dram["vtok"].ap().rearrange("(c p) d -> p c d", p=128)[:, :, g * 128:(g + 1) * 128], allv, [V.r])
            for hq in range(4):
                hd = 4 * g + hq
                for i in range(9):
                    n = 512 if i < 8 else CT
                    nsub = n // 128
                    t0 = i * 512
                    kcs = list(range(34)) if i < 8 else [32, 33]
                    q = qr[it % 2]
                    obk = [self.psum[2 * (it % 2)], self.psum[2 * (it % 2) + 1]]
                    ri, ot, o_ = rinv[it % 2], otok[it % 2], ob[it % 2]
                    it += 1
                    self.load(q[:, 0:n], qv[:, hd, t0:t0 + n], [self.dr("qT", i)], [q.r])
                    sbanks = {}
                    if tasks and it % 4 == 2:
                        tasks.pop(0)()

                    def issue_s(kc):
                        nonlocal pi
                        b_ = self.psum[4 + pi % (LA + 1)]
                        p_ = pr[pi % (LA + 1)]
                        pi += 1
                        self.mm(b_[:, 0:n], KT[:, kc * 128:(kc + 1) * 128], q[:, 0:n], True, True, [KT.r, q.r], [b_.r])
                        self.act(p_[:, 0:n], b_[:, 0:n], AF.Exp, [b_.r], [p_.r], scale=scale)
                        sbanks[kc] = p_
                    for kc in kcs[0:LA]:
                        issue_s(kc)
                    for idx, kc in enumerate(kcs):
                        if idx + LA < len(kcs):
                            issue_s(kcs[idx + LA])
                        p_ = sbanks.pop(kc)
                        first, last = idx == 0, idx == len(kcs) - 1
                        for s_ in range(nsub):
                            bk = obk[s_ // 2]
                            off = (s_ % 2) * 132
                            self.mm(bk[:, off:off + 129], p_[:, s_ * 128:(s_ + 1) * 128], V[:, kc, 0:129], first and s_ % 2 == 0, last,
                                    [p_.r, V.r], [bk.r], inc=(s_ == nsub - 1), skip_group_check=True)
                    for s_ in range(nsub):
                        bk = obk[s_ // 2]
                        off = (s_ % 2) * 132
                        self.S.op("dve", lambda hw, ri=ri, bk=bk, off=off, s_=s_: hw.reciprocal(ri[:, s_:s_ + 1], bk[:, off + 128:off + 129]),
                                  [bk.r], [ri.r])
                        self.ts("dve", ot[:, s_, :], bk[:, off:off + 128], ri[:, s_:s_ + 1], ALU.mult, [bk.r, ri.r], [ot.r])
                    for s_ in range(nsub):
                        self.tr(tbv[:, s_ * 128:(s_ + 1) * 128], ot[:, s_, :], self.ident[:], [ot.r, self.ident.r], [tbk.r], inc=(s_ == nsub - 1))
                    self.copy("act", o_[:, 0:n], tbv[:, 0:n], [tbk.r], [o_.r])
                    self.load(mv[:, hd, t0:t0 + n], o_[:, 0:n], [o_.r], [self.dr("mixT", i)])
        while tasks:
            tasks.pop(0)()
        self.end_phase()

    def pipe(self, items, stages):
        n, ns = len(items), len(stages)
        for step in range(n + ns - 1):
            for j in range(ns - 1, -1, -1):
                k = step - j
                if 0 <= k < n:
                    stages[j](items[k], k)

    def phase_ab_in(self, l, e, xin):
        self.begin_phase()
        wsrc, wres = self.wb("w_in_ab", e)
        W = self.psb("win", [128, 8, 3072], BF16)
        self.load(W[:], wsrc.rearrange("(k p) n -> p k n", p=128), wres, [W.r])
        CS = self.psb("ccs", [128, 256], BF16)
        self.load(CS[:], self.dram["c_cs"].ap(), [], [CS.r])
        cmask = self.psb("cmask", [128, 2, 128], I32)
        self.load(cmask[:], self.dram["c_mask"].ap(), [], [cmask.r])
        smask = self.psb("smask", [128, 512], F32)
        self.load(smask[:], self.dram["c_smask"].ap(), [], [smask.r])
        lg = self.psb("lg", [128, 2, 2, 4], F32)
        self.load(lg[:], self.dram["hglT"].ap(), [], [lg.r])
        lb = self.psb("lb", [128, 2, 4], F32)
        oml = self.psb("oml", [128, 2, 4], F32)
        lbm1 = self.psb("lbm1", [128, 2, 4], F32)
        if e == 0:
            self.memset("pool", lb[:], 0.0, [lb.r])
        else:
            self.tt("dve", lb[:], lg[:, 1], lg[:, 0], ALU.subtract, [lg.r], [lb.r])
            self.act(lb[:], lb[:], AF.Sigmoid, [lb.r], [lb.r])
        self.ts("dve", oml[:], lb[:], -1.0, ALU.mult, [lb.r], [oml.r], s2=1.0, op1=ALU.add)
        self.ts("dve", lbm1[:], lb[:], -1.0, ALU.add, [lb.r], [lbm1.r])
        xv = self.xview(xin)
        xt = self.psb("ax", [128, 8, 512], F32)
        hs_ = self.pring("ah", 2, [128, 8, 512], BF16)
        rstd = self.psb("arstd", [128, 512], F32)
        aT = self.pring("aT", 2, [128, 512], BF16)
        pq = self.pring("apq", 2, [128, 2, 256], BF16)
        gs = self.pring("ags", 2, [128, 512], BF16)
        vsb = self.psb("avs", [128, 4, 512], BF16)
        qs = self.psb("aqs", [128, 4, 512], F32)
        sig8 = self.psb("asig", [128, 8, 512], F32)
        R = 2
        lf = self.pring("alf", R, [128, 512], F32)
        kk = self.pring("akk", R + 1, [128, 512], F32)
        bb = self.pring("ab", R, [128, 512], F32)
        bm = self.pring("abm", R, [128, 512], F32)
        E1 = self.pring("aE1", R, [128, 512], F32)
        E2 = self.pring("aE2", R, [128, 512], F32)
        qt = [self.psb("aqt%d" % d, [128, 4, 512], BF16) for d in range(2)]
        kt = [self.psb("akt%d" % d, [128, 4, 512], BF16) for d in range(2)]
        kh = self.pring("akh", 2, [128, 512], BF16)
        khs = self.pring("akhs", 2, [128, 4, 128], BF16)
        dsb = self.pring("adsb", 2, [128, 512 // CH], F32)
        qib = self.pring("aqib", 2, [128, 512], BF16)
        attT = [self.pring("aatt%d" % d, 2, [128, 128], BF16) for d in range(2)]
        oi = self.psb("aoi", [128, 512], F32)
        for d in range(2):
            for a_ in attT[d]:
                self.memset("pool", a_[:], 0.0, [a_.r])
        PQv = self.dram["PQ"].ap()
        nl_, ncx_ = L // CH, CT // CH

        def prologue(i):
            n = 512 if i < 8 else CT
            self.load(xt[:, :, 0:n], xv[:, :, i * 512:i * 512 + n], [self.dr(xin, i)], [xt.r])
            self.norm_mod(xt, 0, n, rstd, hs_[i % 2], l, 0, 1, 0 if i < 8 else 1)
        prologue(0)
        ai = [0]
        for i in range(9):
            n = 512 if i < 8 else CT
            t0 = i * 512
            nsub = n // 128
            nch = n // CH
            h = hs_[i % 2]
            banks = {}

            def proj_to(key, c0, M=None):
                bk = self.bank()
                for k in range(8):
                    self.mm(bk[:, 0:n], W[:, k, c0:c0 + 128], h[:, k, 0:n], k == 0, k == 7, [W.r, h.r], [bk.r])
                banks[key] = bk

            items = [("a", g) for g in range(4)] + [("g", hd) for hd in range(4)] + [("v", s_) for s_ in range(nsub)] + \
                    [("q", hd) for hd in range(4)] + [("z", j) for j in range(8)]

            def p1s0(it, k):
                kind, j = it
                if kind == "a":
                    proj_to(it, j * 128)
                elif kind == "g":
                    proj_to(it, 2560 + j * 128)
                elif kind == "q":
                    proj_to(it, 512 + j * 128)
                elif kind == "z":
                    proj_to(it, 1024 + j * 128)
                else:
                    bk = self.bank()
                    for kx in range(8):
                        self.mm(bk[:, 0:512], h[:, kx, j * 128:(j + 1) * 128], W[:, kx, 2048:2560], kx == 0, kx == 7, [h.r, W.r], [bk.r])
                    banks[it] = bk

            def p1s1(it, k):
                kind, j = it
                bk = banks.pop(it)
                if kind == "a":
                    self.copy("dve", aT[j % 2][:, 0:n], bk[:, 0:n], [bk.r], [aT[j % 2].r])
                elif kind == "g":
                    self.act(sig8[:, j, 0:n], bk[:, 0:n], AF.Sigmoid, [bk.r], [sig8.r])
                    g_ = gs[j % 2]
                    self.tt("dve", g_[:, 0:n], bk[:, 0:n], sig8[:, j, 0:n], ALU.mult, [bk.r, sig8.r], [g_.r])
                    self.load(self.dram["gT"].ap()[j * 128:(j + 1) * 128, t0:t0 + n], g_[:, 0:n], [g_.r], [self.dr("gT", i)])
                elif kind == "q":
                    self.act(sig8[:, 4 + j, 0:n], bk[:, 0:n], AF.Sigmoid, [bk.r], [sig8.r])
                    self.tt("dve", qs[:, j, 0:n], bk[:, 0:n], sig8[:, 4 + j, 0:n], ALU.mult, [bk.r, sig8.r], [qs.r])
                elif kind == "z":
                    self.act(sig8[:, j, 0:n], bk[:, 0:n], AF.Sigmoid, [bk.r], [sig8.r])
                else:
                    self.copy("act", vsb[:, j, :], bk[:, 0:512], [bk.r], [vsb.r])
                    if j == nsub - 1:
                        self.load(self.dram["vtok2"].ap()[t0:t0 + n, :].rearrange("(s p) d -> p s d", p=128), vsb[:, 0:nsub, :], [vsb.r],
                                  [self.dr("vtok2", i)])

            def p1s2(it, k):
                kind, g = it
                if kind != "a":
                    return
                for s2 in range(0, nsub, 2):
                    b2 = self.bank()
                    ns2 = min(2, nsub - s2)
                    for u in range(ns2):
                        self.mm(b2[:, u * 256:(u + 1) * 256], aT[g % 2][:, (s2 + u) * 128:(s2 + u + 1) * 128], CS[:], True, True,
                                [aT[g % 2].r, CS.r], [b2.r], inc=(u == ns2 - 1))
                    p_ = pq[(s2 // 2) % 2]
                    self.copy("act", p_[:, 0:ns2, :], b2[:, 0:ns2 * 256].rearrange("p (u n) -> p u n", n=256), [b2.r], [p_.r])
                    self.load(PQv[t0 + s2 * 128:t0 + (s2 + ns2) * 128, g, :].rearrange("(u p) n -> p u n", p=128), p_[:, 0:ns2, :],
                              [p_.r], [self.dr("PQ", i)])
            self.pipe(items, [p1s0, p1s1, p1s2])

            if i + 1 < 9:
                prologue(i + 1)

            items2 = [(d, hd) for d in range(2) for hd in range(4)]

            def b3of(t):
                return t[:, 0:n].rearrange("p (c t) -> p c t", t=CH)

            def s1(it, k):
                d, hd = it
                j = d * 4 + hd
                r = k % R
                self.ts("dve", lf[r][:, 0:n], sig8[:, j, 0:n], oml[:, d, hd:hd + 1], ALU.mult, [sig8.r, oml.r, lb.r], [lf[r].r],
                        s2=lb[:, d, hd:hd + 1], op1=ALU.add)
                self.act(lf[r][:, 0:n], lf[r][:, 0:n], AF.Ln, [lf[r].r], [lf[r].r])
                kr = kk[k % (R + 1)]
                self.act(kr[:, 0:n], sig8[:, j, 0:n], AF.Identity, [sig8.r, lbm1.r, oml.r], [kr.r],
                         bias=oml[:, d, hd:hd + 1], scale=lbm1[:, d, hd:hd + 1])

            def s2(it, k):
                d, hd = it
                r = k % R
                b = bb[r]
                if d == 0:
                    bo_, lo_ = b[:, 0:n], lf[r][:, 0:n]
                else:
                    bo_, lo_ = b[:, 0:n][:, ::-1], lf[r][:, 0:n][:, ::-1]
                self.S.op("dve", lambda hw, bo_=bo_, lo_=lo_, sm_=smask[:, 0:n]: hw.tensor_tensor_scan(bo_, sm_, lo_, 0.0, ALU.mult, ALU.add),
                          [smask.r, lf[r].r], [b.r])
                b3 = b3of(b)
                self.tt("dve", b3of(bm[r]), b3, b3[:, :, CH // 2:CH // 2 + 1].broadcast_to([128, nch, CH]), ALU.subtract, [b.r], [bm[r].r])
                self.act(E1[r][:, 0:n], bm[r][:, 0:n], AF.Exp, [bm[r].r], [E1[r].r])
                self.act(E2[r][:, 0:n], bm[r][:, 0:n], AF.Exp, [bm[r].r], [E2[r].r], scale=-1.0)

            def s3(it, k):
                d, hd = it
                r = k % R
                b = bb[r]
                kr = kk[k % (R + 1)]
                li = CH - 1 if d == 0 else 0
                b3 = b3of(b)
                self.tt("dve", qt[d][:, hd, 0:n], qs[:, hd, 0:n], E1[r][:, 0:n], ALU.mult, [qs.r, E1[r].r], [qt[d].r])
                self.tt("pool", kt[d][:, hd, 0:n], kr[:, 0:n], E2[r][:, 0:n], ALU.mult, [kr.r, E2[r].r], [kt[d].r])
                self.tt("dve", b3of(bm[r]), b3[:, :, li:li + 1].broadcast_to([128, nch, CH]), b3, ALU.subtract, [b.r], [bm[r].r])
                self.act(E1[r][:, 0:n], bm[r][:, 0:n], AF.Exp, [bm[r].r], [E1[r].r])
                self.act(E2[r][:, 0:n], b[:, 0:n], AF.Exp, [b.r], [E2[r].r])
                ds_ = dsb[k % 2]
                c0_ = t0 // CH
                if d == 0:
                    p0_ = (ncx_ + c0_) if i < 8 else 0
                    self.act(ds_[:, 0:nch], b3[:, :, li], AF.Exp, [b.r], [ds_.r])
                else:
                    p0_ = (ncx_ + nl_ - c0_ - nch) if i < 8 else 0
                    self.act(ds_[:, 0:nch][:, ::-1], b3[:, :, li], AF.Exp, [b.r], [ds_.r])
                self.load(self.dram["dT"].ap()[d, hd * 128:(hd + 1) * 128, p0_:p0_ + nch], ds_[:, 0:nch], [ds_.r], [self.dr("dT", i)])
                self.load(self.dram["qtT"].ap()[d, hd * 128:(hd + 1) * 128, t0:t0 + n], qt[d][:, hd, 0:n], [qt[d].r], [self.dr("qtT", i)])

            def s4(it, k):
                d, hd = it
                r = k % R
                kr = kk[k % (R + 1)]
                kh_ = kh[k % 2]
                qi_ = qib[k % 2]
                self.tt("pool", kh_[:, 0:n], kr[:, 0:n], E1[r][:, 0:n], ALU.mult, [kr.r, E1[r].r], [kh_.r])
                self.tt("pool", qi_[:, 0:n], qs[:, hd, 0:n], E2[r][:, 0:n], ALU.mult, [qs.r, E2[r].r], [qi_.r])
                self.load(self.dram["qiT"].ap()[d, hd * 128:(hd + 1) * 128, t0:t0 + n], qi_[:, 0:n], [qi_.r], [self.dr("qiT", i)])
                tb = self.bank()
                tbv = tb[:].bitcast(BF16)
                for s_ in range(nsub):
                    self.tr(tbv[:, s_ * 128:(s_ + 1) * 128], kh_[:, s_ * 128:(s_ + 1) * 128], self.ident[:], [kh_.r, self.ident.r], [tb.r],
                            inc=(s_ == nsub - 1))
                k_ = khs[k % 2]
                self.copy("act", k_[:, 0:nsub, :], tbv[:, 0:nsub * 128].rearrange("p (s k) -> p s k", k=128), [tb.r], [k_.r])
                self.load(self.dram["khat"].ap()[d, t0:t0 + n, hd * 128:(hd + 1) * 128].rearrange("(s p) k -> p s k", p=128), k_[:, 0:nsub, :],
                          [k_.r], [self.dr("khat", i)])
            self.pipe(items2, [s1, s2, s3, s4])

            items3 = [(hd, s_) for hd in range(4) for s_ in range(nsub)]
            abk = {}

            def i0(it, k):
                hd, s_ = it
                sl = slice(s_ * 128, (s_ + 1) * 128)
                for d in range(2):
                    ab_ = self.psum[2 + self.psi % 6]
                    self.psi += 1
                    self.mm(ab_[:, 0:128], kt[d][:, hd, sl], qt[d][:, hd, sl], True, True, [kt[d].r, qt[d].r], [ab_.r])
                    abk[(it, d)] = ab_

            def i1(it, k):
                hd, s_ = it
                sl = slice(s_ * 128, (s_ + 1) * 128)
                obk = self.psum[hd % 2]
                ats = []
                for d in range(2):
                    ab_ = abk.pop((it, d))
                    a_ = attT[d][ai[0] % 2]
                    self.S.op("dve", lambda hw, a_=a_, ab_=ab_, d=d: hw.copy_predicated(a_[:], cmask[:, d, :], ab_[:, 0:128]),
                              [cmask.r, ab_.r], [a_.r])
                    ats.append(a_)
                ai[0] += 1
                self.mm(obk[:, sl], vsb[:, s_, hd * 128:(hd + 1) * 128], ats[0][:], True, False, [vsb.r, ats[0].r], [obk.r], inc=False)
                self.mm(obk[:, sl], vsb[:, s_, hd * 128:(hd + 1) * 128], ats[1][:], False, True, [vsb.r, ats[1].r], [obk.r], inc=True)
                if s_ == nsub - 1:
                    self.copy("act", oi[:, 0:n], obk[:, 0:n], [obk.r], [oi.r])
                    self.load(self.dram["ointra"].ap()[hd * 128:(hd + 1) * 128, t0:t0 + n], oi[:, 0:n], [oi.r], [self.dr("ointra", i)])
            self.pipe(items3, [i0, i1])
        self.end_phase()

    def phase_fourier(self, mod_next=None):
        self.begin_phase()
        fj = [0]
        tasks = self.mod_tasks(mod_next, lambda: self.psum[4 if fj[0] % 2 == 0 else 0]) if mod_next is not None and mod_next < DEPTH else []
        PQ = self.psb("fPQ", [128, 34, 4, 256], BF16)
        allpq = [self.dr("PQ", i) for i in range(9)]
        pv = self.dram["PQ"].ap().rearrange("(c p) g n -> p c (g n)", p=128)
        for c0 in range(0, 34, 2):
            self.load(PQ[:, c0:c0 + 2].rearrange("p c g n -> p c (g n)"), pv[:, c0:c0 + 2, :], allpq, [PQ.r])
        cr = self.pring("fC", 3, [128, 2, 4, 512], BF16)
        yo = self.pring("fy", 2, [128, 512], BF16)
        dft = self.dram["c_dft"].ap()
        mv = self.xview("mixT")
        ci = 0
        for j in range(8):
            banks = [self.psum[g] for g in range(4)] if j % 2 == 0 else [self.psum[4 + g] for g in range(4)]
            for tq in range(8):
                if tasks and (j * 8 + tq) % 4 == 1:
                    fj[0] = j
                    tasks.pop(0)()
                c_ = cr[ci % 3]
                ci += 1
                for z in range(2):
                    self.load(c_[:, z], dft[z, tq * 512:(tq + 1) * 512, j * 512:(j + 1) * 512].rearrange("(t p) n -> p t n", p=128),
                              [], [c_.r])
                for t4 in range(4):
                    tc = tq * 4 + t4
                    for g in range(4):
                        self.mm(banks[g][:, 0:512], PQ[:, tc, g, 0:128], c_[:, 0, t4, :], tc == 0, False, [PQ.r, c_.r], [banks[g].r], inc=False)
                        self.mm(banks[g][:, 0:512], PQ[:, tc, g, 128:256], c_[:, 1, t4, :], False, tc == 31, [PQ.r, c_.r], [banks[g].r],
                                inc=(tc == 31 or (t4 == 3 and g == 3)))
            for g in range(4):
                y = yo[g % 2]
                self.copy("act" if g % 2 == 0 else "dve", y[:], banks[g][:, 0:512], [banks[g].r], [y.r])
                self.load(mv[:, g, j * 512:(j + 1) * 512], y[:], [y.r], [self.dr("mixT", j)])
        while tasks:
            tasks.pop(0)()
        c2 = self.psb("fC2", [128, 2, 2, 256], BF16)
        for z in range(2):
            self.load(c2[:, z], self.dram["c_dft256"].ap()[z].rearrange("(c p) n -> p c n", p=128), [], [c2.r])
        for g in range(4):
            bk = self.bank()
            for tc in range(2):
                self.mm(bk[:, 0:256], PQ[:, 32 + tc, g, 0:128], c2[:, 0, tc, :], tc == 0, False, [PQ.r, c2.r], [bk.r], inc=False)
                self.mm(bk[:, 0:256], PQ[:, 32 + tc, g, 128:256], c2[:, 1, tc, :], False, tc == 1, [PQ.r, c2.r], [bk.r], inc=(tc == 1))
            y = yo[g % 2]
            self.copy("act", y[:, 0:256], bk[:, 0:256], [bk.r], [y.r])
            self.load(mv[:, g, L:L + CT], y[:, 0:256], [y.r], [self.dr("mixT", 8)])
        self.end_phase()

    def phase_hgrn_scan(self, e):
        self.begin_phase()
        nl, ncx = L // CH, CT // CH
        PC = 8
        NP = NCH // PC
        gn = self.psb("hgn", [128, 4], F32)
        self.load(gn[:], self.dram["gnT"].ap()[e], [], [gn.r])
        oacc = self.psb("hoacc", [128, 4, T], F32)
        S32 = self.psb("hS32", [128, 8, 128], F32)
        Sbf = self.pring("hSbf", 8, [128, 128], BF16)
        DD = self.psb("hDD", [128, 8, NCH], F32)
        KHr = self.pring("hKH", 2, [CH, 8, PC, 128], BF16)
        VHr = self.pring("hVH", 2, [CH, 8, PC, 128], BF16)
        QIr = self.pring("hQI", 2, [128, 8, PC * CH], BF16)
        sq = self.psb("hsq", [128, 512], BF16)
        rstd = self.psb("hrstd", [128, 512], F32)
        tmp = self.psb("htmp", [128, 512], F32)
        gsb = self.pring("hgs", 2, [128, 512], BF16)
        ob = self.pring("hob", 2, [128, 512], BF16)
        mv = self.xview("mixT")
        al = lambda nm: [self.dr(nm, i) for i in range(9)]
        S32r = [self.S.res("S32_%d" % c_) for c_ in range(8)]
        oar = [self.S.res("oacc_%d" % c_) for c_ in range(4)]
        self.memset("pool", S32[:], 0.0, S32r)
        for ch in range(8):
            self.memset("pool", Sbf[ch][:], 0.0, [Sbf[ch].r])
        for hd in range(4):
            self.load(oacc[:, hd, :], self.dram["ointra"].ap()[hd * 128:(hd + 1) * 128, :], al("ointra"), [oar[hd]])
        for ch in range(8):
            d, hd = ch // 4, ch % 4
            self.load(DD[:, ch, :], self.dram["dT"].ap()[d, hd * 128:(hd + 1) * 128, :], al("dT"), [DD.r])
        kv = self.dram["khat"].ap()
        vv = self.dram["vtok2"].ap()
        qiv = self.dram["qiT"].ap()

        def chunk_range(d, j):
            if j == 0:
                return nl
            return (j - 1) * PC if d == 0 else nl - j * PC

        def load_piece(j):
            kh, vh, qi = KHr[j % 2], VHr[j % 2], QIr[j % 2]
            for ch in range(8):
                d, hd = ch // 4, ch % 4
                c0 = chunk_range(d, j)
                ts_ = slice(c0 * CH, (c0 + PC) * CH)
                hs = slice(hd * 128, (hd + 1) * 128)
                self.load(kh[:, ch], kv[d, ts_, hs].rearrange("(c p) k -> p c k", p=CH), al("khat"), [kh.r])
                self.load(vh[:, ch], vv[ts_, hs].rearrange("(c p) k -> p c k", p=CH), al("vtok2"), [vh.r])
                self.load(qi[:, ch, :], qiv[d, hs, ts_], al("qiT"), [qi.r])

        def bidx(d, q):
            return q if d == 0 else PC - 1 - q

        def emit_U(p):
            j, q = p // PC, p % PC
            kh, vh = KHr[j % 2], VHr[j % 2]
            for half in range(2):
                ub = self.psum[4 + 2 * (p % 2) + half]
                for c4 in range(4):
                    ch = half * 4 + c4
                    ix = bidx(ch // 4, q)
                    self.mm(ub[:, c4 * 128:(c4 + 1) * 128], kh[:, ch, ix, :], vh[:, ch, ix, :], True, True, [kh.r, vh.r], [ub.r],
                            inc=(c4 == 3))

        load_piece(0)
        emit_U(0)
        for p in range(NCH):
            j, q = p // PC, p % PC
            if q == 0 and j + 1 < NP:
                load_piece(j + 1)
            if p + 1 < NCH:
                emit_U(p + 1)
            qi = QIr[j % 2]
            for ch in range(8):
                d = ch // 4
                ix = bidx(d, q)
                ib = self.psum[ch // 2]
                col = (ch % 2) * 256 + ix * CH
                self.mm(ib[:, col:col + CH], Sbf[ch][:], qi[:, ch, ix * CH:(ix + 1) * CH], True, True, [Sbf[ch].r, qi.r], [ib.r],
                        inc=(ch % 2 == 1))
            if q == PC - 1:
                for ch in range(8):
                    d, hd = ch // 4, ch % 4
                    c0 = chunk_range(d, j)
                    ib = self.psum[ch // 2]
                    cb_ = (ch % 2) * 256
                    dst = oacc[:, hd, c0 * CH:(c0 + PC) * CH]
                    self.tt("dve", dst, dst, ib[:, cb_:cb_ + PC * CH], ALU.add, [oar[hd], ib.r], [oar[hd]])
            for ch in range(8):
                ub = self.psum[4 + 2 * (p % 2) + ch // 4]
                c4 = ch % 4
                self.stt(S32[:, ch, :], S32[:, ch, :], DD[:, ch, p:p + 1], ub[:, c4 * 128:(c4 + 1) * 128], ALU.mult, ALU.add,
                         [S32r[ch], DD.r, ub.r], [S32r[ch]])
            if p + 1 < NCH:
                for ch in range(8):
                    self.copy("act", Sbf[ch][:], S32[:, ch, :], [S32r[ch]], [Sbf[ch].r])
        for hd in range(4):
            hs = slice(hd * 128, (hd + 1) * 128)
            for i in range(9):
                n = 512 if i < 8 else CT
                t0 = i * 512
                g_ = gsb[i % 2]
                o_ = ob[i % 2]
                self.load(g_[:, 0:n], self.dram["gT"].ap()[hs, t0:t0 + n], al("gT"), [g_.r])
                self.head_rstd(oacc[:, hd, t0:t0 + n], [oar[hd]], n, sq, rstd)
                self.stt(tmp[:, 0:n], oacc[:, hd, t0:t0 + n], gn[:, hd:hd + 1], rstd[:, 0:n], ALU.mult, ALU.mult, [oar[hd], gn.r, rstd.r], [tmp.r])
                self.tt("pool", o_[:, 0:n], tmp[:, 0:n], g_[:, 0:n], ALU.mult, [tmp.r, g_.r], [o_.r])
                self.load(mv[:, 4 + hd, t0:t0 + n], o_[:, 0:n], [o_.r], [self.dr("mixT", i)])
        self.end_phase()

    def finish(self):
        self.S.wait_all("sp", self.out_res)
        self.S.emit()
        return self.nc


def bf(a):
    return np.asarray(a, dtype=np.float32).astype(ml_dtypes.bfloat16)


def host_consts():
    c = {}
    c["c_ident"] = bf(np.eye(128))
    return c


def prep_core(b, inp, consts):
    f = lambda a: np.ascontiguousarray(np.asarray(a, dtype=np.float32))
    m = dict(consts)
    m["xT"] = f(np.concatenate([inp["x"][b].T, inp["ctx"][b].T], axis=1))
    cc = np.stack([inp["c"][b], inp["c_ctx"]], axis=1)
    m["cT"] = f(cc.reshape(8, 128, 2).transpose(1, 0, 2))
    m["bmodT"] = f(inp["b_mod"].reshape(DEPTH, 48, 128).transpose(2, 0, 1))
    m["w_mod"] = f(inp["w_mod"])
    m["fngT"] = f(inp["final_norm_g"].reshape(8, 128).T)
    return m


def prep_weights(inp):
    f = lambda a: np.ascontiguousarray(np.asarray(a, dtype=np.float32))
    m = {}
    for k in ("w_mod", "w_in_ab", "w_out_ab", "w_qkv", "w_out_att", "w_up", "w_down"):
        m[k] = f(inp[k])
    m["bmodT"] = f(inp["b_mod"].reshape(DEPTH, 48, 128).transpose(2, 0, 1))
    m["fngT"] = f(inp["final_norm_g"].reshape(8, 128).T)
    cwt = np.asarray(inp["conv_w"]).reshape(DEPTH, 9, NFF, 128).transpose(0, 3, 2, 1)
    m["convwT"] = f(cwt)
    m["convbT"] = f(np.asarray(inp["conv_b"]).reshape(DEPTH, NFF, 128).transpose(0, 2, 1))
    return m


def rope_tables():
    t = np.arange(L)
    row = (t // GRID).astype(np.float32)
    colp = (t % GRID).astype(np.float32)
    nf = 32
    freqs = (10000.0 ** (-np.arange(nf, dtype=np.float32) / nf)).astype(np.float32)
    ang = np.concatenate([row[:, None] * freqs, colp[:, None] * freqs], axis=-1)
    cos = np.repeat(np.cos(ang), 2, axis=1).T
    sin = np.repeat(np.sin(ang), 2, axis=1).T
    perm = np.zeros((128, 128), np.float32)
    for dp in range(128):
        if dp % 2 == 0:
            perm[dp + 1, dp] = -1.0
        else:
            perm[dp - 1, dp] = 1.0
    return np.ascontiguousarray(cos, np.float32), np.ascontiguousarray(sin, np.float32), bf(perm)


def ab_consts():
    c = {}
    ch = np.arange(128)
    ang = 2 * np.pi * np.outer(ch, ch) / 128.0
    c["c_cs"] = bf(np.concatenate([np.cos(ang), -np.sin(ang)], axis=1) / np.sqrt(128.0))
    t = np.arange(L, dtype=np.int64)
    m = np.outer(t, t) % L
    a = (2 * np.pi / L) * m
    c["c_dft"] = np.stack([bf(np.cos(a) / 64.0), bf(np.sin(a) / 64.0)])
    t2 = np.arange(CT, dtype=np.int64)
    a2 = (2 * np.pi / CT) * (np.outer(t2, t2) % CT)
    c["c_dft256"] = np.stack([bf(np.cos(a2) / 16.0), bf(np.sin(a2) / 16.0)])
    s_ = np.arange(128)[:, None]
    t_ = np.arange(128)[None, :]
    same = (s_ // CH) == (t_ // CH)
    mk = np.stack([(same & (s_ <= t_)), (same & (s_ >= t_))], axis=1).astype(np.int32)
    c["c_mask"] = np.ascontiguousarray(mk)
    sm = np.ones((128, 512), np.float32)
    sm[:, ::CH] = 0.0
    c["c_smask"] = sm
    return c


def build_full():
    K = KB()
    K.setup_consts()
    K.din('xT', [D, T])
    K.din('convwT', [DEPTH, 128, NFF, 9]); K.din('convbT', [DEPTH, 128, NFF])
    K.din('qkgT', [2, 128, 2]); K.din('c_perm', [128, 128], BF16); K.din('cosT', [128, L]); K.din('sinT', [128, L])
    K.din('c_cs', [128, 256], BF16); K.din('c_dft', [2, L, L], BF16); K.din('c_dft256', [2, CT, CT], BF16)
    K.din('c_mask', [128, 2, 128], I32); K.din('c_smask', [128, 512]); K.din('hglT', [128, 2, 2, 4]); K.din('gnT', [2, 128, 4])
    K.dscratch('xa', [D, T], F32); K.dscratch('xb', [D, T], F32); K.dscratch('mixT', [D, T], BF16)
    K.dscratch('qT', [1280, T], BF16); K.dscratch('vtok', [T, 256], BF16)
    K.dscratch('PQ', [T, 4, 256], BF16); K.dscratch('gT', [512, T], BF16); K.dscratch('vtok2', [T, 512], BF16)
    K.dscratch('qtT', [2, 512, T], BF16); K.dscratch('khat', [2, T, 512], BF16)
    K.dscratch('dT', [2, 512, NCH], F32); K.dscratch('qiT', [2, 512, T], BF16); K.dscratch('ointra', [512, T], F32)
    def worder(l):
        o = [('w_in_ab', l // 2), ('w_out_ab', l // 2)] if l % 2 == 0 else [('w_qkv', l // 2), ('w_out_att', l // 2)]
        return o + [('w_up', l), ('w_down', l)]
    K.wcast_setup([('w_in_ab', [2, D, 3072]), ('w_out_ab', [2, D, D]), ('w_up', [DEPTH, D, 2 * DFF]), ('w_down', [DEPTH, DFF, D]),
                   ('w_qkv', [2, D, 1536]), ('w_out_att', [2, D, D])])
    K.wcast_issue(worder(0))
    K.mod_setup()
    K.phase_mod0()
    for l in range(DEPTH):
        xin = 'xT' if l == 0 else 'xa'
        if l % 2 == 0:
            K.phase_ab_in(l, l // 2, xin)
            K.phase_fourier(l + 1)
            K.phase_hgrn_scan(l // 2)
            K.phase_out(l, 'w_out_ab', l // 2, xin, 'xb')
        else:
            K.phase_qkv(l, l // 2, xin)
            K.phase_attn(l + 1)
            K.phase_out(l, 'w_out_att', l // 2, xin, 'xb')
        K.phase_ffn(l, 'xb', 'xa', bg=(K.wcast_tasks(worder(l + 1)) if l + 1 < DEPTH else None))
    K.phase_final('xa')
    nc = K.finish()
    return nc, K


def kernel(**inputs):
    inp = {k: np.asarray(v) for k, v in inputs.items()}
    nc, K = build_full()
    shared = dict(host_consts())
    shared.update(prep_weights(inp))
    shared.update(ab_consts())
    cos, sin, perm = rope_tables()
    shared['cosT'] = cos; shared['sinT'] = sin; shared['c_perm'] = perm
    shared['qkgT'] = np.ascontiguousarray(np.stack([inp['q_norm_g'], inp['k_norm_g']], axis=2).astype(np.float32))
    shared['hglT'] = np.ascontiguousarray(inp['hg_lb_logits'].reshape(2, 2, 4, 128).transpose(3, 0, 1, 2).astype(np.float32))
    shared['gnT'] = np.ascontiguousarray(inp['hg_norm_g'].reshape(2, 4, 128).transpose(0, 2, 1).astype(np.float32))
    in_maps = []
    for b in range(8):
        m = dict(shared)
        m.update(prep_core(b, inp, {}))
        in_maps.append({k: v for k, v in m.items() if k in K.dram})
    res = run_bass_kernel_spmd(nc, in_maps, core_ids=list(range(8)))
    out = np.stack([np.ascontiguousarray(r['outT'].T) for r in res.results], axis=0)
    return out.astype(np.float32)
```

```python
import numpy as np
import concourse.bass as bass
import concourse.mybir as mybir

F32 = mybir.dt.float32
BF16 = mybir.dt.bfloat16
I32 = mybir.dt.int32
U8 = mybir.dt.uint8
AF = mybir.ActivationFunctionType
ALU = mybir.AluOpType

SEM_LIMIT = 30000


class Res:
    __slots__ = ("name", "w", "r")

    def __init__(self, name):
        self.name = name
        self.w = None
        self.r = {}


class Eng:
    def __init__(self, name, hw):
        self.name = name
        self.hw = hw
        self.ops = []
        self.sem = None
        self.count = 0
        self.seen = {}
        self.pending = []
        self.nsem = 0


class Sched:
    def __init__(self, nc, same_eng_sync=True):
        self.nc = nc
        self.same_eng_sync = same_eng_sync
        self.engs = {
            "pe": Eng("pe", nc.tensor),
            "act": Eng("act", nc.scalar),
            "dve": Eng("dve", nc.vector),
            "pool": Eng("pool", nc.gpsimd),
            "sp": Eng("sp", nc.sync),
        }
        self.semid = 0
        self.dma_slots = {}
        self.dma_rr = {}
        self.nres = 0
        self.barrier_exempt = set()

    def res(self, name=None):
        self.nres += 1
        return Res(name or f"r{self.nres}")

    def _newsem(self, tag):
        self.semid += 1
        s = self.nc.alloc_semaphore(name=f"s{self.semid}_{tag}")
        return (self.semid, s)

    def _deps(self, e, reads, writes):
        deps = []
        for r in reads:
            if r.w is not None:
                deps.append(r.w)
        for w in writes:
            if w.w is not None:
                deps.append(w.w)
            for ev in w.r.values():
                deps.append(ev)
        waits = []
        for ev in deps:
            key, sem, val, en = ev
            if en == e.name and (e.name == "pe" or not self.same_eng_sync):
                continue
            if e.seen.get(key, 0) >= val:
                continue
            e.seen[key] = val
            waits.append((sem, val))
        return waits

    def _mark(self, ev, reads, writes, en):
        for r in reads:
            r.r[en] = ev
        for w in writes:
            w.w = ev
            w.r = {}

    def op(self, en, fn, reads=(), writes=(), inc=True):
        e = self.engs[en]
        waits = self._deps(e, reads, writes)
        if inc:
            if e.sem is None or e.count >= SEM_LIMIT:
                e.sem = self._newsem(en)
                e.count = 0
            e.count += 1
            ev = (e.sem[0], e.sem[1], e.count, en)
            for (res, mode) in e.pending:
                if mode == "r":
                    res.r[en] = ev
                else:
                    res.w = ev
                    res.r = {}
            e.pending = []
            self._mark(ev, reads, writes, en)
            e.ops.append((waits, fn, (e.sem[1], 1)))
        else:
            for r in reads:
                e.pending.append((r, "r"))
            for w in writes:
                e.pending.append((w, "w"))
            e.ops.append((waits, fn, None))

    def dma(self, q, out, in_, reads=(), writes=(), nslots=8, **kw):
        e = self.engs[q]
        waits = self._deps(e, reads, writes)
        slots = self.dma_slots.setdefault(q, [])
        if len(slots) < nslots:
            slots.append([self._newsem("dma" + q), 0])
            si = len(slots) - 1
        else:
            si = self.dma_rr.get(q, 0) % nslots
        self.dma_rr[q] = si + 1
        slot = slots[si]
        if 16 * (slot[1] + 1) > SEM_LIMIT:
            slot[0] = self._newsem("dma" + q)
            slot[1] = 0
        key, sem = slot[0]
        if slot[1] > 0 and e.seen.get(key, 0) < 16 * slot[1]:
            e.seen[key] = 16 * slot[1]
            waits.append((sem, 16 * slot[1]))
        slot[1] += 1
        ev = (key, sem, 16 * slot[1], "dma")
        self._mark(ev, reads, writes, "dma%d_%s" % (si, q))

        def fn(hw, out=out, in_=in_, kw=kw):
            return hw.dma_start(out=out, in_=in_, **kw)
        e.ops.append((waits, fn, (sem, 16)))
        return ev

    def barrier(self):
        evs = []
        for e in self.engs.values():
            assert not e.pending
            if e.sem is not None and e.count > 0:
                evs.append((e.sem[0], e.sem[1], e.count))
        for q, slots in self.dma_slots.items():
            if q in self.barrier_exempt:
                continue
            for slot in slots:
                if slot[1] > 0:
                    evs.append((slot[0][0], slot[0][1], 16 * slot[1]))
        for e in self.engs.values():
            waits = []
            for key, sem, val in evs:
                if e.seen.get(key, 0) >= val:
                    continue
                if e.sem is not None and key == e.sem[0]:
                    continue
                e.seen[key] = val
                waits.append((sem, val))
            e.ops.append((waits, None, None))

    def wait_all(self, en, resources):
        e = self.engs[en]
        waits = self._deps(e, list(resources), [])
        e.ops.append((waits, None, None))

    def emit(self):
        nc = self.nc
        for e in self.engs.values():
            assert not e.pending, f"engine {e.name} has pending non-inc ops at end"
        with nc.Block() as block:
            def run(e, hw):
                for waits, fn, inc in e.ops:
                    for (sem, val) in waits:
                        hw.wait_ge(sem, val)
                    if fn is None:
                        continue
                    ins = fn(hw)
                    if inc is not None:
                        ins.then_inc(inc[0], inc[1])

            @block.tensor
            def _(hw):
                run(self.engs["pe"], hw)

            @block.scalar
            def _(hw):
                run(self.engs["act"], hw)

            @block.vector
            def _(hw):
                run(self.engs["dve"], hw)

            @block.gpsimd
            def _(hw):
                run(self.engs["pool"], hw)

            @block.sync
            def _(hw):
                run(self.engs["sp"], hw)

    def stats(self):
        return {k: len(v.ops) for k, v in self.engs.items()}
from contextlib import ExitStack
import ml_dtypes
from concourse.bass_utils import run_bass_kernel_spmd

D = 1024
L = 4096
CT = 256
T = L + CT
DEPTH = 4
DFF = 2816
NFF = DFF // 128
EPS = 1e-6
GRID = 64
CH = 32
NCH = T // CH


class TL:
    def __init__(self, t, r):
        self.t = t
        self.r = r

    def __getitem__(self, k):
        return self.t[k]


class KB:
    def __init__(self, debug_outs=()):
        self.nc = bass.Bass("TRN2", target_bir_lowering=False)
        self.S = Sched(self.nc)
        self.dram = {}
        self.dres = {}
        self.debug_outs = set(debug_outs)
        self.nt = 0
        self.psum = []
        self.psi = 0
        self.wres = {}

    def din(self, name, shape, dt=F32):
        t = self.nc.dram_tensor(name, list(shape), dt, kind="ExternalInput")
        self.dram[name] = t
        return t

    def dscratch(self, name, shape, dt, out=False):
        kind = "ExternalOutput" if (out or name in self.debug_outs) else "Internal"
        t = self.nc.dram_tensor(name, list(shape), dt, kind=kind)
        self.dram[name] = t
        return t

    def dr(self, name, idx=0):
        k = (name, idx)
        if k not in self.dres:
            self.dres[k] = self.S.res("%s_%s" % (name, idx))
        return self.dres[k]

    def sb(self, name, shape, dt):
        self.nt += 1
        t = self.nc.alloc_sbuf_tensor("%s_%d" % (name, self.nt), list(shape), dt)
        return TL(t, self.S.res(name))

    def ring(self, name, n, shape, dt):
        return [self.sb("%s%d" % (name, i), shape, dt) for i in range(n)]

    def init_psum(self):
        for i in range(8):
            t = self.nc.alloc_psum_tensor("ps%d" % i, [128, 512], F32)
            self.psum.append(TL(t, self.S.res("ps%d" % i)))

    def bank(self):
        b = self.psum[self.psi % 8]
        self.psi += 1
        return b

    def mm(self, out, lhsT, rhs, start, stop, reads, writes, inc=None, **kw):
        if inc is None:
            inc = stop
        self.S.op("pe", lambda hw: hw.matmul(out, lhsT, rhs, start=start, stop=stop, **kw), reads, writes, inc)

    def tr(self, out, in_, ident, reads, writes, inc=True):
        self.S.op("pe", lambda hw: hw.transpose(out, in_, ident), reads, writes, inc)

    def act(self, out, in_, func, reads, writes, bias=None, scale=None, accum_out=None):
        kw = {}
        if bias is not None:
            kw["bias"] = bias
        if scale is not None:
            kw["scale"] = scale
        if accum_out is not None:
            kw["accum_out"] = accum_out
        self.S.op("act", lambda hw: hw.activation(out, in_, func, **kw), reads, writes)

    def tt(self, en, out, in0, in1, op, reads, writes):
        self.S.op(en, lambda hw: hw.tensor_tensor(out, in0, in1, op), reads, writes)

    def ts(self, en, out, in0, s1, op0, reads, writes, s2=None, op1=None):
        if op1 is None:
            self.S.op(en, lambda hw: hw.tensor_scalar(out, in0, s1, None, op0), reads, writes)
        else:
            self.S.op(en, lambda hw: hw.tensor_scalar(out, in0, s1, s2, op0, op1), reads, writes)

    def stt(self, out, in0, scalar, in1, op0, op1, reads, writes):
        self.S.op("dve", lambda hw: hw.scalar_tensor_tensor(out, in0, scalar, in1, op0, op1), reads, writes)

    def copy(self, en, out, in_, reads, writes):
        if en == "act":
            self.S.op(en, lambda hw: hw.copy(out, in_), reads, writes)
        else:
            self.S.op(en, lambda hw: hw.tensor_copy(out, in_), reads, writes)

    def memset(self, en, ap, val, writes):
        self.S.op(en, lambda hw: hw.memset(ap, val), (), writes)

    def load(self, out, in_, reads, writes, q="sp", **kw):
        return self.S.dma(q, out, in_, reads, writes, **kw)

    def setup_consts(self):
        nc = self.nc
        self.init_psum()
        self.c_ident = self.din("c_ident", [128, 128], BF16)
        self.ident = self.sb("ident", [128, 128], BF16)
        self.load(self.ident[:], self.c_ident.ap(), [], [self.ident.r])
        self.ones = self.sb("ones", [128, 128], BF16)
        self.memset("pool", self.ones[:], 1.0, [self.ones.r])

    def rms_rstd(self, xt, n, rstd, sq, width=D):
        nch = width // 128
        self.act(sq[:, 0:nch, 0:n], xt[:, 0:nch, 0:n], AF.Square, [xt.r], [sq.r])
        for n0 in range(0, n, 512):
            n1 = min(n, n0 + 512)
            bk = self.bank()
            for c in range(nch):
                self.mm(bk[:, 0:n1 - n0], self.ones[:], sq[:, c, n0:n1], c == 0, c == nch - 1,
                        [self.ones.r, sq.r], [bk.r])
            self.act(rstd[:, n0:n1], bk[:, 0:n1 - n0], AF.Ln, [bk.r], [rstd.r], bias=self.epsb[:, 0:1], scale=1.0 / width)
            self.act(rstd[:, n0:n1], rstd[:, n0:n1], AF.Exp, [rstd.r], [rstd.r], scale=-0.5)

    def xview(self, name):
        return self.dram[name].ap().rearrange("(c p) n -> p c n", p=128)

    def mod_setup(self):
        cT = self.din("cT", [128, 8, 2])
        bmodT = self.din("bmodT", [128, DEPTH, 48])
        self.wmod = self.din("w_mod", [DEPTH, D, 6 * D])
        self.epsb = self.sb("epsb", [128, 1], F32)
        self.memset("pool", self.epsb[:], EPS, [self.epsb.r])
        self.modsb = self.sb("modsb", [128, DEPTH, 48, 2], F32)
        self.modr = [self.S.res("mod%d" % l) for l in range(DEPTH)]
        csb = self.sb("csb", [128, 8, 2], F32)
        self.ssb = self.sb("ssb", [128, 8, 2], F32)
        self.bsb = self.sb("bsb", [128, DEPTH, 48], F32)
        self.identf = self.sb("identf", [2, 2], F32)
        self.load(csb[:], cT.ap(), [], [csb.r])
        self.load(self.bsb[:], bmodT.ap(), [], [self.bsb.r])
        self.act(self.ssb[:], csb[:], AF.Silu, [csb.r], [self.ssb.r])
        self.copy("dve", self.identf[:], self.ident[0:2, 0:2], [self.ident.r], [self.identf.r])

    def mod_tasks(self, l, bank_fn=None):
        bank_fn = bank_fn or self.bank
        NW = 3
        wring = self.pring("wmod", NW, [128, 8, 512], F32)
        mr = self.psb("mrow", [2, 6 * D], F32)
        wv = self.wmod.ap()[l].rearrange("(k p) n -> p k n", p=128)
        ssb, bsb = self.ssb, self.bsb
        loaded = set()

        def ld(pi):
            if pi < 12 and pi not in loaded:
                loaded.add(pi)
                w = wring[pi % NW]
                self.load(w[:], wv[:, :, pi * 512:(pi + 1) * 512], [], [w.r])

        def piece(pi):
            def f():
                for a_ in range(NW):
                    ld(pi + a_)
                w = wring[pi % NW]
                bk = bank_fn()
                for k in range(8):
                    self.mm(bk[0:2, 0:512], ssb[:, k, :], w[:, k, :], k == 0, k == 7, [w.r, ssb.r], [bk.r])
                self.copy("act", mr[0:2, pi * 512:(pi + 1) * 512], bk[0:2, 0:512], [bk.r], [mr.r])
            return f

        def fin():
            bk = bank_fn()
            for j in range(48):
                self.tr(bk[:, 2 * j:2 * j + 2], mr[0:2, j * 128:(j + 1) * 128], self.identf[0:2, 0:2], [mr.r, self.identf.r], [bk.r],
                        inc=(j == 47))
            self.tt("dve", self.modsb[:, l], bk[:, 0:96].rearrange("p (j t) -> p j t", t=2),
                    bsb[:, l, :].unsqueeze(2).broadcast_to([128, 48, 2]), ALU.add, [bk.r, bsb.r], [self.modr[l]])
            for sp in (1, 4):
                sl = self.modsb[:, l, sp * 8:(sp + 1) * 8, :]
                self.ts("dve", sl, sl, 1.0, ALU.add, [self.modr[l]], [self.modr[l]])
        def pre():
            for a_ in range(NW - 1):
                ld(a_)
        return [pre] + [piece(pi) for pi in range(12)] + [fin]

    def phase_mod0(self):
        self.begin_phase()
        for t in self.mod_tasks(0):
            t()
        self.end_phase()

    def mod(self, l, split, c, col):
        return self.modsb[:, l, split * 8 + c, col:col + 1]

    def phase_final(self, xa):
        gT = self.din("fngT", [128, 8])
        gsb = self.sb("fng", [128, 8], F32)
        self.load(gsb[:], gT.ap(), [], [gsb.r])
        outT = self.dscratch("outT", [D, L], F32, out=True)
        xv = self.xview(xa)
        ov = self.xview("outT")
        xr = self.ring("fx", 2, [128, 8, 512], F32)
        sq = self.ring("fsq", 2, [128, 8, 512], BF16)
        rs = self.ring("frs", 2, [128, 512], F32)
        yr = self.ring("fy", 2, [128, 8, 512], F32)
        for i in range(L // 512):
            x = xr[i % 2]
            self.load(x[:], xv[:, :, i * 512:(i + 1) * 512], [self.dr(xa, i)], [x.r])
            self.rms_rstd(x, 512, rs[i % 2], sq[i % 2])
            r = rs[i % 2]
            y = yr[i % 2]
            for c in range(8):
                if c < 5:
                    self.stt(y[:, c, :], x[:, c, :], gsb[:, c:c + 1], r[:], ALU.mult, ALU.mult, [x.r, r.r, gsb.r], [y.r])
                else:
                    self.tt("pool", y[:, c, :], x[:, c, :], r[:], ALU.mult, [x.r, r.r], [y.r])
                    self.act(y[:, c, :], y[:, c, :], AF.Identity, [y.r, gsb.r], [y.r], scale=gsb[:, c:c + 1])
            self.load(ov[:, :, i * 512:(i + 1) * 512], y[:], [y.r], [self.dr("outT", i)])
        self.out_res = [self.dr("outT", i) for i in range(L // 512)]

    def begin_phase(self):
        self.pstack = ExitStack()

    def psb(self, name, shape, dt):
        self.nt += 1
        t = self.pstack.enter_context(self.nc.sbuf_tensor("%s_%d" % (name, self.nt), list(shape), dt))
        return TL(t, self.S.res(name))

    def pring(self, name, n, shape, dt):
        return [self.psb("%s%d" % (name, i), shape, dt) for i in range(n)]

    def end_phase(self):
        self.S.barrier()
        self.pstack.close()

    def phase_wcast(self, specs, order):
        self.wcast_setup(specs)
        self.wcast_issue(order)

    def wcast_setup(self, specs):
        self.S.barrier_exempt.add("pool")
        self.wc = {}
        for name, shape in specs:
            src = self.din(name, shape)
            dst = self.dscratch("wb_" + name, shape, BF16)
            per = int(np.prod(shape[1:]))
            assert per % 1024 == 0
            self.wc[name] = (src, dst, per // 1024)

    def wcast_tasks(self, order):
        tasks = []
        for name, l in order:
            src, dst, rows = self.wc[name]
            sv = src.ap()[l].rearrange("a b -> (a b)").rearrange("(r n) -> r n", n=1024)
            dv = dst.ap()[l].rearrange("a b -> (a b)").rearrange("(r n) -> r n", n=1024)
            rl = []
            self.wres[(name, l)] = rl
            for r0 in range(0, rows, 1024):
                r1 = min(rows, r0 + 1024)
                rr = self.S.res("wb")
                rl.append(rr)
                tasks.append(lambda dv=dv, sv=sv, r0=r0, r1=r1, rr=rr: self.load(dv[r0:r1, :], sv[r0:r1, :], [], [rr], q="pool"))
        return tasks

    def wcast_issue(self, order):
        for t in self.wcast_tasks(order):
            t()

    def wb(self, name, l):
        return self.dram["wb_" + name].ap()[l], self.wres[(name, l)]

    def norm_mod(self, xt, n0, n1, rstd, h, l, sh_split, sc_split, col):
        n = n1 - n0
        self.act(h[:, :, n0:n1], xt[:, :, n0:n1], AF.Square, [xt.r], [h.r])
        for a in range(n0, n1, 512):
            b_ = min(n1, a + 512)
            bk = self.bank()
            for c in range(8):
                self.mm(bk[:, 0:b_ - a], self.ones[:], h[:, c, a:b_], c == 0, c == 7, [self.ones.r, h.r], [bk.r])
            self.act(rstd[:, a:b_], bk[:, 0:b_ - a], AF.Ln, [bk.r], [rstd.r], bias=self.epsb[:, 0:1], scale=1.0 / D)
            self.act(rstd[:, a:b_], rstd[:, a:b_], AF.Exp, [rstd.r], [rstd.r], scale=-0.5)
        for c in range(8):
            self.stt(h[:, c, n0:n1], xt[:, c, n0:n1], self.mod(l, sc_split, c, col), rstd[:, n0:n1], ALU.mult, ALU.mult,
                     [xt.r, rstd.r, self.modr[l]], [h.r])
            self.ts("pool", h[:, c, n0:n1], h[:, c, n0:n1], self.mod(l, sh_split, c, col), ALU.add, [h.r, self.modr[l]], [h.r],
                    s2=1.0, op1=ALU.mult)

    def phase_out(self, l, wname, widx, xin, xout):
        self.begin_phase()
        tasks = []
        wsrc, wres = self.wb(wname, widx)
        W = self.psb("wout", [128, 8, D], BF16)
        self.load(W[:], wsrc.rearrange("(k p) n -> p k n", p=128), wres, [W.r])
        mv = self.xview("mixT")
        xv = self.xview(xin)
        ov = self.xview(xout)
        mr = self.pring("omix", 2, [128, 8, 512], BF16)
        xr = self.pring("ox", 2, [128, 8, 512], F32)
        for i in range(9):
            n = 512 if i < 8 else CT
            col = 0 if i < 8 else 1
            t0 = i * 512
            m = mr[i % 2]
            x = xr[i % 2]
            self.load(m[:, :, 0:n], mv[:, :, t0:t0 + n], [self.dr("mixT", i)], [m.r])
            self.load(x[:, :, 0:n], xv[:, :, t0:t0 + n], [self.dr(xin, i)], [x.r])
            for c in range(8):
                bk = self.bank()
                for k in range(8):
                    self.mm(bk[:, 0:n], W[:, k, c * 128:(c + 1) * 128], m[:, k, 0:n], k == 0, k == 7, [W.r, m.r], [bk.r])
                self.stt(x[:, c, 0:n], bk[:, 0:n], self.mod(l, 2, c, col), x[:, c, 0:n], ALU.mult, ALU.add,
                         [bk.r, x.r, self.modr[l]], [x.r])
            self.load(ov[:, :, t0:t0 + n], x[:, :, 0:n], [x.r], [self.dr(xout, i)])
            for _ in range(2):
                if tasks:
                    tasks.pop(0)()
        while tasks:
            tasks.pop(0)()
        self.end_phase()

    def phase_ffn(self, l, xin, xout, bg=None):
        self.begin_phase()
        bg = list(bg or [])
        wd_src, wd_res = self.wb("w_down", l)
        wu_src, wu_res = self.wb("w_up", l)
        wuv = wu_src.rearrange("(k p) (g n) -> p k g n", p=128, g=2)
        wdv = wd_src.rearrange("(c p) n -> p c n", p=128)
        cw = self.psb("convw", [128, NFF, 9], F32)
        cb = self.psb("convb", [128, NFF], F32)
        self.load(cw[:], self.dram["convwT"].ap()[l], [], [cw.r])
        self.load(cb[:], self.dram["convbT"].ap()[l], [], [cb.r])
        xt = self.psb("fx", [128, 8, 1152], F32)
        h = self.psb("fh", [128, 8, 1152], BF16)
        rstd = self.psb("frstd", [128, 1152], F32)
        actT = self.psb("factT", [128, NFF, 1024], BF16)
        wur = self.pring("fwu", 3, [128, 8, 2, 256], BF16)
        Wd = self.psb("wdown", [128, NFF, D], BF16)
        self.load(Wd[:], wdv, wd_res, [Wd.r])
        Gr = self.pring("fG", 2, [128, 18 * 66], BF16)
        sr = self.pring("fsilu", 2, [128, 512], F32)
        vr = self.pring("fval", 2, [128, 1024], BF16)
        dgr = self.pring("fdiag", 2, [128, 9, 128], BF16)
        xrr = self.pring("fxr", 4, [128, 512], F32)
        for G in Gr:
            self.memset("pool", G[:], 0.0, [G.r])
        xv = self.xview(xin)
        ov = self.xview(xout)

        def geom(band):
            if band < 4:
                lo = 64 if band > 0 else 0
                hi = 64 if band < 3 else 0
                return dict(ctx=False, t0=band * 1024, lo=lo, hi=hi, ncol=1152, cen0=64, ncen=1024, col=0)
            return dict(ctx=True, t0=L, lo=0, hi=0, ncol=CT, cen0=0, ncen=CT, col=1)

        def prologue(band):
            g = geom(band)
            if not g["ctx"]:
                t0, lo, hi = g["t0"], g["lo"], g["hi"]
                rd = [self.dr(xin, i) for i in range(max(0, 2 * band - 1), min(8, 2 * band + 3))]
                if lo == 0:
                    self.memset("pool", xt[:, :, 0:64], 0.0, [xt.r])
                if hi == 0:
                    self.memset("pool", xt[:, :, 1088:1152], 0.0, [xt.r])
                self.load(xt[:, :, 64 - lo:64 + 1024 + hi], xv[:, :, t0 - lo:t0 + 1024 + hi], rd, [xt.r])
            else:
                self.load(xt[:, :, 0:CT], xv[:, :, L:L + CT], [self.dr(xin, 8)], [xt.r])
            self.norm_mod(xt, 0, g["ncol"], rstd, h, l, 3, 4, g["col"])

        wu_it = [0]
        wu_of = {}

        def load_wu(band, c):
            if c < NFF and (band, c) not in wu_of:
                wu = wur[wu_it[0] % 3]
                wu_it[0] += 1
                for g_ in range(2):
                    self.load(wu[:, :, g_, :], wuv[:, :, g_, c * 128:(c + 2) * 128], wu_res, [wu.r])
                wu_of[(band, c)] = wu
                wu_of[(band, c + 1)] = wu

        prologue(0)
        load_wu(0, 0)
        wd_it = 0
        xr_it = 0
        for band in range(5):
            g = geom(band)
            ctxb, lo, hi, t0 = g["ctx"], g["lo"], g["hi"], g["t0"]
            cen0, ncen, col = g["cen0"], g["ncen"], g["col"]
            nob = 2 if not ctxb else 1
            held = {}

            def st0(c, k):
                if bg:
                    bg.pop(0)()
                if c % 2 == 0:
                    load_wu(band, c + 2)
                wu = wu_of[(band, c)]
                wo = (c % 2) * 128
                G = Gr[c % 2]
                dg = dgr[c % 2]
                val = vr[c % 2]
                self.tt("pool", dg[:], self.ident[:].unsqueeze(1).broadcast_to([128, 9, 128]),
                        cw[:, c, :].unsqueeze(2).broadcast_to([128, 9, 128]), ALU.mult, [self.ident.r, cw.r], [dg.r])
                if not ctxb:
                    G3 = G[:].rearrange("p (r w) -> p r w", w=66)
                    for j in range(3):
                        bk = self.bank()
                        for kx in range(8):
                            self.mm(bk[:, 0:384], wu[:, kx, 0, wo:wo + 128], h[:, kx, j * 384:(j + 1) * 384], kx == 0, kx == 7,
                                    [wu.r, h.r], [bk.r])
                        ra, rb = 6 * j, 6 * j + 6
                        pa = 0
                        if j == 0 and lo == 0:
                            ra, pa = 1, 64
                        if j == 2 and hi == 0:
                            rb = 17
                        self.copy("act", G3[:, ra:rb, 1:65], bk[:, pa:pa + (rb - ra) * 64].rearrange("p (r w) -> p r w", w=64),
                                  [bk.r], [G.r])
                    if lo == 0:
                        self.memset("pool", G3[:, 0:1, :], 0.0, [G.r])
                    if hi == 0:
                        self.memset("pool", G3[:, 17:18, :], 0.0, [G.r])
                    for ob in range(2):
                        vb = self.bank()
                        for kx in range(8):
                            self.mm(vb[:, 0:512], wu[:, kx, 1, wo:wo + 128], h[:, kx, 64 + ob * 512:64 + (ob + 1) * 512], kx == 0, kx == 7,
                                    [wu.r, h.r], [vb.r])
                        self.copy("act" if ob == 0 else "dve", val[:, ob * 512:(ob + 1) * 512], vb[:, 0:512], [vb.r], [val.r])
                else:
                    bk = self.bank()
                    for kx in range(8):
                        self.mm(bk[:, 0:CT], wu[:, kx, 0, wo:wo + 128], h[:, kx, 0:CT], kx == 0, kx == 7, [wu.r, h.r], [bk.r])
                    self.memset("pool", G[:, 0:CT + 2], 0.0, [G.r])
                    self.copy("act", G[:, 1:CT + 1], bk[:, 0:CT], [bk.r], [G.r])
                    vb = self.bank()
                    for kx in range(8):
                        self.mm(vb[:, 0:CT], wu[:, kx, 1, wo:wo + 128], h[:, kx, 0:CT], kx == 0, kx == 7, [wu.r, h.r], [vb.r])
                    self.copy("act", val[:, 0:CT], vb[:, 0:CT], [vb.r], [val.r])

            def st1(c, k):
                G = Gr[c % 2]
                dg = dgr[c % 2]
                val = vr[c % 2]
                if not ctxb:
                    G3 = G[:].rearrange("p (r w) -> p r w", w=66)
                    for ob in range(2):
                        cbk = self.bank()
                        for tap in range(9):
                            dr_, dc_ = tap // 3 - 1, tap % 3 - 1
                            rhs = G3[:, 8 * ob + 1 + dr_:8 * ob + 9 + dr_, 1 + dc_:65 + dc_]
                            self.mm(cbk[:, 0:512].rearrange("p (r w) -> p r w", w=64), dg[:, tap, :], rhs, tap == 0, tap == 8,
                                    [dg.r, G.r], [cbk.r])
                        s_ = sr[ob]
                        self.act(s_[:], cbk[:, 0:512], AF.Silu, [cbk.r, cb.r], [s_.r], bias=cb[:, c:c + 1])
                        self.tt("pool" if ob == 0 else "dve", actT[:, c, ob * 512:(ob + 1) * 512], s_[:], val[:, ob * 512:(ob + 1) * 512],
                                ALU.mult, [s_.r, val.r], [actT.r])
                else:
                    cbk = self.bank()
                    for kx in range(3):
                        self.mm(cbk[:, 0:CT], dg[:, 3 + kx, :], G[:, kx:kx + CT], kx == 0, kx == 2, [dg.r, G.r], [cbk.r])
                    s_ = sr[0]
                    self.act(s_[:, 0:CT], cbk[:, 0:CT], AF.Silu, [cbk.r, cb.r], [s_.r], bias=cb[:, c:c + 1])
                    self.tt("dve", actT[:, c, 0:CT], s_[:, 0:CT], val[:, 0:CT], ALU.mult, [s_.r, val.r], [actT.r])
                    self.memset("pool", G[:, 0:CT + 2], 0.0, [G.r])

            self.pipe(list(range(NFF)), [st0, st1])
            if band + 1 < 5:
                load_wu(band + 1, 0)

            blocks = list(range(0, ncen, 512))
            for bi, a_ in enumerate(blocks):
                nb = min(512, ncen - a_)
                for dc in range(8):
                    xr = xrr[xr_it % 4]
                    xr_it += 1
                    self.load(xr[:, 0:nb], xv[:, dc, t0 + a_:t0 + a_ + nb], [self.dr(xin, (t0 + a_) // 512)], [xr.r])
                    bk = self.bank()
                    for c in range(NFF):
                        self.mm(bk[:, 0:nb], Wd[:, c, dc * 128:(dc + 1) * 128], actT[:, c, a_:a_ + nb], c == 0, c == NFF - 1, [Wd.r, actT.r], [bk.r])
                    self.stt(xr[:, 0:nb], bk[:, 0:nb], self.mod(l, 5, dc, col), xr[:, 0:nb], ALU.mult, ALU.add,
                             [bk.r, xr.r, self.modr[l]], [xr.r])
                    self.load(ov[:, dc, t0 + a_:t0 + a_ + nb], xr[:, 0:nb], [xr.r], [self.dr(xout, (t0 + a_) // 512)])
                if bi == 0 and band + 1 < 5:
                    prologue(band + 1)
        while bg:
            bg.pop(0)()
        self.end_phase()

    def head_rstd(self, src_ap, src_res, n, sq, rstd, width=128):
        self.act(sq[:, 0:n], src_ap, AF.Square, src_res, [sq.r])
        bk = self.bank()
        self.mm(bk[:, 0:n], self.ones[:], sq[:, 0:n], True, True, [self.ones.r, sq.r], [bk.r])
        self.act(rstd[:, 0:n], bk[:, 0:n], AF.Ln, [bk.r], [rstd.r], bias=self.epsb[:, 0:1], scale=1.0 / width)
        self.act(rstd[:, 0:n], rstd[:, 0:n], AF.Exp, [rstd.r], [rstd.r], scale=-0.5)

    def phase_qkv(self, l, o, xin):
        self.begin_phase()
        wsrc, wres = self.wb("w_qkv", o)
        W = self.psb("wqkv", [128, 8, 1536], BF16)
        self.load(W[:], wsrc.rearrange("(k p) n -> p k n", p=128), wres, [W.r])
        gq = self.psb("gq", [128, 2], F32)
        self.load(gq[:], self.dram["qkgT"].ap()[o], [], [gq.r])
        perm = self.psb("perm", [128, 128], BF16)
        self.load(perm[:], self.dram["c_perm"].ap(), [], [perm.r])
        xv = self.xview(xin)
        qv = self.dram["qT"].ap().rearrange("(c p) n -> p c n", p=128)
        vv = self.dram["vtok"].ap()
        xts = self.pring("qx", 2, [128, 8, 512], F32)
        hs_ = self.pring("qh", 2, [128, 8, 512], BF16)
        rstds = self.pring("qrstd", 2, [128, 512], F32)
        css = self.pring("qcos", 2, [128, 512], F32)
        sns = self.pring("qsin", 2, [128, 512], F32)
        NR = 4
        raw = self.pring("qraw", NR, [128, 512], F32)
        sq = self.pring("qsq", NR, [128, 512], BF16)
        hr = self.pring("qhr", NR, [128, 512], F32)
        qn = self.pring("qn", NR, [128, 512], BF16)
        t1 = self.pring("qt1", NR, [128, 512], F32)
        qo = self.pring("qo", NR, [128, 512], BF16)
        vs = self.pring("qvs", 2, [128, 256], BF16)

        def prologue(i):
            n = 512 if i < 8 else CT
            col = 0 if i < 8 else 1
            t0 = i * 512
            xt, h, rstd = xts[i % 2], hs_[i % 2], rstds[i % 2]
            self.load(xt[:, :, 0:n], xv[:, :, t0:t0 + n], [self.dr(xin, i)], [xt.r])
            if i < 8:
                self.load(css[i % 2][:], self.dram["cosT"].ap()[:, t0:t0 + 512], [], [css[i % 2].r])
                self.load(sns[i % 2][:], self.dram["sinT"].ap()[:, t0:t0 + 512], [], [sns[i % 2].r])
            self.norm_mod(xt, 0, n, rstd, h, l, 0, 1, col)
        prologue(0)
        NR = 4
        bkq = {}
        for i in range(9):
            n = 512 if i < 8 else CT
            t0 = i * 512
            h, cs, sn = hs_[i % 2], css[i % 2], sns[i % 2]
            items = list(range(10)) + ["v%d" % s_ for s_ in range(n // 128)]

            def q0(hd, k):
                j = k % NR
                if isinstance(hd, str):
                    s_ = int(hd[1:])
                    bk = self.bank()
                    for kx in range(8):
                        self.mm(bk[:, 0:256], h[:, kx, s_ * 128:(s_ + 1) * 128], W[:, kx, 1280:1536], kx == 0, kx == 7, [h.r, W.r], [bk.r])
                    bkq[(i, hd)] = bk
                    return
                bk = self.bank()
                for kx in range(8):
                    self.mm(bk[:, 0:n], W[:, kx, hd * 128:(hd + 1) * 128], h[:, kx, 0:n], kx == 0, kx == 7, [W.r, h.r], [bk.r])
                self.copy("act", raw[j][:, 0:n], bk[:, 0:n], [bk.r], [raw[j].r])
                self.act(sq[j][:, 0:n], bk[:, 0:n], AF.Square, [bk.r], [sq[j].r])

            def q1(hd, k):
                j = k % NR
                if isinstance(hd, str):
                    s_ = int(hd[1:])
                    bk = bkq.pop((i, hd))
                    v_ = vs[s_ % 2]
                    self.copy("act", v_[:], bk[:, 0:256], [bk.r], [v_.r])
                    self.load(vv[t0 + s_ * 128:t0 + (s_ + 1) * 128, :], v_[:], [v_.r], [self.dr("vtok", i)])
                    return
                bk = self.bank()
                self.mm(bk[:, 0:n], self.ones[:], sq[j][:, 0:n], True, True, [self.ones.r, sq[j].r], [bk.r])
                self.act(hr[j][:, 0:n], bk[:, 0:n], AF.Ln, [bk.r], [hr[j].r], bias=self.epsb[:, 0:1], scale=1.0 / 128)
                self.act(hr[j][:, 0:n], hr[j][:, 0:n], AF.Exp, [hr[j].r], [hr[j].r], scale=-0.5)
                gcol = gq[:, 0:1] if hd < 8 else gq[:, 1:2]
                dst = qn[j] if i < 8 else qo[j]
                self.stt(dst[:, 0:n], raw[j][:, 0:n], gcol, hr[j][:, 0:n], ALU.mult, ALU.mult, [raw[j].r, hr[j].r, gq.r], [dst.r])
                if i == 8:
                    self.load(qv[:, hd, t0:t0 + n], qo[j][:, 0:n], [qo[j].r], [self.dr("qT", i)])

            def q2(hd, k):
                j = k % NR
                if isinstance(hd, str) or i == 8:
                    return
                pb = self.bank()
                self.mm(pb[:, 0:n], perm[:], qn[j][:, 0:n], True, True, [perm.r, qn[j].r], [pb.r])
                self.tt("pool", t1[j][:, 0:n], qn[j][:, 0:n], cs[:, 0:n], ALU.mult, [qn[j].r, cs.r], [t1[j].r])
                self.tt("dve", raw[j][:, 0:n], pb[:, 0:n], sn[:, 0:n], ALU.mult, [pb.r, sn.r], [raw[j].r])
                self.tt("pool", qo[j][:, 0:n], t1[j][:, 0:n], raw[j][:, 0:n], ALU.add, [t1[j].r, raw[j].r], [qo[j].r])
                self.load(qv[:, hd, t0:t0 + n], qo[j][:, 0:n], [qo[j].r], [self.dr("qT", i)])
            self.pipe(items, [q0, q1, q2])
            if i + 1 < 9:
                prologue(i + 1)
        self.end_phase()

    def phase_attn(self, mod_next=None):
        self.begin_phase()
        tasks = self.mod_tasks(mod_next, lambda: self.psum[7]) if mod_next is not None and mod_next < DEPTH else []
        qv = self.dram["qT"].ap().rearrange("(c p) n -> p c n", p=128)
        mv = self.xview("mixT")
        KT = self.psb("aKT", [128, T], BF16)
        V = self.psb("aV", [128, 34, 132], BF16)
        self.memset("pool", V[:], 0.0, [V.r])
        self.memset("pool", V[:, :, 128:129], 1.0, [V.r])
        qr = self.pring("aQ", 2, [128, 512], BF16)
        LA = 2
        pr = self.pring("aP", LA + 1, [128, 512], BF16)
        rinv = self.pring("arinv", 2, [128, 4], F32)
        otok = self.pring("aOt", 2, [128, 4, 128], BF16)
        ob = self.pring("aO", 2, [128, 512], BF16)
        scale = 128 ** -0.5
        allq = [self.dr("qT", i) for i in range(9)]
        allv = [self.dr("vtok", i) for i in range(9)]
        it = 0
        pi = 0
        tbk = self.psum[7]
        tbv = tbk[:].bitcast(BF16)
        for g in range(2):
            self.load(KT[:], qv[:, 8 + g, :], allq, [KT.r])
            self.load(V[:, :, 0:128], self.dram["vtok"].ap().rearrange("(c p) d -> p c d", p=128)[:, :, g * 128:(g + 1) * 128], allv, [V.r])
            for hq in range(4):
                hd = 4 * g + hq
                for i in range(9):
                    n = 512 if i < 8 else CT
                    nsub = n // 128
                    t0 = i * 512
                    kcs = list(range(34)) if i < 8 else [32, 33]
                    q = qr[it % 2]
                    obk = [self.psum[2 * (it % 2)], self.psum[2 * (it % 2) + 1]]
                    ri, ot, o_ = rinv[it % 2], otok[it % 2], ob[it % 2]
                    it += 1
                    self.load(q[:, 0:n], qv[:, hd, t0:t0 + n], [self.dr("qT", i)], [q.r])
                    sbanks = {}
                    if tasks and it % 4 == 2:
                        tasks.pop(0)()

                    def issue_s(kc):
                        nonlocal pi
                        b_ = self.psum[4 + pi % (LA + 1)]
                        p_ = pr[pi % (LA + 1)]
                        pi += 1
                        self.mm(b_[:, 0:n], KT[:, kc * 128:(kc + 1) * 128], q[:, 0:n], True, True, [KT.r, q.r], [b_.r])
                        self.act(p_[:, 0:n], b_[:, 0:n], AF.Exp, [b_.r], [p_.r], scale=scale)
                        sbanks[kc] = p_
                    for kc in kcs[0:LA]:
                        issue_s(kc)
                    for idx, kc in enumerate(kcs):
                        if idx + LA < len(kcs):
                            issue_s(kcs[idx + LA])
                        p_ = sbanks.pop(kc)
                        first, last = idx == 0, idx == len(kcs) - 1
                        for s_ in range(nsub):
                            bk = obk[s_ // 2]
                            off = (s_ % 2) * 132
                            self.mm(bk[:, off:off + 129], p_[:, s_ * 128:(s_ + 1) * 128], V[:, kc, 0:129], first and s_ % 2 == 0, last,
                                    [p_.r, V.r], [bk.r], inc=(s_ == nsub - 1), skip_group_check=True)
                    for s_ in range(nsub):
                        bk = obk[s_ // 2]
                        off = (s_ % 2) * 132
                        self.S.op("dve", lambda hw, ri=ri, bk=bk, off=off, s_=s_: hw.reciprocal(ri[:, s_:s_ + 1], bk[:, off + 128:off + 129]),
                                  [bk.r], [ri.r])
                        self.ts("dve", ot[:, s_, :], bk[:, off:off + 128], ri[:, s_:s_ + 1], ALU.mult, [bk.r, ri.r], [ot.r])
                    for s_ in range(nsub):
                        self.tr(tbv[:, s_ * 128:(s_ + 1) * 128], ot[:, s_, :], self.ident[:], [ot.r, self.ident.r], [tbk.r], inc=(s_ == nsub - 1))
                    self.copy("act", o_[:, 0:n], tbv[:, 0:n], [tbk.r], [o_.r])
                    self.load(mv[:, hd, t0:t0 + n], o_[:, 0:n], [o_.r], [self.dr("mixT", i)])
        while tasks:
            tasks.pop(0)()
        self.end_phase()

    def pipe(self, items, stages):
        n, ns = len(items), len(stages)
        for step in range(n + ns - 1):
            for j in range(ns - 1, -1, -1):
                k = step - j
                if 0 <= k < n:
                    stages[j](items[k], k)

    def phase_ab_in(self, l, e, xin):
        self.begin_phase()
        wsrc, wres = self.wb("w_in_ab", e)
        W = self.psb("win", [128, 8, 3072], BF16)
        self.load(W[:], wsrc.rearrange("(k p) n -> p k n", p=128), wres, [W.r])
        CS = self.psb("ccs", [128, 256], BF16)
        self.load(CS[:], self.dram["c_cs"].ap(), [], [CS.r])
        cmask = self.psb("cmask", [128, 2, 128], I32)
        self.load(cmask[:], self.dram["c_mask"].ap(), [], [cmask.r])
        smask = self.psb("smask", [128, 512], F32)
        self.load(smask[:], self.dram["c_smask"].ap(), [], [smask.r])
        lg = self.psb("lg", [128, 2, 2, 4], F32)
        self.load(lg[:], self.dram["hglT"].ap(), [], [lg.r])
        lb = self.psb("lb", [128, 2, 4], F32)
        oml = self.psb("oml", [128, 2, 4], F32)
        lbm1 = self.psb("lbm1", [128, 2, 4], F32)
        if e == 0:
            self.memset("pool", lb[:], 0.0, [lb.r])
        else:
            self.tt("dve", lb[:], lg[:, 1], lg[:, 0], ALU.subtract, [lg.r], [lb.r])
            self.act(lb[:], lb[:], AF.Sigmoid, [lb.r], [lb.r])
        self.ts("dve", oml[:], lb[:], -1.0, ALU.mult, [lb.r], [oml.r], s2=1.0, op1=ALU.add)
        self.ts("dve", lbm1[:], lb[:], -1.0, ALU.add, [lb.r], [lbm1.r])
        xv = self.xview(xin)
        xt = self.psb("ax", [128, 8, 512], F32)
        hs_ = self.pring("ah", 2, [128, 8, 512], BF16)
        rstd = self.psb("arstd", [128, 512], F32)
        aT = self.pring("aT", 2, [128, 512], BF16)
        pq = self.pring("apq", 2, [128, 2, 256], BF16)
        gs = self.pring("ags", 2, [128, 512], BF16)
        vsb = self.psb("avs", [128, 4, 512], BF16)
        qs = self.psb("aqs", [128, 4, 512], F32)
        sig8 = self.psb("asig", [128, 8, 512], F32)
        R = 2
        lf = self.pring("alf", R, [128, 512], F32)
        kk = self.pring("akk", R + 1, [128, 512], F32)
        bb = self.pring("ab", R, [128, 512], F32)
        bm = self.pring("abm", R, [128, 512], F32)
        E1 = self.pring("aE1", R, [128, 512], F32)
        E2 = self.pring("aE2", R, [128, 512], F32)
        qt = [self.psb("aqt%d" % d, [128, 4, 512], BF16) for d in range(2)]
        kt = [self.psb("akt%d" % d, [128, 4, 512], BF16) for d in range(2)]
        kh = self.pring("akh", 2, [128, 512], BF16)
        khs = self.pring("akhs", 2, [128, 4, 128], BF16)
        dsb = self.pring("adsb", 2, [128, 512 // CH], F32)
        qib = self.pring("aqib", 2, [128, 512], BF16)
        attT = [self.pring("aatt%d" % d, 2, [128, 128], BF16) for d in range(2)]
        oi = self.psb("aoi", [128, 512], F32)
        for d in range(2):
            for a_ in attT[d]:
                self.memset("pool", a_[:], 0.0, [a_.r])
        PQv = self.dram["PQ"].ap()
        nl_, ncx_ = L // CH, CT // CH

        def prologue(i):
            n = 512 if i < 8 else CT
            self.load(xt[:, :, 0:n], xv[:, :, i * 512:i * 512 + n], [self.dr(xin, i)], [xt.r])
            self.norm_mod(xt, 0, n, rstd, hs_[i % 2], l, 0, 1, 0 if i < 8 else 1)
        prologue(0)
        ai = [0]
        for i in range(9):
            n = 512 if i < 8 else CT
            t0 = i * 512
            nsub = n // 128
            nch = n // CH
            h = hs_[i % 2]
            banks = {}

            def proj_to(key, c0, M=None):
                bk = self.bank()
                for k in range(8):
                    self.mm(bk[:, 0:n], W[:, k, c0:c0 + 128], h[:, k, 0:n], k == 0, k == 7, [W.r, h.r], [bk.r])
                banks[key] = bk

            items = [("a", g) for g in range(4)] + [("g", hd) for hd in range(4)] + [("v", s_) for s_ in range(nsub)] + \
                    [("q", hd) for hd in range(4)] + [("z", j) for j in range(8)]

            def p1s0(it, k):
                kind, j = it
                if kind == "a":
                    proj_to(it, j * 128)
                elif kind == "g":
                    proj_to(it, 2560 + j * 128)
                elif kind == "q":
                    proj_to(it, 512 + j * 128)
                elif kind == "z":
                    proj_to(it, 1024 + j * 128)
                else:
                    bk = self.bank()
                    for kx in range(8):
                        self.mm(bk[:, 0:512], h[:, kx, j * 128:(j + 1) * 128], W[:, kx, 2048:2560], kx == 0, kx == 7, [h.r, W.r], [bk.r])
                    banks[it] = bk

            def p1s1(it, k):
                kind, j = it
                bk = banks.pop(it)
                if kind == "a":
                    self.copy("dve", aT[j % 2][:, 0:n], bk[:, 0:n], [bk.r], [aT[j % 2].r])
                elif kind == "g":
                    self.act(sig8[:, j, 0:n], bk[:, 0:n], AF.Sigmoid, [bk.r], [sig8.r])
                    g_ = gs[j % 2]
                    self.tt("dve", g_[:, 0:n], bk[:, 0:n], sig8[:, j, 0:n], ALU.mult, [bk.r, sig8.r], [g_.r])
                    self.load(self.dram["gT"].ap()[j * 128:(j + 1) * 128, t0:t0 + n], g_[:, 0:n], [g_.r], [self.dr("gT", i)])
                elif kind == "q":
                    self.act(sig8[:, 4 + j, 0:n], bk[:, 0:n], AF.Sigmoid, [bk.r], [sig8.r])
                    self.tt("dve", qs[:, j, 0:n], bk[:, 0:n], sig8[:, 4 + j, 0:n], ALU.mult, [bk.r, sig8.r], [qs.r])
                elif kind == "z":
                    self.act(sig8[:, j, 0:n], bk[:, 0:n], AF.Sigmoid, [bk.r], [sig8.r])
                else:
                    self.copy("act", vsb[:, j, :], bk[:, 0:512], [bk.r], [vsb.r])
                    if j == nsub - 1:
                        self.load(self.dram["vtok2"].ap()[t0:t0 + n, :].rearrange("(s p) d -> p s d", p=128), vsb[:, 0:nsub, :], [vsb.r],
                                  [self.dr("vtok2", i)])

            def p1s2(it, k):
                kind, g = it
                if kind != "a":
                    return
                for s2 in range(0, nsub, 2):
                    b2 = self.bank()
                    ns2 = min(2, nsub - s2)
                    for u in range(ns2):
                        self.mm(b2[:, u * 256:(u + 1) * 256], aT[g % 2][:, (s2 + u) * 128:(s2 + u + 1) * 128], CS[:], True, True,
                                [aT[g % 2].r, CS.r], [b2.r], inc=(u == ns2 - 1))
                    p_ = pq[(s2 // 2) % 2]
                    self.copy("act", p_[:, 0:ns2, :], b2[:, 0:ns2 * 256].rearrange("p (u n) -> p u n", n=256), [b2.r], [p_.r])
                    self.load(PQv[t0 + s2 * 128:t0 + (s2 + ns2) * 128, g, :].rearrange("(u p) n -> p u n", p=128), p_[:, 0:ns2, :],
                              [p_.r], [self.dr("PQ", i)])
            self.pipe(items, [p1s0, p1s1, p1s2])

            if i + 1 < 9:
                prologue(i + 1)

            items2 = [(d, hd) for d in range(2) for hd in range(4)]

            def b3of(t):
                return t[:, 0:n].rearrange("p (c t) -> p c t", t=CH)

            def s1(it, k):
                d, hd = it
                j = d * 4 + hd
                r = k % R
                self.ts("dve", lf[r][:, 0:n], sig8[:, j, 0:n], oml[:, d, hd:hd + 1], ALU.mult, [sig8.r, oml.r, lb.r], [lf[r].r],
                        s2=lb[:, d, hd:hd + 1], op1=ALU.add)
                self.act(lf[r][:, 0:n], lf[r][:, 0:n], AF.Ln, [lf[r].r], [lf[r].r])
                kr = kk[k % (R + 1)]
                self.act(kr[:, 0:n], sig8[:, j, 0:n], AF.Identity, [sig8.r, lbm1.r, oml.r], [kr.r],
                         bias=oml[:, d, hd:hd + 1], scale=lbm1[:, d, hd:hd + 1])

            def s2(it, k):
                d, hd = it
                r = k % R
                b = bb[r]
                if d == 0:
                    bo_, lo_ = b[:, 0:n], lf[r][:, 0:n]
                else:
                    bo_, lo_ = b[:, 0:n][:, ::-1], lf[r][:, 0:n][:, ::-1]
                self.S.op("dve", lambda hw, bo_=bo_, lo_=lo_, sm_=smask[:, 0:n]: hw.tensor_tensor_scan(bo_, sm_, lo_, 0.0, ALU.mult, ALU.add),
                          [smask.r, lf[r].r], [b.r])
                b3 = b3of(b)
                self.tt("dve", b3of(bm[r]), b3, b3[:, :, CH // 2:CH // 2 + 1].broadcast_to([128, nch, CH]), ALU.subtract, [b.r], [bm[r].r])
                self.act(E1[r][:, 0:n], bm[r][:, 0:n], AF.Exp, [bm[r].r], [E1[r].r])
                self.act(E2[r][:, 0:n], bm[r][:, 0:n], AF.Exp, [bm[r].r], [E2[r].r], scale=-1.0)

            def s3(it, k):
                d, hd = it
                r = k % R
                b = bb[r]
                kr = kk[k % (R + 1)]
                li = CH - 1 if d == 0 else 0
                b3 = b3of(b)
                self.tt("dve", qt[d][:, hd, 0:n], qs[:, hd, 0:n], E1[r][:, 0:n], ALU.mult, [qs.r, E1[r].r], [qt[d].r])
                self.tt("pool", kt[d][:, hd, 0:n], kr[:, 0:n], E2[r][:, 0:n], ALU.mult, [kr.r, E2[r].r], [kt[d].r])
                self.tt("dve", b3of(bm[r]), b3[:, :, li:li + 1].broadcast_to([128, nch, CH]), b3, ALU.subtract, [b.r], [bm[r].r])
                self.act(E1[r][:, 0:n], bm[r][:, 0:n], AF.Exp, [bm[r].r], [E1[r].r])
                self.act(E2[r][:, 0:n], b[:, 0:n], AF.Exp, [b.r], [E2[r].r])
                ds_ = dsb[k % 2]
                c0_ = t0 // CH
                if d == 0:
                    p0_ = (ncx_ + c0_) if i < 8 else 0
                    self.act(ds_[:, 0:nch], b3[:, :, li], AF.Exp, [b.r], [ds_.r])
                else:
                    p0_ = (ncx_ + nl_ - c0_ - nch) if i < 8 else 0
                    self.act(ds_[:, 0:nch][:, ::-1], b3[:, :, li], AF.Exp, [b.r], [ds_.r])
                self.load(self.dram["dT"].ap()[d, hd * 128:(hd + 1) * 128, p0_:p0_ + nch], ds_[:, 0:nch], [ds_.r], [self.dr("dT", i)])
                self.load(self.dram["qtT"].ap()[d, hd * 128:(hd + 1) * 128, t0:t0 + n], qt[d][:, hd, 0:n], [qt[d].r], [self.dr("qtT", i)])

            def s4(it, k):
                d, hd = it
                r = k % R
                kr = kk[k % (R + 1)]
                kh_ = kh[k % 2]
                qi_ = qib[k % 2]
                self.tt("pool", kh_[:, 0:n], kr[:, 0:n], E1[r][:, 0:n], ALU.mult, [kr.r, E1[r].r], [kh_.r])
                self.tt("pool", qi_[:, 0:n], qs[:, hd, 0:n], E2[r][:, 0:n], ALU.mult, [qs.r, E2[r].r], [qi_.r])
                self.load(self.dram["qiT"].ap()[d, hd * 128:(hd + 1) * 128, t0:t0 + n], qi_[:, 0:n], [qi_.r], [self.dr("qiT", i)])
                tb = self.bank()
                tbv = tb[:].bitcast(BF16)
                for s_ in range(nsub):
                    self.tr(tbv[:, s_ * 128:(s_ + 1) * 128], kh_[:, s_ * 128:(s_ + 1) * 128], self.ident[:], [kh_.r, self.ident.r], [tb.r],
                            inc=(s_ == nsub - 1))
                k_ = khs[k % 2]
                self.copy("act", k_[:, 0:nsub, :], tbv[:, 0:nsub * 128].rearrange("p (s k) -> p s k", k=128), [tb.r], [k_.r])
                self.load(self.dram["khat"].ap()[d, t0:t0 + n, hd * 128:(hd + 1) * 128].rearrange("(s p) k -> p s k", p=128), k_[:, 0:nsub, :],
                          [k_.r], [self.dr("khat", i)])
            self.pipe(items2, [s1, s2, s3, s4])

            items3 = [(hd, s_) for hd in range(4) for s_ in range(nsub)]
            abk = {}

            def i0(it, k):
                hd, s_ = it
                sl = slice(s_ * 128, (s_ + 1) * 128)
                for d in range(2):
                    ab_ = self.psum[2 + self.psi % 6]
                    self.psi += 1
                    self.mm(ab_[:, 0:128], kt[d][:, hd, sl], qt[d][:, hd, sl], True, True, [kt[d].r, qt[d].r], [ab_.r])
                    abk[(it, d)] = ab_

            def i1(it, k):
                hd, s_ = it
                sl = slice(s_ * 128, (s_ + 1) * 128)
                obk = self.psum[hd % 2]
                ats = []
                for d in range(2):
                    ab_ = abk.pop((it, d))
                    a_ = attT[d][ai[0] % 2]
                    self.S.op("dve", lambda hw, a_=a_, ab_=ab_, d=d: hw.copy_predicated(a_[:], cmask[:, d, :], ab_[:, 0:128]),
                              [cmask.r, ab_.r], [a_.r])
                    ats.append(a_)
                ai[0] += 1
                self.mm(obk[:, sl], vsb[:, s_, hd * 128:(hd + 1) * 128], ats[0][:], True, False, [vsb.r, ats[0].r], [obk.r], inc=False)
                self.mm(obk[:, sl], vsb[:, s_, hd * 128:(hd + 1) * 128], ats[1][:], False, True, [vsb.r, ats[1].r], [obk.r], inc=True)
                if s_ == nsub - 1:
                    self.copy("act", oi[:, 0:n], obk[:, 0:n], [obk.r], [oi.r])
                    self.load(self.dram["ointra"].ap()[hd * 128:(hd + 1) * 128, t0:t0 + n], oi[:, 0:n], [oi.r], [self.dr("ointra", i)])
            self.pipe(items3, [i0, i1])
        self.end_phase()

    def phase_fourier(self, mod_next=None):
        self.begin_phase()
        fj = [0]
        tasks = self.mod_tasks(mod_next, lambda: self.psum[4 if fj[0] % 2 == 0 else 0]) if mod_next is not None and mod_next < DEPTH else []
        PQ = self.psb("fPQ", [128, 34, 4, 256], BF16)
        allpq = [self.dr("PQ", i) for i in range(9)]
        pv = self.dram["PQ"].ap().rearrange("(c p) g n -> p c (g n)", p=128)
        for c0 in range(0, 34, 2):
            self.load(PQ[:, c0:c0 + 2].rearrange("p c g n -> p c (g n)"), pv[:, c0:c0 + 2, :], allpq, [PQ.r])
        cr = self.pring("fC", 3, [128, 2, 4, 512], BF16)
        yo = self.pring("fy", 2, [128, 512], BF16)
        dft = self.dram["c_dft"].ap()
        mv = self.xview("mixT")
        ci = 0
        for j in range(8):
            banks = [self.psum[g] for g in range(4)] if j % 2 == 0 else [self.psum[4 + g] for g in range(4)]
            for tq in range(8):
                if tasks and (j * 8 + tq) % 4 == 1:
                    fj[0] = j
                    tasks.pop(0)()
                c_ = cr[ci % 3]
                ci += 1
                for z in range(2):
                    self.load(c_[:, z], dft[z, tq * 512:(tq + 1) * 512, j * 512:(j + 1) * 512].rearrange("(t p) n -> p t n", p=128),
                              [], [c_.r])
                for t4 in range(4):
                    tc = tq * 4 + t4
                    for g in range(4):
                        self.mm(banks[g][:, 0:512], PQ[:, tc, g, 0:128], c_[:, 0, t4, :], tc == 0, False, [PQ.r, c_.r], [banks[g].r], inc=False)
                        self.mm(banks[g][:, 0:512], PQ[:, tc, g, 128:256], c_[:, 1, t4, :], False, tc == 31, [PQ.r, c_.r], [banks[g].r],
                                inc=(tc == 31 or (t4 == 3 and g == 3)))
            for g in range(4):
                y = yo[g % 2]
                self.copy("act" if g % 2 == 0 else "dve", y[:], banks[g][:, 0:512], [banks[g].r], [y.r])
                self.load(mv[:, g, j * 512:(j + 1) * 512], y[:], [y.r], [self.dr("mixT", j)])
        while tasks:
            tasks.pop(0)()
        c2 = self.psb("fC2", [128, 2, 2, 256], BF16)
        for z in range(2):
            self.load(c2[:, z], self.dram["c_dft256"].ap()[z].rearrange("(c p) n -> p c n", p=128), [], [c2.r])
        for g in range(4):
            bk = self.bank()
            for tc in range(2):
                self.mm(bk[:, 0:256], PQ[:, 32 + tc, g, 0:128], c2[:, 0, tc, :], tc == 0, False, [PQ.r, c2.r], [bk.r], inc=False)
                self.mm(bk[:, 0:256], PQ[:, 32 + tc, g, 128:256], c2[:, 1, tc, :], False, tc == 1, [PQ.r, c2.r], [bk.r], inc=(tc == 1))
            y = yo[g % 2]
            self.copy("act", y[:, 0:256], bk[:, 0:256], [bk.r], [y.r])
            self.load(mv[:, g, L:L + CT], y[:, 0:256], [y.r], [self.dr("mixT", 8)])
        self.end_phase()

    def phase_hgrn_scan(self, e):
        self.begin_phase()
        nl, ncx = L // CH, CT // CH
        PC = 8
        NP = NCH // PC
        gn = self.psb("hgn", [128, 4], F32)
        self.load(gn[:], self.dram["gnT"].ap()[e], [], [gn.r])
        oacc = self.psb("hoacc", [128, 4, T], F32)
        S32 = self.psb("hS32", [128, 8, 128], F32)
        Sbf = self.pring("hSbf", 8, [128, 128], BF16)
        DD = self.psb("hDD", [128, 8, NCH], F32)
        KHr = self.pring("hKH", 2, [CH, 8, PC, 128], BF16)
        VHr = self.pring("hVH", 2, [CH, 8, PC, 128], BF16)
        QIr = self.pring("hQI", 2, [128, 8, PC * CH], BF16)
        sq = self.psb("hsq", [128, 512], BF16)
        rstd = self.psb("hrstd", [128, 512], F32)
        tmp = self.psb("htmp", [128, 512], F32)
        gsb = self.pring("hgs", 2, [128, 512], BF16)
        ob = self.pring("hob", 2, [128, 512], BF16)
        mv = self.xview("mixT")
        al = lambda nm: [self.dr(nm, i) for i in range(9)]
        S32r = [self.S.res("S32_%d" % c_) for c_ in range(8)]
        oar = [self.S.res("oacc_%d" % c_) for c_ in range(4)]
        self.memset("pool", S32[:], 0.0, S32r)
        for ch in range(8):
            self.memset("pool", Sbf[ch][:], 0.0, [Sbf[ch].r])
        for hd in range(4):
            self.load(oacc[:, hd, :], self.dram["ointra"].ap()[hd * 128:(hd + 1) * 128, :], al("ointra"), [oar[hd]])
        for ch in range(8):
            d, hd = ch // 4, ch % 4
            self.load(DD[:, ch, :], self.dram["dT"].ap()[d, hd * 128:(hd + 1) * 128, :], al("dT"), [DD.r])
        kv = self.dram["khat"].ap()
        vv = self.dram["vtok2"].ap()
        qiv = self.dram["qiT"].ap()

        def chunk_range(d, j):
            if j == 0:
                return nl
            return (j - 1) * PC if d == 0 else nl - j * PC

        def load_piece(j):
            kh, vh, qi = KHr[j % 2], VHr[j % 2], QIr[j % 2]
            for ch in range(8):
                d, hd = ch // 4, ch % 4
                c0 = chunk_range(d, j)
                ts_ = slice(c0 * CH, (c0 + PC) * CH)
                hs = slice(hd * 128, (hd + 1) * 128)
                self.load(kh[:, ch], kv[d, ts_, hs].rearrange("(c p) k -> p c k", p=CH), al("khat"), [kh.r])
                self.load(vh[:, ch], vv[ts_, hs].rearrange("(c p) k -> p c k", p=CH), al("vtok2"), [vh.r])
                self.load(qi[:, ch, :], qiv[d, hs, ts_], al("qiT"), [qi.r])

        def bidx(d, q):
            return q if d == 0 else PC - 1 - q

        def emit_U(p):
            j, q = p // PC, p % PC
            kh, vh = KHr[j % 2], VHr[j % 2]
            for half in range(2):
                ub = self.psum[4 + 2 * (p % 2) + half]
                for c4 in range(4):
                    ch = half * 4 + c4
                    ix = bidx(ch // 4, q)
                    self.mm(ub[:, c4 * 128:(c4 + 1) * 128], kh[:, ch, ix, :], vh[:, ch, ix, :], True, True, [kh.r, vh.r], [ub.r],
                            inc=(c4 == 3))

        load_piece(0)
        emit_U(0)
        for p in range(NCH):
            j, q = p // PC, p % PC
            if q == 0 and j + 1 < NP:
                load_piece(j + 1)
            if p + 1 < NCH:
                emit_U(p + 1)
            qi = QIr[j % 2]
            for ch in range(8):
                d = ch // 4
                ix = bidx(d, q)
                ib = self.psum[ch // 2]
                col = (ch % 2) * 256 + ix * CH
                self.mm(ib[:, col:col + CH], Sbf[ch][:], qi[:, ch, ix * CH:(ix + 1) * CH], True, True, [Sbf[ch].r, qi.r], [ib.r],
                        inc=(ch % 2 == 1))
            if q == PC - 1:
                for ch in range(8):
                    d, hd = ch // 4, ch % 4
                    c0 = chunk_range(d, j)
                    ib = self.psum[ch // 2]
                    cb_ = (ch % 2) * 256
                    dst = oacc[:, hd, c0 * CH:(c0 + PC) * CH]
                    self.tt("dve", dst, dst, ib[:, cb_:cb_ + PC * CH], ALU.add, [oar[hd], ib.r], [oar[hd]])
            for ch in range(8):
                ub = self.psum[4 + 2 * (p % 2) + ch // 4]
                c4 = ch % 4
                self.stt(S32[:, ch, :], S32[:, ch, :], DD[:, ch, p:p + 1], ub[:, c4 * 128:(c4 + 1) * 128], ALU.mult, ALU.add,
                         [S32r[ch], DD.r, ub.r], [S32r[ch]])
            if p + 1 < NCH:
                for ch in range(8):
                    self.copy("act", Sbf[ch][:], S32[:, ch, :], [S32r[ch]], [Sbf[ch].r])
        for hd in range(4):
            hs = slice(hd * 128, (hd + 1) * 128)
            for i in range(9):
                n = 512 if i < 8 else CT
                t0 = i * 512
                g_ = gsb[i % 2]
                o_ = ob[i % 2]
                self.load(g_[:, 0:n], self.dram["gT"].ap()[hs, t0:t0 + n], al("gT"), [g_.r])
                self.head_rstd(oacc[:, hd, t0:t0 + n], [oar[hd]], n, sq, rstd)
                self.stt(tmp[:, 0:n], oacc[:, hd, t0:t0 + n], gn[:, hd:hd + 1], rstd[:, 0:n], ALU.mult, ALU.mult, [oar[hd], gn.r, rstd.r], [tmp.r])
                self.tt("pool", o_[:, 0:n], tmp[:, 0:n], g_[:, 0:n], ALU.mult, [tmp.r, g_.r], [o_.r])
                self.load(mv[:, 4 + hd, t0:t0 + n], o_[:, 0:n], [o_.r], [self.dr("mixT", i)])
        self.end_phase()

    def finish(self):
        self.S.wait_all("sp", self.out_res)
        self.S.emit()
        return self.nc


def bf(a):
    return np.asarray(a, dtype=np.float32).astype(ml_dtypes.bfloat16)


def host_consts():
    c = {}
    c["c_ident"] = bf(np.eye(128))
    return c


def prep_core(b, inp, consts):
    f = lambda a: np.ascontiguousarray(np.asarray(a, dtype=np.float32))
    m = dict(consts)
    m["xT"] = f(np.concatenate([inp["x"][b].T, inp["ctx"][b].T], axis=1))
    cc = np.stack([inp["c"][b], inp["c_ctx"]], axis=1)
    m["cT"] = f(cc.reshape(8, 128, 2).transpose(1, 0, 2))
    m["bmodT"] = f(inp["b_mod"].reshape(DEPTH, 48, 128).transpose(2, 0, 1))
    m["w_mod"] = f(inp["w_mod"])
    m["fngT"] = f(inp["final_norm_g"].reshape(8, 128).T)
    return m


def prep_weights(inp):
    f = lambda a: np.ascontiguousarray(np.asarray(a, dtype=np.float32))
    m = {}
    for k in ("w_mod", "w_in_ab", "w_out_ab", "w_qkv", "w_out_att", "w_up", "w_down"):
        m[k] = f(inp[k])
    m["bmodT"] = f(inp["b_mod"].reshape(DEPTH, 48, 128).transpose(2, 0, 1))
    m["fngT"] = f(inp["final_norm_g"].reshape(8, 128).T)
    cwt = np.asarray(inp["conv_w"]).reshape(DEPTH, 9, NFF, 128).transpose(0, 3, 2, 1)
    m["convwT"] = f(cwt)
    m["convbT"] = f(np.asarray(inp["conv_b"]).reshape(DEPTH, NFF, 128).transpose(0, 2, 1))
    return m


def rope_tables():
    t = np.arange(L)
    row = (t // GRID).astype(np.float32)
    colp = (t % GRID).astype(np.float32)
    nf = 32
    freqs = (10000.0 ** (-np.arange(nf, dtype=np.float32) / nf)).astype(np.float32)
    ang = np.concatenate([row[:, None] * freqs, colp[:, None] * freqs], axis=-1)
    cos = np.repeat(np.cos(ang), 2, axis=1).T
    sin = np.repeat(np.sin(ang), 2, axis=1).T
    perm = np.zeros((128, 128), np.float32)
    for dp in range(128):
        if dp % 2 == 0:
            perm[dp + 1, dp] = -1.0
        else:
            perm[dp - 1, dp] = 1.0
    return np.ascontiguousarray(cos, np.float32), np.ascontiguousarray(sin, np.float32), bf(perm)


def ab_consts():
    c = {}
    ch = np.arange(128)
    ang = 2 * np.pi * np.outer(ch, ch) / 128.0
    c["c_cs"] = bf(np.concatenate([np.cos(ang), -np.sin(ang)], axis=1) / np.sqrt(128.0))
    t = np.arange(L, dtype=np.int64)
    m = np.outer(t, t) % L
    a = (2 * np.pi / L) * m
    c["c_dft"] = np.stack([bf(np.cos(a) / 64.0), bf(np.sin(a) / 64.0)])
    t2 = np.arange(CT, dtype=np.int64)
    a2 = (2 * np.pi / CT) * (np.outer(t2, t2) % CT)
    c["c_dft256"] = np.stack([bf(np.cos(a2) / 16.0), bf(np.sin(a2) / 16.0)])
    s_ = np.arange(128)[:, None]
    t_ = np.arange(128)[None, :]
    same = (s_ // CH) == (t_ // CH)
    mk = np.stack([(same & (s_ <= t_)), (same & (s_ >= t_))], axis=1).astype(np.int32)
    c["c_mask"] = np.ascontiguousarray(mk)
    sm = np.ones((128, 512), np.float32)
    sm[:, ::CH] = 0.0
    c["c_smask"] = sm
    return c


def build_full():
    K = KB()
    K.setup_consts()
    K.din('xT', [D, T])
    K.din('convwT', [DEPTH, 128, NFF, 9]); K.din('convbT', [DEPTH, 128, NFF])
    K.din('qkgT', [2, 128, 2]); K.din('c_perm', [128, 128], BF16); K.din('cosT', [128, L]); K.din('sinT', [128, L])
    K.din('c_cs', [128, 256], BF16); K.din('c_dft', [2, L, L], BF16); K.din('c_dft256', [2, CT, CT], BF16)
    K.din('c_mask', [128, 2, 128], I32); K.din('c_smask', [128, 512]); K.din('hglT', [128, 2, 2, 4]); K.din('gnT', [2, 128, 4])
    K.dscratch('xa', [D, T], F32); K.dscratch('xb', [D, T], F32); K.dscratch('mixT', [D, T], BF16)
    K.dscratch('qT', [1280, T], BF16); K.dscratch('vtok', [T, 256], BF16)
    K.dscratch('PQ', [T, 4, 256], BF16); K.dscratch('gT', [512, T], BF16); K.dscratch('vtok2', [T, 512], BF16)
    K.dscratch('qtT', [2, 512, T], BF16); K.dscratch('khat', [2, T, 512], BF16)
    K.dscratch('dT', [2, 512, NCH], F32); K.dscratch('qiT', [2, 512, T], BF16); K.dscratch('ointra', [512, T], F32)
    def worder(l):
        o = [('w_in_ab', l // 2), ('w_out_ab', l // 2)] if l % 2 == 0 else [('w_qkv', l // 2), ('w_out_att', l // 2)]
        return o + [('w_up', l), ('w_down', l)]
    K.wcast_setup([('w_in_ab', [2, D, 3072]), ('w_out_ab', [2, D, D]), ('w_up', [DEPTH, D, 2 * DFF]), ('w_down', [DEPTH, DFF, D]),
                   ('w_qkv', [2, D, 1536]), ('w_out_att', [2, D, D])])
    K.wcast_issue(worder(0)[0:2])
    K.mod_setup()
    K.phase_mod0()
    K.wcast_issue(worder(0)[2:4])
    for l in range(DEPTH):
        xin = 'xT' if l == 0 else 'xa'
        if l % 2 == 0:
            K.phase_ab_in(l, l // 2, xin)
            K.phase_fourier(l + 1)
            K.phase_hgrn_scan(l // 2)
            K.phase_out(l, 'w_out_ab', l // 2, xin, 'xb')
        else:
            K.phase_qkv(l, l // 2, xin)
            K.phase_attn(l + 1)
            K.phase_out(l, 'w_out_att', l // 2, xin, 'xb')
        K.phase_ffn(l, 'xb', 'xa', bg=(K.wcast_tasks(worder(l + 1)) if l + 1 < DEPTH else None))
    K.phase_final('xa')
    nc = K.finish()
    return nc, K


def kernel(**inputs):
    inp = {k: np.asarray(v) for k, v in inputs.items()}
    nc, K = build_full()
    shared = dict(host_consts())
    shared.update(prep_weights(inp))
    shared.update(ab_consts())
    cos, sin, perm = rope_tables()
    shared['cosT'] = cos; shared['sinT'] = sin; shared['c_perm'] = perm
    shared['qkgT'] = np.ascontiguousarray(np.stack([inp['q_norm_g'], inp['k_norm_g']], axis=2).astype(np.float32))
    shared['hglT'] = np.ascontiguousarray(inp['hg_lb_logits'].reshape(2, 2, 4, 128).transpose(3, 0, 1, 2).astype(np.float32))
    shared['gnT'] = np.ascontiguousarray(inp['hg_norm_g'].reshape(2, 4, 128).transpose(0, 2, 1).astype(np.float32))
    in_maps = []
    for b in range(8):
        m = dict(shared)
        m.update(prep_core(b, inp, {}))
        in_maps.append({k: v for k, v in m.items() if k in K.dram})
    res = run_bass_kernel_spmd(nc, in_maps, core_ids=list(range(8)))
    out = np.stack([np.ascontiguousarray(r['outT'].T) for r in res.results], axis=0)
    return out.astype(np.float32)
```

```python
import numpy as np
import concourse.bass as bass
import concourse.mybir as mybir

F32 = mybir.dt.float32
BF16 = mybir.dt.bfloat16
I32 = mybir.dt.int32
U8 = mybir.dt.uint8
AF = mybir.ActivationFunctionType
ALU = mybir.AluOpType

SEM_LIMIT = 30000


class Res:
    __slots__ = ("name", "w", "r")

    def __init__(self, name):
        self.name = name
        self.w = None
        self.r = {}


class Eng:
    def __init__(self, name, hw):
        self.name = name
        self.hw = hw
        self.ops = []
        self.sem = None
        self.count = 0
        self.seen = {}
        self.pending = []
        self.nsem = 0


class Sched:
    def __init__(self, nc, same_eng_sync=True):
        self.nc = nc
        self.same_eng_sync = same_eng_sync
        self.engs = {
            "pe": Eng("pe", nc.tensor),
            "act": Eng("act", nc.scalar),
            "dve": Eng("dve", nc.vector),
            "pool": Eng("pool", nc.gpsimd),
            "sp": Eng("sp", nc.sync),
        }
        self.semid = 0
        self.dma_slots = {}
        self.dma_rr = {}
        self.nres = 0
        self.barrier_exempt = set()

    def res(self, name=None):
        self.nres += 1
        return Res(name or f"r{self.nres}")

    def _newsem(self, tag):
        self.semid += 1
        s = self.nc.alloc_semaphore(name=f"s{self.semid}_{tag}")
        return (self.semid, s)

    def _deps(self, e, reads, writes):
        deps = []
        for r in reads:
            if r.w is not None:
                deps.append(r.w)
        for w in writes:
            if w.w is not None:
                deps.append(w.w)
            for ev in w.r.values():
                deps.append(ev)
        waits = []
        for ev in deps:
            key, sem, val, en = ev
            if en == e.name and (e.name == "pe" or not self.same_eng_sync):
                continue
            if e.seen.get(key, 0) >= val:
                continue
            e.seen[key] = val
            waits.append((sem, val))
        return waits

    def _mark(self, ev, reads, writes, en):
        for r in reads:
            r.r[en] = ev
        for w in writes:
            w.w = ev
            w.r = {}

    def op(self, en, fn, reads=(), writes=(), inc=True):
        e = self.engs[en]
        waits = self._deps(e, reads, writes)
        if inc:
            if e.sem is None or e.count >= SEM_LIMIT:
                e.sem = self._newsem(en)
                e.count = 0
            e.count += 1
            ev = (e.sem[0], e.sem[1], e.count, en)
            for (res, mode) in e.pending:
                if mode == "r":
                    res.r[en] = ev
                else:
                    res.w = ev
                    res.r = {}
            e.pending = []
            self._mark(ev, reads, writes, en)
            e.ops.append((waits, fn, (e.sem[1], 1)))
        else:
            for r in reads:
                e.pending.append((r, "r"))
            for w in writes:
                e.pending.append((w, "w"))
            e.ops.append((waits, fn, None))

    def dma(self, q, out, in_, reads=(), writes=(), nslots=8, **kw):
        e = self.engs[q]
        waits = self._deps(e, reads, writes)
        slots = self.dma_slots.setdefault(q, [])
        if len(slots) < nslots:
            slots.append([self._newsem("dma" + q), 0])
            si = len(slots) - 1
        else:
            si = self.dma_rr.get(q, 0) % nslots
        self.dma_rr[q] = si + 1
        slot = slots[si]
        if 16 * (slot[1] + 1) > SEM_LIMIT:
            slot[0] = self._newsem("dma" + q)
            slot[1] = 0
        key, sem = slot[0]
        if slot[1] > 0 and e.seen.get(key, 0) < 16 * slot[1]:
            e.seen[key] = 16 * slot[1]
            waits.append((sem, 16 * slot[1]))
        slot[1] += 1
        ev = (key, sem, 16 * slot[1], "dma")
        self._mark(ev, reads, writes, "dma%d_%s" % (si, q))

        def fn(hw, out=out, in_=in_, kw=kw):
            return hw.dma_start(out=out, in_=in_, **kw)
        e.ops.append((waits, fn, (sem, 16)))
        return ev

    def barrier(self):
        evs = []
        for e in self.engs.values():
            assert not e.pending
            if e.sem is not None and e.count > 0:
                evs.append((e.sem[0], e.sem[1], e.count))
        for q, slots in self.dma_slots.items():
            if q in self.barrier_exempt:
                continue
            for slot in slots:
                if slot[1] > 0:
                    evs.append((slot[0][0], slot[0][1], 16 * slot[1]))
        for e in self.engs.values():
            waits = []
            for key, sem, val in evs:
                if e.seen.get(key, 0) >= val:
                    continue
                if e.sem is not None and key == e.sem[0]:
                    continue
                e.seen[key] = val
                waits.append((sem, val))
            e.ops.append((waits, None, None))

    def wait_all(self, en, resources):
        e = self.engs[en]
        waits = self._deps(e, list(resources), [])
        e.ops.append((waits, None, None))

    def emit(self):
        nc = self.nc
        for e in self.engs.values():
            assert not e.pending, f"engine {e.name} has pending non-inc ops at end"
        with nc.Block() as block:
            def run(e, hw):
                for waits, fn, inc in e.ops:
                    for (sem, val) in waits:
                        hw.wait_ge(sem, val)
                    if fn is None:
                        continue
                    ins = fn(hw)
                    if inc is not None:
                        ins.then_inc(inc[0], inc[1])

            @block.tensor
            def _(hw):
                run(self.engs["pe"], hw)

            @block.scalar
            def _(hw):
                run(self.engs["act"], hw)

            @block.vector
            def _(hw):
                run(self.engs["dve"], hw)

            @block.gpsimd
            def _(hw):
                run(self.engs["pool"], hw)

            @block.sync
            def _(hw):
                run(self.engs["sp"], hw)

    def stats(self):
        return {k: len(v.ops) for k, v in self.engs.items()}
from contextlib import ExitStack
import ml_dtypes
from concourse.bass_utils import run_bass_kernel_spmd

D = 1024
L = 4096
CT = 256
T = L + CT
DEPTH = 4
DFF = 2816
NFF = DFF // 128
EPS = 1e-6
GRID = 64
CH = 32
NCH = T // CH


class TL:
    def __init__(self, t, r):
        self.t = t
        self.r = r

    def __getitem__(self, k):
        return self.t[k]


class KB:
    def __init__(self, debug_outs=()):
        self.nc = bass.Bass("TRN2", target_bir_lowering=False)
        self.S = Sched(self.nc)
        self.dram = {}
        self.dres = {}
        self.debug_outs = set(debug_outs)
        self.nt = 0
        self.psum = []
        self.psi = 0
        self.wres = {}

    def din(self, name, shape, dt=F32):
        t = self.nc.dram_tensor(name, list(shape), dt, kind="ExternalInput")
        self.dram[name] = t
        return t

    def dscratch(self, name, shape, dt, out=False):
        kind = "ExternalOutput" if (out or name in self.debug_outs) else "Internal"
        t = self.nc.dram_tensor(name, list(shape), dt, kind=kind)
        self.dram[name] = t
        return t

    def dr(self, name, idx=0):
        k = (name, idx)
        if k not in self.dres:
            self.dres[k] = self.S.res("%s_%s" % (name, idx))
        return self.dres[k]

    def sb(self, name, shape, dt):
        self.nt += 1
        t = self.nc.alloc_sbuf_tensor("%s_%d" % (name, self.nt), list(shape), dt)
        return TL(t, self.S.res(name))

    def ring(self, name, n, shape, dt):
        return [self.sb("%s%d" % (name, i), shape, dt) for i in range(n)]

    def init_psum(self):
        for i in range(8):
            t = self.nc.alloc_psum_tensor("ps%d" % i, [128, 512], F32)
            self.psum.append(TL(t, self.S.res("ps%d" % i)))

    def bank(self):
        b = self.psum[self.psi % 8]
        self.psi += 1
        return b

    def mm(self, out, lhsT, rhs, start, stop, reads, writes, inc=None, **kw):
        if inc is None:
            inc = stop
        self.S.op("pe", lambda hw: hw.matmul(out, lhsT, rhs, start=start, stop=stop, **kw), reads, writes, inc)

    def tr(self, out, in_, ident, reads, writes, inc=True):
        self.S.op("pe", lambda hw: hw.transpose(out, in_, ident), reads, writes, inc)

    def act(self, out, in_, func, reads, writes, bias=None, scale=None, accum_out=None):
        kw = {}
        if bias is not None:
            kw["bias"] = bias
        if scale is not None:
            kw["scale"] = scale
        if accum_out is not None:
            kw["accum_out"] = accum_out
        self.S.op("act", lambda hw: hw.activation(out, in_, func, **kw), reads, writes)

    def tt(self, en, out, in0, in1, op, reads, writes):
        self.S.op(en, lambda hw: hw.tensor_tensor(out, in0, in1, op), reads, writes)

    def ts(self, en, out, in0, s1, op0, reads, writes, s2=None, op1=None):
        if op1 is None:
            self.S.op(en, lambda hw: hw.tensor_scalar(out, in0, s1, None, op0), reads, writes)
        else:
            self.S.op(en, lambda hw: hw.tensor_scalar(out, in0, s1, s2, op0, op1), reads, writes)

    def stt(self, out, in0, scalar, in1, op0, op1, reads, writes):
        self.S.op("dve", lambda hw: hw.scalar_tensor_tensor(out, in0, scalar, in1, op0, op1), reads, writes)

    def copy(self, en, out, in_, reads, writes):
        if en == "act":
            self.S.op(en, lambda hw: hw.copy(out, in_), reads, writes)
        else:
            self.S.op(en, lambda hw: hw.tensor_copy(out, in_), reads, writes)

    def memset(self, en, ap, val, writes):
        self.S.op(en, lambda hw: hw.memset(ap, val), (), writes)

    def load(self, out, in_, reads, writes, q="sp", **kw):
        return self.S.dma(q, out, in_, reads, writes, **kw)

    def setup_consts(self):
        nc = self.nc
        self.init_psum()
        self.c_ident = self.din("c_ident", [128, 128], BF16)
        self.ident = self.sb("ident", [128, 128], BF16)
        self.load(self.ident[:], self.c_ident.ap(), [], [self.ident.r])
        self.ones = self.sb("ones", [128, 128], BF16)
        self.memset("pool", self.ones[:], 1.0, [self.ones.r])

    def rms_rstd(self, xt, n, rstd, sq, width=D):
        nch = width // 128
        self.act(sq[:, 0:nch, 0:n], xt[:, 0:nch, 0:n], AF.Square, [xt.r], [sq.r])
        for n0 in range(0, n, 512):
            n1 = min(n, n0 + 512)
            bk = self.bank()
            for c in range(nch):
                self.mm(bk[:, 0:n1 - n0], self.ones[:], sq[:, c, n0:n1], c == 0, c == nch - 1,
                        [self.ones.r, sq.r], [bk.r])
            self.act(rstd[:, n0:n1], bk[:, 0:n1 - n0], AF.Ln, [bk.r], [rstd.r], bias=self.epsb[:, 0:1], scale=1.0 / width)
            self.act(rstd[:, n0:n1], rstd[:, n0:n1], AF.Exp, [rstd.r], [rstd.r], scale=-0.5)

    def xview(self, name):
        return self.dram[name].ap().rearrange("(c p) n -> p c n", p=128)

    def mod_setup(self):
        cT = self.din("cT", [128, 8, 2])
        bmodT = self.din("bmodT", [128, DEPTH, 48])
        self.wmod = self.din("w_mod", [DEPTH, D, 6 * D])
        self.epsb = self.sb("epsb", [128, 1], F32)
        self.memset("pool", self.epsb[:], EPS, [self.epsb.r])
        self.modsb = self.sb("modsb", [128, DEPTH, 48, 2], F32)
        self.modr = [self.S.res("mod%d" % l) for l in range(DEPTH)]
        csb = self.sb("csb", [128, 8, 2], F32)
        self.ssb = self.sb("ssb", [128, 8, 2], F32)
        self.bsb = self.sb("bsb", [128, DEPTH, 48], F32)
        self.identf = self.sb("identf", [2, 2], F32)
        self.load(csb[:], cT.ap(), [], [csb.r])
        self.load(self.bsb[:], bmodT.ap(), [], [self.bsb.r])
        self.act(self.ssb[:], csb[:], AF.Silu, [csb.r], [self.ssb.r])
        self.copy("dve", self.identf[:], self.ident[0:2, 0:2], [self.ident.r], [self.identf.r])

    def mod_tasks(self, l, bank_fn=None):
        bank_fn = bank_fn or self.bank
        NW = 3
        wring = self.pring("wmod", NW, [128, 8, 512], F32)
        mr = self.psb("mrow", [2, 6 * D], F32)
        wv = self.wmod.ap()[l].rearrange("(k p) n -> p k n", p=128)
        ssb, bsb = self.ssb, self.bsb
        loaded = set()

        def ld(pi):
            if pi < 12 and pi not in loaded:
                loaded.add(pi)
                w = wring[pi % NW]
                self.load(w[:], wv[:, :, pi * 512:(pi + 1) * 512], [], [w.r])

        def piece(pi):
            def f():
                for a_ in range(NW):
                    ld(pi + a_)
                w = wring[pi % NW]
                bk = bank_fn()
                for k in range(8):
                    self.mm(bk[0:2, 0:512], ssb[:, k, :], w[:, k, :], k == 0, k == 7, [w.r, ssb.r], [bk.r])
                self.copy("act", mr[0:2, pi * 512:(pi + 1) * 512], bk[0:2, 0:512], [bk.r], [mr.r])
            return f

        def fin():
            bk = bank_fn()
            for j in range(48):
                self.tr(bk[:, 2 * j:2 * j + 2], mr[0:2, j * 128:(j + 1) * 128], self.identf[0:2, 0:2], [mr.r, self.identf.r], [bk.r],
                        inc=(j == 47))
            self.tt("dve", self.modsb[:, l], bk[:, 0:96].rearrange("p (j t) -> p j t", t=2),
                    bsb[:, l, :].unsqueeze(2).broadcast_to([128, 48, 2]), ALU.add, [bk.r, bsb.r], [self.modr[l]])
            for sp in (1, 4):
                sl = self.modsb[:, l, sp * 8:(sp + 1) * 8, :]
                self.ts("dve", sl, sl, 1.0, ALU.add, [self.modr[l]], [self.modr[l]])
        def pre():
            for a_ in range(NW - 1):
                ld(a_)
        return [pre] + [piece(pi) for pi in range(12)] + [fin]

    def phase_mod0(self):
        self.begin_phase()
        for t in self.mod_tasks(0):
            t()
        self.end_phase()

    def mod(self, l, split, c, col):
        return self.modsb[:, l, split * 8 + c, col:col + 1]

    def phase_final(self, xa):
        gT = self.din("fngT", [128, 8])
        gsb = self.sb("fng", [128, 8], F32)
        self.load(gsb[:], gT.ap(), [], [gsb.r])
        outT = self.dscratch("outT", [D, L], F32, out=True)
        xv = self.xview(xa)
        ov = self.xview("outT")
        xr = self.ring("fx", 2, [128, 8, 512], F32)
        sq = self.ring("fsq", 2, [128, 8, 512], BF16)
        rs = self.ring("frs", 2, [128, 512], F32)
        yr = self.ring("fy", 2, [128, 8, 512], F32)
        for i in range(L // 512):
            x = xr[i % 2]
            self.load(x[:], xv[:, :, i * 512:(i + 1) * 512], [self.dr(xa, i)], [x.r])
            self.rms_rstd(x, 512, rs[i % 2], sq[i % 2])
            r = rs[i % 2]
            y = yr[i % 2]
            for c in range(8):
                if c < 5:
                    self.stt(y[:, c, :], x[:, c, :], gsb[:, c:c + 1], r[:], ALU.mult, ALU.mult, [x.r, r.r, gsb.r], [y.r])
                else:
                    self.tt("pool", y[:, c, :], x[:, c, :], r[:], ALU.mult, [x.r, r.r], [y.r])
                    self.act(y[:, c, :], y[:, c, :], AF.Identity, [y.r, gsb.r], [y.r], scale=gsb[:, c:c + 1])
            self.load(ov[:, :, i * 512:(i + 1) * 512], y[:], [y.r], [self.dr("outT", i)])
        self.out_res = [self.dr("outT", i) for i in range(L // 512)]

    def begin_phase(self):
        self.pstack = ExitStack()

    def psb(self, name, shape, dt):
        self.nt += 1
        t = self.pstack.enter_context(self.nc.sbuf_tensor("%s_%d" % (name, self.nt), list(shape), dt))
        return TL(t, self.S.res(name))

    def pring(self, name, n, shape, dt):
        return [self.psb("%s%d" % (name, i), shape, dt) for i in range(n)]

    def end_phase(self):
        self.S.barrier()
        self.pstack.close()

    def phase_wcast(self, specs, order):
        self.wcast_setup(specs)
        self.wcast_issue(order)

    def wcast_setup(self, specs):
        self.S.barrier_exempt.add("pool")
        self.wc = {}
        for name, shape in specs:
            src = self.din(name, shape)
            dst = self.dscratch("wb_" + name, shape, BF16)
            per = int(np.prod(shape[1:]))
            assert per % 1024 == 0
            self.wc[name] = (src, dst, per // 1024)

    def wcast_tasks(self, order):
        tasks = []
        for name, l in order:
            src, dst, rows = self.wc[name]
            sv = src.ap()[l].rearrange("a b -> (a b)").rearrange("(r n) -> r n", n=1024)
            dv = dst.ap()[l].rearrange("a b -> (a b)").rearrange("(r n) -> r n", n=1024)
            rl = []
            self.wres[(name, l)] = rl
            for r0 in range(0, rows, 1024):
                r1 = min(rows, r0 + 1024)
                rr = self.S.res("wb")
                rl.append(rr)
                tasks.append(lambda dv=dv, sv=sv, r0=r0, r1=r1, rr=rr: self.load(dv[r0:r1, :], sv[r0:r1, :], [], [rr], q="pool"))
        return tasks

    def wcast_issue(self, order):
        for t in self.wcast_tasks(order):
            t()

    def wb(self, name, l):
        return self.dram["wb_" + name].ap()[l], self.wres[(name, l)]

    def norm_mod(self, xt, n0, n1, rstd, h, l, sh_split, sc_split, col):
        n = n1 - n0
        self.act(h[:, :, n0:n1], xt[:, :, n0:n1], AF.Square, [xt.r], [h.r])
        for a in range(n0, n1, 512):
            b_ = min(n1, a + 512)
            bk = self.bank()
            for c in range(8):
                self.mm(bk[:, 0:b_ - a], self.ones[:], h[:, c, a:b_], c == 0, c == 7, [self.ones.r, h.r], [bk.r])
            self.act(rstd[:, a:b_], bk[:, 0:b_ - a], AF.Ln, [bk.r], [rstd.r], bias=self.epsb[:, 0:1], scale=1.0 / D)
            self.act(rstd[:, a:b_], rstd[:, a:b_], AF.Exp, [rstd.r], [rstd.r], scale=-0.5)
        for c in range(8):
            self.stt(h[:, c, n0:n1], xt[:, c, n0:n1], self.mod(l, sc_split, c, col), rstd[:, n0:n1], ALU.mult, ALU.mult,
                     [xt.r, rstd.r, self.modr[l]], [h.r])
            self.ts("pool", h[:, c, n0:n1], h[:, c, n0:n1], self.mod(l, sh_split, c, col), ALU.add, [h.r, self.modr[l]], [h.r],
                    s2=1.0, op1=ALU.mult)

    def phase_out(self, l, wname, widx, xin, xout, skip_ctx=False):
        self.begin_phase()
        tasks = []
        wsrc, wres = self.wb(wname, widx)
        W = self.psb("wout", [128, 8, D], BF16)
        self.load(W[:], wsrc.rearrange("(k p) n -> p k n", p=128), wres, [W.r])
        mv = self.xview("mixT")
        xv = self.xview(xin)
        ov = self.xview(xout)
        mr = self.pring("omix", 2, [128, 8, 512], BF16)
        xr = self.pring("ox", 2, [128, 8, 512], F32)
        for i in range(8 if skip_ctx else 9):
            n = 512 if i < 8 else CT
            col = 0 if i < 8 else 1
            t0 = i * 512
            m = mr[i % 2]
            x = xr[i % 2]
            self.load(m[:, :, 0:n], mv[:, :, t0:t0 + n], [self.dr("mixT", i)], [m.r])
            self.load(x[:, :, 0:n], xv[:, :, t0:t0 + n], [self.dr(xin, i)], [x.r])
            for c in range(8):
                bk = self.bank()
                for k in range(8):
                    self.mm(bk[:, 0:n], W[:, k, c * 128:(c + 1) * 128], m[:, k, 0:n], k == 0, k == 7, [W.r, m.r], [bk.r])
                self.stt(x[:, c, 0:n], bk[:, 0:n], self.mod(l, 2, c, col), x[:, c, 0:n], ALU.mult, ALU.add,
                         [bk.r, x.r, self.modr[l]], [x.r])
            self.load(ov[:, :, t0:t0 + n], x[:, :, 0:n], [x.r], [self.dr(xout, i)])
            for _ in range(2):
                if tasks:
                    tasks.pop(0)()
        while tasks:
            tasks.pop(0)()
        self.end_phase()

    def phase_ffn(self, l, xin, xout, bg=None, skip_ctx=False):
        self.begin_phase()
        bg = list(bg or [])
        wd_src, wd_res = self.wb("w_down", l)
        wu_src, wu_res = self.wb("w_up", l)
        wuv = wu_src.rearrange("(k p) (g n) -> p k g n", p=128, g=2)
        wdv = wd_src.rearrange("(c p) n -> p c n", p=128)
        cw = self.psb("convw", [128, NFF, 9], F32)
        cb = self.psb("convb", [128, NFF], F32)
        self.load(cw[:], self.dram["convwT"].ap()[l], [], [cw.r])
        self.load(cb[:], self.dram["convbT"].ap()[l], [], [cb.r])
        xt = self.psb("fx", [128, 8, 1152], F32)
        h = self.psb("fh", [128, 8, 1152], BF16)
        rstd = self.psb("frstd", [128, 1152], F32)
        actT = self.psb("factT", [128, NFF, 1024], BF16)
        wur = self.pring("fwu", 3, [128, 8, 2, 256], BF16)
        Wd = self.psb("wdown", [128, NFF, D], BF16)
        self.load(Wd[:], wdv, wd_res, [Wd.r])
        Gr = self.pring("fG", 2, [128, 18 * 66], BF16)
        sr = self.pring("fsilu", 2, [128, 512], F32)
        vr = self.pring("fval", 2, [128, 1024], BF16)
        dgr = self.pring("fdiag", 2, [128, 9, 128], BF16)
        xrr = self.pring("fxr", 4, [128, 512], F32)
        for G in Gr:
            self.memset("pool", G[:], 0.0, [G.r])
        xv = self.xview(xin)
        ov = self.xview(xout)

        def geom(band):
            if band < 4:
                lo = 64 if band > 0 else 0
                hi = 64 if band < 3 else 0
                return dict(ctx=False, t0=band * 1024, lo=lo, hi=hi, ncol=1152, cen0=64, ncen=1024, col=0)
            return dict(ctx=True, t0=L, lo=0, hi=0, ncol=CT, cen0=0, ncen=CT, col=1)

        def prologue(band):
            g = geom(band)
            if not g["ctx"]:
                t0, lo, hi = g["t0"], g["lo"], g["hi"]
                rd = [self.dr(xin, i) for i in range(max(0, 2 * band - 1), min(8, 2 * band + 3))]
                if lo == 0:
                    self.memset("pool", xt[:, :, 0:64], 0.0, [xt.r])
                if hi == 0:
                    self.memset("pool", xt[:, :, 1088:1152], 0.0, [xt.r])
                self.load(xt[:, :, 64 - lo:64 + 1024 + hi], xv[:, :, t0 - lo:t0 + 1024 + hi], rd, [xt.r])
            else:
                self.load(xt[:, :, 0:CT], xv[:, :, L:L + CT], [self.dr(xin, 8)], [xt.r])
            self.norm_mod(xt, 0, g["ncol"], rstd, h, l, 3, 4, g["col"])

        wu_it = [0]
        wu_of = {}

        def load_wu(band, c):
            if c < NFF and (band, c) not in wu_of:
                wu = wur[wu_it[0] % 3]
                wu_it[0] += 1
                for g_ in range(2):
                    self.load(wu[:, :, g_, :], wuv[:, :, g_, c * 128:(c + 2) * 128], wu_res, [wu.r])
                wu_of[(band, c)] = wu
                wu_of[(band, c + 1)] = wu

        prologue(0)
        load_wu(0, 0)
        wd_it = 0
        xr_it = 0
        NB = 4 if skip_ctx else 5
        for band in range(NB):
            g = geom(band)
            ctxb, lo, hi, t0 = g["ctx"], g["lo"], g["hi"], g["t0"]
            cen0, ncen, col = g["cen0"], g["ncen"], g["col"]
            nob = 2 if not ctxb else 1
            held = {}

            def st0(c, k):
                if bg:
                    bg.pop(0)()
                if c % 2 == 0:
                    load_wu(band, c + 2)
                wu = wu_of[(band, c)]
                wo = (c % 2) * 128
                G = Gr[c % 2]
                dg = dgr[c % 2]
                val = vr[c % 2]
                self.tt("pool", dg[:], self.ident[:].unsqueeze(1).broadcast_to([128, 9, 128]),
                        cw[:, c, :].unsqueeze(2).broadcast_to([128, 9, 128]), ALU.mult, [self.ident.r, cw.r], [dg.r])
                if not ctxb:
                    G3 = G[:].rearrange("p (r w) -> p r w", w=66)
                    for j in range(3):
                        bk = self.bank()
                        for kx in range(8):
                            self.mm(bk[:, 0:384], wu[:, kx, 0, wo:wo + 128], h[:, kx, j * 384:(j + 1) * 384], kx == 0, kx == 7,
                                    [wu.r, h.r], [bk.r])
                        ra, rb = 6 * j, 6 * j + 6
                        pa = 0
                        if j == 0 and lo == 0:
                            ra, pa = 1, 64
                        if j == 2 and hi == 0:
                            rb = 17
                        self.copy("act", G3[:, ra:rb, 1:65], bk[:, pa:pa + (rb - ra) * 64].rearrange("p (r w) -> p r w", w=64),
                                  [bk.r], [G.r])
                    if lo == 0:
                        self.memset("pool", G3[:, 0:1, :], 0.0, [G.r])
                    if hi == 0:
                        self.memset("pool", G3[:, 17:18, :], 0.0, [G.r])
                    for ob in range(2):
                        vb = self.bank()
                        for kx in range(8):
                            self.mm(vb[:, 0:512], wu[:, kx, 1, wo:wo + 128], h[:, kx, 64 + ob * 512:64 + (ob + 1) * 512], kx == 0, kx == 7,
                                    [wu.r, h.r], [vb.r])
                        self.copy("act" if ob == 0 else "dve", val[:, ob * 512:(ob + 1) * 512], vb[:, 0:512], [vb.r], [val.r])
                else:
                    bk = self.bank()
                    for kx in range(8):
                        self.mm(bk[:, 0:CT], wu[:, kx, 0, wo:wo + 128], h[:, kx, 0:CT], kx == 0, kx == 7, [wu.r, h.r], [bk.r])
                    self.memset("pool", G[:, 0:CT + 2], 0.0, [G.r])
                    self.copy("act", G[:, 1:CT + 1], bk[:, 0:CT], [bk.r], [G.r])
                    vb = self.bank()
                    for kx in range(8):
                        self.mm(vb[:, 0:CT], wu[:, kx, 1, wo:wo + 128], h[:, kx, 0:CT], kx == 0, kx == 7, [wu.r, h.r], [vb.r])
                    self.copy("act", val[:, 0:CT], vb[:, 0:CT], [vb.r], [val.r])

            def st1(c, k):
                G = Gr[c % 2]
                dg = dgr[c % 2]
                val = vr[c % 2]
                if not ctxb:
                    G3 = G[:].rearrange("p (r w) -> p r w", w=66)
                    for ob in range(2):
                        cbk = self.bank()
                        for tap in range(9):
                            dr_, dc_ = tap // 3 - 1, tap % 3 - 1
                            rhs = G3[:, 8 * ob + 1 + dr_:8 * ob + 9 + dr_, 1 + dc_:65 + dc_]
                            self.mm(cbk[:, 0:512].rearrange("p (r w) -> p r w", w=64), dg[:, tap, :], rhs, tap == 0, tap == 8,
                                    [dg.r, G.r], [cbk.r])
                        s_ = sr[ob]
                        self.act(s_[:], cbk[:, 0:512], AF.Silu, [cbk.r, cb.r], [s_.r], bias=cb[:, c:c + 1])
                        self.tt("pool" if ob == 0 else "dve", actT[:, c, ob * 512:(ob + 1) * 512], s_[:], val[:, ob * 512:(ob + 1) * 512],
                                ALU.mult, [s_.r, val.r], [actT.r])
                else:
                    cbk = self.bank()
                    for kx in range(3):
                        self.mm(cbk[:, 0:CT], dg[:, 3 + kx, :], G[:, kx:kx + CT], kx == 0, kx == 2, [dg.r, G.r], [cbk.r])
                    s_ = sr[0]
                    self.act(s_[:, 0:CT], cbk[:, 0:CT], AF.Silu, [cbk.r, cb.r], [s_.r], bias=cb[:, c:c + 1])
                    self.tt("dve", actT[:, c, 0:CT], s_[:, 0:CT], val[:, 0:CT], ALU.mult, [s_.r, val.r], [actT.r])
                    self.memset("pool", G[:, 0:CT + 2], 0.0, [G.r])

            self.pipe(list(range(NFF)), [st0, st1])
            if band + 1 < NB:
                load_wu(band + 1, 0)

            blocks = list(range(0, ncen, 512))
            for bi, a_ in enumerate(blocks):
                nb = min(512, ncen - a_)
                for dc in range(8):
                    xr = xrr[xr_it % 4]
                    xr_it += 1
                    self.load(xr[:, 0:nb], xv[:, dc, t0 + a_:t0 + a_ + nb], [self.dr(xin, (t0 + a_) // 512)], [xr.r])
                    bk = self.bank()
                    for c in range(NFF):
                        self.mm(bk[:, 0:nb], Wd[:, c, dc * 128:(dc + 1) * 128], actT[:, c, a_:a_ + nb], c == 0, c == NFF - 1, [Wd.r, actT.r], [bk.r])
                    self.stt(xr[:, 0:nb], bk[:, 0:nb], self.mod(l, 5, dc, col), xr[:, 0:nb], ALU.mult, ALU.add,
                             [bk.r, xr.r, self.modr[l]], [xr.r])
                    self.load(ov[:, dc, t0 + a_:t0 + a_ + nb], xr[:, 0:nb], [xr.r], [self.dr(xout, (t0 + a_) // 512)])
                if bi == 0 and band + 1 < NB:
                    prologue(band + 1)
        while bg:
            bg.pop(0)()
        self.end_phase()

    def head_rstd(self, src_ap, src_res, n, sq, rstd, width=128):
        self.act(sq[:, 0:n], src_ap, AF.Square, src_res, [sq.r])
        bk = self.bank()
        self.mm(bk[:, 0:n], self.ones[:], sq[:, 0:n], True, True, [self.ones.r, sq.r], [bk.r])
        self.act(rstd[:, 0:n], bk[:, 0:n], AF.Ln, [bk.r], [rstd.r], bias=self.epsb[:, 0:1], scale=1.0 / width)
        self.act(rstd[:, 0:n], rstd[:, 0:n], AF.Exp, [rstd.r], [rstd.r], scale=-0.5)

    def phase_qkv(self, l, o, xin):
        self.begin_phase()
        wsrc, wres = self.wb("w_qkv", o)
        W = self.psb("wqkv", [128, 8, 1536], BF16)
        self.load(W[:], wsrc.rearrange("(k p) n -> p k n", p=128), wres, [W.r])
        gq = self.psb("gq", [128, 2], F32)
        self.load(gq[:], self.dram["qkgT"].ap()[o], [], [gq.r])
        perm = self.psb("perm", [128, 128], BF16)
        self.load(perm[:], self.dram["c_perm"].ap(), [], [perm.r])
        xv = self.xview(xin)
        qv = self.dram["qT"].ap().rearrange("(c p) n -> p c n", p=128)
        vv = self.dram["vtok"].ap()
        xts = self.pring("qx", 2, [128, 8, 512], F32)
        hs_ = self.pring("qh", 2, [128, 8, 512], BF16)
        rstds = self.pring("qrstd", 2, [128, 512], F32)
        css = self.pring("qcos", 2, [128, 512], F32)
        sns = self.pring("qsin", 2, [128, 512], F32)
        NR = 4
        raw = self.pring("qraw", NR, [128, 512], F32)
        sq = self.pring("qsq", NR, [128, 512], BF16)
        hr = self.pring("qhr", NR, [128, 512], F32)
        qn = self.pring("qn", NR, [128, 512], BF16)
        t1 = self.pring("qt1", NR, [128, 512], F32)
        qo = self.pring("qo", NR, [128, 512], BF16)
        vs = self.pring("qvs", 2, [128, 256], BF16)

        def prologue(i):
            n = 512 if i < 8 else CT
            col = 0 if i < 8 else 1
            t0 = i * 512
            xt, h, rstd = xts[i % 2], hs_[i % 2], rstds[i % 2]
            self.load(xt[:, :, 0:n], xv[:, :, t0:t0 + n], [self.dr(xin, i)], [xt.r])
            if i < 8:
                self.load(css[i % 2][:], self.dram["cosT"].ap()[:, t0:t0 + 512], [], [css[i % 2].r])
                self.load(sns[i % 2][:], self.dram["sinT"].ap()[:, t0:t0 + 512], [], [sns[i % 2].r])
            self.norm_mod(xt, 0, n, rstd, h, l, 0, 1, col)
        prologue(0)
        NR = 4
        bkq = {}
        for i in range(9):
            n = 512 if i < 8 else CT
            t0 = i * 512
            h, cs, sn = hs_[i % 2], css[i % 2], sns[i % 2]
            items = list(range(10)) + ["v%d" % s_ for s_ in range(n // 128)]

            def q0(hd, k):
                j = k % NR
                if isinstance(hd, str):
                    s_ = int(hd[1:])
                    bk = self.bank()
                    for kx in range(8):
                        self.mm(bk[:, 0:256], h[:, kx, s_ * 128:(s_ + 1) * 128], W[:, kx, 1280:1536], kx == 0, kx == 7, [h.r, W.r], [bk.r])
                    bkq[(i, hd)] = bk
                    return
                bk = self.bank()
                for kx in range(8):
                    self.mm(bk[:, 0:n], W[:, kx, hd * 128:(hd + 1) * 128], h[:, kx, 0:n], kx == 0, kx == 7, [W.r, h.r], [bk.r])
                self.copy("act", raw[j][:, 0:n], bk[:, 0:n], [bk.r], [raw[j].r])
                self.act(sq[j][:, 0:n], bk[:, 0:n], AF.Square, [bk.r], [sq[j].r])

            def q1(hd, k):
                j = k % NR
                if isinstance(hd, str):
                    s_ = int(hd[1:])
                    bk = bkq.pop((i, hd))
                    v_ = vs[s_ % 2]
                    self.copy("act", v_[:], bk[:, 0:256], [bk.r], [v_.r])
                    self.load(vv[t0 + s_ * 128:t0 + (s_ + 1) * 128, :], v_[:], [v_.r], [self.dr("vtok", i)])
                    return
                bk = self.bank()
                self.mm(bk[:, 0:n], self.ones[:], sq[j][:, 0:n], True, True, [self.ones.r, sq[j].r], [bk.r])
                self.act(hr[j][:, 0:n], bk[:, 0:n], AF.Ln, [bk.r], [hr[j].r], bias=self.epsb[:, 0:1], scale=1.0 / 128)
                self.act(hr[j][:, 0:n], hr[j][:, 0:n], AF.Exp, [hr[j].r], [hr[j].r], scale=-0.5)
                gcol = gq[:, 0:1] if hd < 8 else gq[:, 1:2]
                dst = qn[j] if i < 8 else qo[j]
                self.stt(dst[:, 0:n], raw[j][:, 0:n], gcol, hr[j][:, 0:n], ALU.mult, ALU.mult, [raw[j].r, hr[j].r, gq.r], [dst.r])
                if i == 8:
                    self.load(qv[:, hd, t0:t0 + n], qo[j][:, 0:n], [qo[j].r], [self.dr("qT", i)])

            def q2(hd, k):
                j = k % NR
                if isinstance(hd, str) or i == 8:
                    return
                pb = self.bank()
                self.mm(pb[:, 0:n], perm[:], qn[j][:, 0:n], True, True, [perm.r, qn[j].r], [pb.r])
                self.tt("pool", t1[j][:, 0:n], qn[j][:, 0:n], cs[:, 0:n], ALU.mult, [qn[j].r, cs.r], [t1[j].r])
                self.tt("dve", raw[j][:, 0:n], pb[:, 0:n], sn[:, 0:n], ALU.mult, [pb.r, sn.r], [raw[j].r])
                self.tt("pool", qo[j][:, 0:n], t1[j][:, 0:n], raw[j][:, 0:n], ALU.add, [t1[j].r, raw[j].r], [qo[j].r])
                self.load(qv[:, hd, t0:t0 + n], qo[j][:, 0:n], [qo[j].r], [self.dr("qT", i)])
            self.pipe(items, [q0, q1, q2])
            if i + 1 < 9:
                prologue(i + 1)
        self.end_phase()

    def phase_attn(self, mod_next=None, skip_ctx=False):
        self.begin_phase()
        itc = [0]
        tasks = self.mod_tasks(mod_next, lambda: self.psum[2 * (itc[0] % 2)]) if mod_next is not None and mod_next < DEPTH else []
        qv = self.dram["qT"].ap().rearrange("(c p) n -> p c n", p=128)
        mv = self.xview("mixT")
        KT = self.psb("aKT", [128, T], BF16)
        V = self.psb("aV", [128, 34, 132], BF16)
        self.memset("pool", V[:], 0.0, [V.r])
        self.memset("pool", V[:, :, 128:129], 1.0, [V.r])
        qr = self.pring("aQ", 2, [128, 512], BF16)
        LA = 3
        pr = self.pring("aP", LA + 1, [128, 512], BF16)
        rinv = self.pring("arinv", 2, [128, 4], F32)
        otok = self.pring("aOt", 2, [128, 4, 128], BF16)
        ob = self.pring("aO", 2, [128, 512], BF16)
        scale = 128 ** -0.5
        allq = [self.dr("qT", i) for i in range(9)]
        allv = [self.dr("vtok", i) for i in range(9)]
        it = 0
        pi = 0
        for g in range(2):
            self.load(KT[:], qv[:, 8 + g, :], allq, [KT.r])
            self.load(V[:, :, 0:128], self.dram["vtok"].ap().rearrange("(c p) d -> p c d", p=128)[:, :, g * 128:(g + 1) * 128], allv, [V.r])
            for hq in range(4):
                hd = 4 * g + hq
                for i in range(8 if skip_ctx else 9):
                    n = 512 if i < 8 else CT
                    nsub = n // 128
                    t0 = i * 512
                    kcs = list(range(34)) if i < 8 else [32, 33]
                    q = qr[it % 2]
                    obk = [self.psum[2 * (it % 2)], self.psum[2 * (it % 2) + 1]]
                    ri, ot, o_ = rinv[it % 2], otok[it % 2], ob[it % 2]
                    it += 1
                    self.load(q[:, 0:n], qv[:, hd, t0:t0 + n], [self.dr("qT", i)], [q.r])
                    sbanks = {}
                    if tasks and it % 4 == 2:
                        itc[0] = it
                        tasks.pop(0)()

                    def issue_s(kc):
                        nonlocal pi
                        b_ = self.psum[4 + pi % (LA + 1)]
                        p_ = pr[pi % (LA + 1)]
                        pi += 1
                        self.mm(b_[:, 0:n], KT[:, kc * 128:(kc + 1) * 128], q[:, 0:n], True, True, [KT.r, q.r], [b_.r])
                        self.act(p_[:, 0:n], b_[:, 0:n], AF.Exp, [b_.r], [p_.r], scale=scale)
                        sbanks[kc] = p_
                    for kc in kcs[0:LA]:
                        issue_s(kc)
                    for idx, kc in enumerate(kcs):
                        if idx + LA < len(kcs):
                            issue_s(kcs[idx + LA])
                        p_ = sbanks.pop(kc)
                        first, last = idx == 0, idx == len(kcs) - 1
                        for s_ in range(nsub):
                            bk = obk[s_ // 2]
                            off = (s_ % 2) * 132
                            self.mm(bk[:, off:off + 129], p_[:, s_ * 128:(s_ + 1) * 128], V[:, kc, 0:129], first and s_ % 2 == 0, last,
                                    [p_.r, V.r], [bk.r], inc=(s_ == nsub - 1), skip_group_check=True)
                    for s_ in range(nsub):
                        bk = obk[s_ // 2]
                        off = (s_ % 2) * 132
                        self.S.op("dve", lambda hw, ri=ri, bk=bk, off=off, s_=s_: hw.reciprocal(ri[:, s_:s_ + 1], bk[:, off + 128:off + 129]),
                                  [bk.r], [ri.r])
                        self.ts("dve", ot[:, s_, :], bk[:, off:off + 128], ri[:, s_:s_ + 1], ALU.mult, [bk.r, ri.r], [ot.r])
                    tbk = obk[0]
                    tbv = tbk[:].bitcast(BF16)
                    for s_ in range(nsub):
                        self.tr(tbv[:, s_ * 128:(s_ + 1) * 128], ot[:, s_, :], self.ident[:], [ot.r, self.ident.r], [tbk.r, obk[1].r],
                                inc=(s_ == nsub - 1))
                    self.copy("act", o_[:, 0:n], tbv[:, 0:n], [tbk.r], [o_.r])
                    self.load(mv[:, hd, t0:t0 + n], o_[:, 0:n], [o_.r], [self.dr("mixT", i)])
        while tasks:
            tasks.pop(0)()
        self.end_phase()

    def pipe(self, items, stages):
        n, ns = len(items), len(stages)
        for step in range(n + ns - 1):
            for j in range(ns - 1, -1, -1):
                k = step - j
                if 0 <= k < n:
                    stages[j](items[k], k)

    def phase_ab_in(self, l, e, xin):
        self.begin_phase()
        wsrc, wres = self.wb("w_in_ab", e)
        W = self.psb("win", [128, 8, 3072], BF16)
        self.load(W[:], wsrc.rearrange("(k p) n -> p k n", p=128), wres, [W.r])
        CS = self.psb("ccs", [128, 256], BF16)
        self.load(CS[:], self.dram["c_cs"].ap(), [], [CS.r])
        cmask = self.psb("cmask", [128, 2, 128], I32)
        self.load(cmask[:], self.dram["c_mask"].ap(), [], [cmask.r])
        smask = self.psb("smask", [128, 512], F32)
        self.load(smask[:], self.dram["c_smask"].ap(), [], [smask.r])
        lg = self.psb("lg", [128, 2, 2, 4], F32)
        self.load(lg[:], self.dram["hglT"].ap(), [], [lg.r])
        lb = self.psb("lb", [128, 2, 4], F32)
        oml = self.psb("oml", [128, 2, 4], F32)
        lbm1 = self.psb("lbm1", [128, 2, 4], F32)
        if e == 0:
            self.memset("pool", lb[:], 0.0, [lb.r])
        else:
            self.tt("dve", lb[:], lg[:, 1], lg[:, 0], ALU.subtract, [lg.r], [lb.r])
            self.act(lb[:], lb[:], AF.Sigmoid, [lb.r], [lb.r])
        self.ts("dve", oml[:], lb[:], -1.0, ALU.mult, [lb.r], [oml.r], s2=1.0, op1=ALU.add)
        self.ts("dve", lbm1[:], lb[:], -1.0, ALU.add, [lb.r], [lbm1.r])
        xv = self.xview(xin)
        xt = self.psb("ax", [128, 8, 512], F32)
        hs_ = self.pring("ah", 2, [128, 8, 512], BF16)
        rstd = self.psb("arstd", [128, 512], F32)
        aT = self.pring("aT", 2, [128, 512], BF16)
        pq = self.pring("apq", 2, [128, 2, 256], BF16)
        gs = self.pring("ags", 2, [128, 512], BF16)
        vsb = self.psb("avs", [128, 4, 512], BF16)
        qs = self.psb("aqs", [128, 4, 512], F32)
        sig8 = self.psb("asig", [128, 8, 512], F32)
        R = 2
        lf = self.pring("alf", R, [128, 512], F32)
        kk = self.pring("akk", R + 1, [128, 512], F32)
        bb = self.pring("ab", R, [128, 512], F32)
        bm = self.pring("abm", R, [128, 512], F32)
        E1 = self.pring("aE1", R, [128, 512], F32)
        E2 = self.pring("aE2", R, [128, 512], F32)
        qt = [self.psb("aqt%d" % d, [128, 4, 512], BF16) for d in range(2)]
        kt = [self.psb("akt%d" % d, [128, 4, 512], BF16) for d in range(2)]
        kh = self.pring("akh", 2, [128, 512], BF16)
        khs = self.pring("akhs", 2, [128, 4, 128], BF16)
        dsb = self.pring("adsb", 2, [128, 512 // CH], F32)
        qib = self.pring("aqib", 2, [128, 512], BF16)
        attT = [self.pring("aatt%d" % d, 2, [128, 128], BF16) for d in range(2)]
        oi = self.psb("aoi", [128, 512], F32)
        for d in range(2):
            for a_ in attT[d]:
                self.memset("pool", a_[:], 0.0, [a_.r])
        PQv = self.dram["PQ"].ap()
        nl_, ncx_ = L // CH, CT // CH

        def prologue(i):
            n = 512 if i < 8 else CT
            self.load(xt[:, :, 0:n], xv[:, :, i * 512:i * 512 + n], [self.dr(xin, i)], [xt.r])
            self.norm_mod(xt, 0, n, rstd, hs_[i % 2], l, 0, 1, 0 if i < 8 else 1)
        prologue(0)
        ai = [0]
        for i in range(9):
            n = 512 if i < 8 else CT
            t0 = i * 512
            nsub = n // 128
            nch = n // CH
            h = hs_[i % 2]
            banks = {}

            def proj_to(key, c0, M=None):
                bk = self.bank()
                for k in range(8):
                    self.mm(bk[:, 0:n], W[:, k, c0:c0 + 128], h[:, k, 0:n], k == 0, k == 7, [W.r, h.r], [bk.r])
                banks[key] = bk

            items = [("a", g) for g in range(4)] + [("g", hd) for hd in range(4)] + [("v", s_) for s_ in range(nsub)] + \
                    [("q", hd) for hd in range(4)] + [("z", j) for j in range(8)]

            def p1s0(it, k):
                kind, j = it
                if kind == "a":
                    proj_to(it, j * 128)
                elif kind == "g":
                    proj_to(it, 2560 + j * 128)
                elif kind == "q":
                    proj_to(it, 512 + j * 128)
                elif kind == "z":
                    proj_to(it, 1024 + j * 128)
                else:
                    bk = self.bank()
                    for kx in range(8):
                        self.mm(bk[:, 0:512], h[:, kx, j * 128:(j + 1) * 128], W[:, kx, 2048:2560], kx == 0, kx == 7, [h.r, W.r], [bk.r])
                    banks[it] = bk

            def p1s1(it, k):
                kind, j = it
                bk = banks.pop(it)
                if kind == "a":
                    self.copy("dve", aT[j % 2][:, 0:n], bk[:, 0:n], [bk.r], [aT[j % 2].r])
                elif kind == "g":
                    self.act(sig8[:, j, 0:n], bk[:, 0:n], AF.Sigmoid, [bk.r], [sig8.r])
                    g_ = gs[j % 2]
                    self.tt("dve", g_[:, 0:n], bk[:, 0:n], sig8[:, j, 0:n], ALU.mult, [bk.r, sig8.r], [g_.r])
                    self.load(self.dram["gT"].ap()[j * 128:(j + 1) * 128, t0:t0 + n], g_[:, 0:n], [g_.r], [self.dr("gT", i)])
                elif kind == "q":
                    self.act(sig8[:, 4 + j, 0:n], bk[:, 0:n], AF.Sigmoid, [bk.r], [sig8.r])
                    self.tt("dve", qs[:, j, 0:n], bk[:, 0:n], sig8[:, 4 + j, 0:n], ALU.mult, [bk.r, sig8.r], [qs.r])
                elif kind == "z":
                    self.act(sig8[:, j, 0:n], bk[:, 0:n], AF.Sigmoid, [bk.r], [sig8.r])
                else:
                    self.copy("act", vsb[:, j, :], bk[:, 0:512], [bk.r], [vsb.r])
                    if j == nsub - 1:
                        self.load(self.dram["vtok2"].ap()[t0:t0 + n, :].rearrange("(s p) d -> p s d", p=128), vsb[:, 0:nsub, :], [vsb.r],
                                  [self.dr("vtok2", i)])

            def p1s2(it, k):
                kind, g = it
                if kind != "a":
                    return
                for s2 in range(0, nsub, 2):
                    b2 = self.bank()
                    ns2 = min(2, nsub - s2)
                    for u in range(ns2):
                        self.mm(b2[:, u * 256:(u + 1) * 256], aT[g % 2][:, (s2 + u) * 128:(s2 + u + 1) * 128], CS[:], True, True,
                                [aT[g % 2].r, CS.r], [b2.r], inc=(u == ns2 - 1))
                    p_ = pq[(s2 // 2) % 2]
                    self.copy("act", p_[:, 0:ns2, :], b2[:, 0:ns2 * 256].rearrange("p (u n) -> p u n", n=256), [b2.r], [p_.r])
                    self.load(PQv[t0 + s2 * 128:t0 + (s2 + ns2) * 128, g, :].rearrange("(u p) n -> p u n", p=128), p_[:, 0:ns2, :],
                              [p_.r], [self.dr("PQ", i)])
            self.pipe(items, [p1s0, p1s1, p1s2])

            if i + 1 < 9:
                prologue(i + 1)

            items2 = [(d, hd) for d in range(2) for hd in range(4)]

            def b3of(t):
                return t[:, 0:n].rearrange("p (c t) -> p c t", t=CH)

            def s1(it, k):
                d, hd = it
                j = d * 4 + hd
                r = k % R
                self.ts("dve", lf[r][:, 0:n], sig8[:, j, 0:n], oml[:, d, hd:hd + 1], ALU.mult, [sig8.r, oml.r, lb.r], [lf[r].r],
                        s2=lb[:, d, hd:hd + 1], op1=ALU.add)
                self.act(lf[r][:, 0:n], lf[r][:, 0:n], AF.Ln, [lf[r].r], [lf[r].r])
                kr = kk[k % (R + 1)]
                self.act(kr[:, 0:n], sig8[:, j, 0:n], AF.Identity, [sig8.r, lbm1.r, oml.r], [kr.r],
                         bias=oml[:, d, hd:hd + 1], scale=lbm1[:, d, hd:hd + 1])

            def s2(it, k):
                d, hd = it
                r = k % R
                b = bb[r]
                if d == 0:
                    bo_, lo_ = b[:, 0:n], lf[r][:, 0:n]
                else:
                    bo_, lo_ = b[:, 0:n][:, ::-1], lf[r][:, 0:n][:, ::-1]
                self.S.op("dve", lambda hw, bo_=bo_, lo_=lo_, sm_=smask[:, 0:n]: hw.tensor_tensor_scan(bo_, sm_, lo_, 0.0, ALU.mult, ALU.add),
                          [smask.r, lf[r].r], [b.r])
                b3 = b3of(b)
                self.tt("dve", b3of(bm[r]), b3, b3[:, :, CH // 2:CH // 2 + 1].broadcast_to([128, nch, CH]), ALU.subtract, [b.r], [bm[r].r])
                self.act(E1[r][:, 0:n], bm[r][:, 0:n], AF.Exp, [bm[r].r], [E1[r].r])
                self.act(E2[r][:, 0:n], bm[r][:, 0:n], AF.Exp, [bm[r].r], [E2[r].r], scale=-1.0)

            def s3(it, k):
                d, hd = it
                r = k % R
                b = bb[r]
                kr = kk[k % (R + 1)]
                li = CH - 1 if d == 0 else 0
                b3 = b3of(b)
                self.tt("dve", qt[d][:, hd, 0:n], qs[:, hd, 0:n], E1[r][:, 0:n], ALU.mult, [qs.r, E1[r].r], [qt[d].r])
                self.tt("pool", kt[d][:, hd, 0:n], kr[:, 0:n], E2[r][:, 0:n], ALU.mult, [kr.r, E2[r].r], [kt[d].r])
                self.tt("dve", b3of(bm[r]), b3[:, :, li:li + 1].broadcast_to([128, nch, CH]), b3, ALU.subtract, [b.r], [bm[r].r])
                self.act(E1[r][:, 0:n], bm[r][:, 0:n], AF.Exp, [bm[r].r], [E1[r].r])
                self.act(E2[r][:, 0:n], b[:, 0:n], AF.Exp, [b.r], [E2[r].r])
                ds_ = dsb[k % 2]
                c0_ = t0 // CH
                if d == 0:
                    p0_ = (ncx_ + c0_) if i < 8 else 0
                    self.act(ds_[:, 0:nch], b3[:, :, li], AF.Exp, [b.r], [ds_.r])
                else:
                    p0_ = (ncx_ + nl_ - c0_ - nch) if i < 8 else 0
                    self.act(ds_[:, 0:nch][:, ::-1], b3[:, :, li], AF.Exp, [b.r], [ds_.r])
                self.load(self.dram["dT"].ap()[d, hd * 128:(hd + 1) * 128, p0_:p0_ + nch], ds_[:, 0:nch], [ds_.r], [self.dr("dT", i)])
                self.load(self.dram["qtT"].ap()[d, hd * 128:(hd + 1) * 128, t0:t0 + n], qt[d][:, hd, 0:n], [qt[d].r], [self.dr("qtT", i)])

            def s4(it, k):
                d, hd = it
                r = k % R
                kr = kk[k % (R + 1)]
                kh_ = kh[k % 2]
                qi_ = qib[k % 2]
                self.tt("pool", kh_[:, 0:n], kr[:, 0:n], E1[r][:, 0:n], ALU.mult, [kr.r, E1[r].r], [kh_.r])
                self.tt("pool", qi_[:, 0:n], qs[:, hd, 0:n], E2[r][:, 0:n], ALU.mult, [qs.r, E2[r].r], [qi_.r])
                self.load(self.dram["qiT"].ap()[d, hd * 128:(hd + 1) * 128, t0:t0 + n], qi_[:, 0:n], [qi_.r], [self.dr("qiT", i)])
                tb = self.bank()
                tbv = tb[:].bitcast(BF16)
                for s_ in range(nsub):
                    self.tr(tbv[:, s_ * 128:(s_ + 1) * 128], kh_[:, s_ * 128:(s_ + 1) * 128], self.ident[:], [kh_.r, self.ident.r], [tb.r],
                            inc=(s_ == nsub - 1))
                k_ = khs[k % 2]
                self.copy("act", k_[:, 0:nsub, :], tbv[:, 0:nsub * 128].rearrange("p (s k) -> p s k", k=128), [tb.r], [k_.r])
                self.load(self.dram["khat"].ap()[d, t0:t0 + n, hd * 128:(hd + 1) * 128].rearrange("(s p) k -> p s k", p=128), k_[:, 0:nsub, :],
                          [k_.r], [self.dr("khat", i)])
            self.pipe(items2, [s1, s2, s3, s4])

            items3 = [(hd, s_) for hd in range(4) for s_ in range(nsub)]
            abk = {}

            def i0(it, k):
                hd, s_ = it
                sl = slice(s_ * 128, (s_ + 1) * 128)
                for d in range(2):
                    ab_ = self.psum[2 + self.psi % 6]
                    self.psi += 1
                    self.mm(ab_[:, 0:128], kt[d][:, hd, sl], qt[d][:, hd, sl], True, True, [kt[d].r, qt[d].r], [ab_.r])
                    abk[(it, d)] = ab_

            def i1(it, k):
                hd, s_ = it
                sl = slice(s_ * 128, (s_ + 1) * 128)
                obk = self.psum[hd % 2]
                ats = []
                for d in range(2):
                    ab_ = abk.pop((it, d))
                    a_ = attT[d][ai[0] % 2]
                    self.S.op("dve", lambda hw, a_=a_, ab_=ab_, d=d: hw.copy_predicated(a_[:], cmask[:, d, :], ab_[:, 0:128]),
                              [cmask.r, ab_.r], [a_.r])
                    ats.append(a_)
                ai[0] += 1
                self.mm(obk[:, sl], vsb[:, s_, hd * 128:(hd + 1) * 128], ats[0][:], True, False, [vsb.r, ats[0].r], [obk.r], inc=False)
                self.mm(obk[:, sl], vsb[:, s_, hd * 128:(hd + 1) * 128], ats[1][:], False, True, [vsb.r, ats[1].r], [obk.r], inc=True)
                if s_ == nsub - 1:
                    self.copy("act", oi[:, 0:n], obk[:, 0:n], [obk.r], [oi.r])
                    self.load(self.dram["ointra"].ap()[hd * 128:(hd + 1) * 128, t0:t0 + n], oi[:, 0:n], [oi.r], [self.dr("ointra", i)])
            self.pipe(items3, [i0, i1])
        self.end_phase()

    def phase_fourier(self, mod_next=None):
        self.begin_phase()
        fj = [0]
        tasks = self.mod_tasks(mod_next, lambda: self.psum[4 if fj[0] % 2 == 0 else 0]) if mod_next is not None and mod_next < DEPTH else []
        PQ = self.psb("fPQ", [128, 34, 4, 256], BF16)
        allpq = [self.dr("PQ", i) for i in range(9)]
        pv = self.dram["PQ"].ap().rearrange("(c p) g n -> p c (g n)", p=128)
        for c0 in range(0, 34, 2):
            self.load(PQ[:, c0:c0 + 2].rearrange("p c g n -> p c (g n)"), pv[:, c0:c0 + 2, :], allpq, [PQ.r])
        cr = self.pring("fC", 3, [128, 2, 4, 512], BF16)
        yo = self.pring("fy", 2, [128, 512], BF16)
        dft = self.dram["c_dft"].ap()
        mv = self.xview("mixT")
        ci = 0
        for j in range(8):
            banks = [self.psum[g] for g in range(4)] if j % 2 == 0 else [self.psum[4 + g] for g in range(4)]
            for tq in range(8):
                if tasks and (j * 8 + tq) % 4 == 1:
                    fj[0] = j
                    tasks.pop(0)()
                c_ = cr[ci % 3]
                ci += 1
                for z in range(2):
                    self.load(c_[:, z], dft[z, tq * 512:(tq + 1) * 512, j * 512:(j + 1) * 512].rearrange("(t p) n -> p t n", p=128),
                              [], [c_.r])
                for t4 in range(4):
                    tc = tq * 4 + t4
                    for g in range(4):
                        self.mm(banks[g][:, 0:512], PQ[:, tc, g, 0:128], c_[:, 0, t4, :], tc == 0, False, [PQ.r, c_.r], [banks[g].r], inc=False)
                        self.mm(banks[g][:, 0:512], PQ[:, tc, g, 128:256], c_[:, 1, t4, :], False, tc == 31, [PQ.r, c_.r], [banks[g].r],
                                inc=(tc == 31 or (t4 == 3 and g == 3)))
            for g in range(4):
                y = yo[g % 2]
                self.copy("act" if g % 2 == 0 else "dve", y[:], banks[g][:, 0:512], [banks[g].r], [y.r])
                self.load(mv[:, g, j * 512:(j + 1) * 512], y[:], [y.r], [self.dr("mixT", j)])
        while tasks:
            tasks.pop(0)()
        c2 = self.psb("fC2", [128, 2, 2, 256], BF16)
        for z in range(2):
            self.load(c2[:, z], self.dram["c_dft256"].ap()[z].rearrange("(c p) n -> p c n", p=128), [], [c2.r])
        for g in range(4):
            bk = self.bank()
            for tc in range(2):
                self.mm(bk[:, 0:256], PQ[:, 32 + tc, g, 0:128], c2[:, 0, tc, :], tc == 0, False, [PQ.r, c2.r], [bk.r], inc=False)
                self.mm(bk[:, 0:256], PQ[:, 32 + tc, g, 128:256], c2[:, 1, tc, :], False, tc == 1, [PQ.r, c2.r], [bk.r], inc=(tc == 1))
            y = yo[g % 2]
            self.copy("act", y[:, 0:256], bk[:, 0:256], [bk.r], [y.r])
            self.load(mv[:, g, L:L + CT], y[:, 0:256], [y.r], [self.dr("mixT", 8)])
        self.end_phase()

    def phase_hgrn_scan(self, e):
        self.begin_phase()
        nl, ncx = L // CH, CT // CH
        PC = 8
        NP = NCH // PC
        gn = self.psb("hgn", [128, 4], F32)
        self.load(gn[:], self.dram["gnT"].ap()[e], [], [gn.r])
        oacc = self.psb("hoacc", [128, 4, T], F32)
        S32 = self.psb("hS32", [128, 8, 128], F32)
        Sbf = self.pring("hSbf", 8, [128, 128], BF16)
        DD = self.psb("hDD", [128, 8, NCH], F32)
        KHr = self.pring("hKH", 2, [CH, 8, PC, 128], BF16)
        VHr = self.pring("hVH", 2, [CH, 8, PC, 128], BF16)
        QIr = self.pring("hQI", 2, [128, 8, PC * CH], BF16)
        sq = self.psb("hsq", [128, 512], BF16)
        rstd = self.psb("hrstd", [128, 512], F32)
        tmp = self.psb("htmp", [128, 512], F32)
        gsb = self.pring("hgs", 2, [128, 512], BF16)
        ob = self.pring("hob", 2, [128, 512], BF16)
        mv = self.xview("mixT")
        al = lambda nm: [self.dr(nm, i) for i in range(9)]
        S32r = [self.S.res("S32_%d" % c_) for c_ in range(8)]
        oar = [self.S.res("oacc_%d" % c_) for c_ in range(4)]
        self.memset("pool", S32[:], 0.0, S32r)
        for ch in range(8):
            self.memset("pool", Sbf[ch][:], 0.0, [Sbf[ch].r])
        for hd in range(4):
            self.load(oacc[:, hd, :], self.dram["ointra"].ap()[hd * 128:(hd + 1) * 128, :], al("ointra"), [oar[hd]])
        for ch in range(8):
            d, hd = ch // 4, ch % 4
            self.load(DD[:, ch, :], self.dram["dT"].ap()[d, hd * 128:(hd + 1) * 128, :], al("dT"), [DD.r])
        kv = self.dram["khat"].ap()
        vv = self.dram["vtok2"].ap()
        qiv = self.dram["qiT"].ap()

        def chunk_range(d, j):
            if j == 0:
                return nl
            return (j - 1) * PC if d == 0 else nl - j * PC

        def load_piece(j):
            kh, vh, qi = KHr[j % 2], VHr[j % 2], QIr[j % 2]
            for ch in range(8):
                d, hd = ch // 4, ch % 4
                c0 = chunk_range(d, j)
                ts_ = slice(c0 * CH, (c0 + PC) * CH)
                hs = slice(hd * 128, (hd + 1) * 128)
                self.load(kh[:, ch], kv[d, ts_, hs].rearrange("(c p) k -> p c k", p=CH), al("khat"), [kh.r])
                self.load(vh[:, ch], vv[ts_, hs].rearrange("(c p) k -> p c k", p=CH), al("vtok2"), [vh.r])
                self.load(qi[:, ch, :], qiv[d, hs, ts_], al("qiT"), [qi.r])

        def bidx(d, q):
            return q if d == 0 else PC - 1 - q

        def emit_U(p):
            j, q = p // PC, p % PC
            kh, vh = KHr[j % 2], VHr[j % 2]
            for half in range(2):
                ub = self.psum[4 + 2 * (p % 2) + half]
                for c4 in range(4):
                    ch = half * 4 + c4
                    ix = bidx(ch // 4, q)
                    self.mm(ub[:, c4 * 128:(c4 + 1) * 128], kh[:, ch, ix, :], vh[:, ch, ix, :], True, True, [kh.r, vh.r], [ub.r],
                            inc=(c4 == 3))

        load_piece(0)
        emit_U(0)
        for p in range(NCH):
            j, q = p // PC, p % PC
            if q == 0 and j + 1 < NP:
                load_piece(j + 1)
            if p + 1 < NCH:
                emit_U(p + 1)
            qi = QIr[j % 2]
            for ch in range(8):
                d = ch // 4
                ix = bidx(d, q)
                ib = self.psum[ch // 2]
                col = (ch % 2) * 256 + ix * CH
                self.mm(ib[:, col:col + CH], Sbf[ch][:], qi[:, ch, ix * CH:(ix + 1) * CH], True, True, [Sbf[ch].r, qi.r], [ib.r],
                        inc=(ch % 2 == 1))
            if q == PC - 1:
                for ch in range(8):
                    d, hd = ch // 4, ch % 4
                    c0 = chunk_range(d, j)
                    ib = self.psum[ch // 2]
                    cb_ = (ch % 2) * 256
                    dst = oacc[:, hd, c0 * CH:(c0 + PC) * CH]
                    self.tt("dve", dst, dst, ib[:, cb_:cb_ + PC * CH], ALU.add, [oar[hd], ib.r], [oar[hd]])
            for ch in range(8):
                ub = self.psum[4 + 2 * (p % 2) + ch // 4]
                c4 = ch % 4
                self.stt(S32[:, ch, :], S32[:, ch, :], DD[:, ch, p:p + 1], ub[:, c4 * 128:(c4 + 1) * 128], ALU.mult, ALU.add,
                         [S32r[ch], DD.r, ub.r], [S32r[ch]])
            if p + 1 < NCH:
                for ch in range(8):
                    self.copy("act", Sbf[ch][:], S32[:, ch, :], [S32r[ch]], [Sbf[ch].r])
        for hd in range(4):
            hs = slice(hd * 128, (hd + 1) * 128)
            for i in range(9):
                n = 512 if i < 8 else CT
                t0 = i * 512
                g_ = gsb[i % 2]
                o_ = ob[i % 2]
                self.load(g_[:, 0:n], self.dram["gT"].ap()[hs, t0:t0 + n], al("gT"), [g_.r])
                self.head_rstd(oacc[:, hd, t0:t0 + n], [oar[hd]], n, sq, rstd)
                self.stt(tmp[:, 0:n], oacc[:, hd, t0:t0 + n], gn[:, hd:hd + 1], rstd[:, 0:n], ALU.mult, ALU.mult, [oar[hd], gn.r, rstd.r], [tmp.r])
                self.tt("pool", o_[:, 0:n], tmp[:, 0:n], g_[:, 0:n], ALU.mult, [tmp.r, g_.r], [o_.r])
                self.load(mv[:, 4 + hd, t0:t0 + n], o_[:, 0:n], [o_.r], [self.dr("mixT", i)])
        self.end_phase()

    def finish(self):
        self.S.wait_all("sp", self.out_res)
        self.S.emit()
        return self.nc


def bf(a):
    return np.asarray(a, dtype=np.float32).astype(ml_dtypes.bfloat16)


def host_consts():
    c = {}
    c["c_ident"] = bf(np.eye(128))
    return c


def prep_core(b, inp, consts):
    f = lambda a: np.ascontiguousarray(np.asarray(a, dtype=np.float32))
    m = dict(consts)
    m["xT"] = f(np.concatenate([inp["x"][b].T, inp["ctx"][b].T], axis=1))
    cc = np.stack([inp["c"][b], inp["c_ctx"]], axis=1)
    m["cT"] = f(cc.reshape(8, 128, 2).transpose(1, 0, 2))
    m["bmodT"] = f(inp["b_mod"].reshape(DEPTH, 48, 128).transpose(2, 0, 1))
    m["w_mod"] = f(inp["w_mod"])
    m["fngT"] = f(inp["final_norm_g"].reshape(8, 128).T)
    return m


def prep_weights(inp):
    f = lambda a: np.ascontiguousarray(np.asarray(a, dtype=np.float32))
    m = {}
    for k in ("w_mod", "w_in_ab", "w_out_ab", "w_qkv", "w_out_att", "w_up", "w_down"):
        m[k] = f(inp[k])
    m["bmodT"] = f(inp["b_mod"].reshape(DEPTH, 48, 128).transpose(2, 0, 1))
    m["fngT"] = f(inp["final_norm_g"].reshape(8, 128).T)
    cwt = np.asarray(inp["conv_w"]).reshape(DEPTH, 9, NFF, 128).transpose(0, 3, 2, 1)
    m["convwT"] = f(cwt)
    m["convbT"] = f(np.asarray(inp["conv_b"]).reshape(DEPTH, NFF, 128).transpose(0, 2, 1))
    return m


def rope_tables():
    t = np.arange(L)
    row = (t // GRID).astype(np.float32)
    colp = (t % GRID).astype(np.float32)
    nf = 32
    freqs = (10000.0 ** (-np.arange(nf, dtype=np.float32) / nf)).astype(np.float32)
    ang = np.concatenate([row[:, None] * freqs, colp[:, None] * freqs], axis=-1)
    cos = np.repeat(np.cos(ang), 2, axis=1).T
    sin = np.repeat(np.sin(ang), 2, axis=1).T
    perm = np.zeros((128, 128), np.float32)
    for dp in range(128):
        if dp % 2 == 0:
            perm[dp + 1, dp] = -1.0
        else:
            perm[dp - 1, dp] = 1.0
    return np.ascontiguousarray(cos, np.float32), np.ascontiguousarray(sin, np.float32), bf(perm)


def ab_consts():
    c = {}
    ch = np.arange(128)
    ang = 2 * np.pi * np.outer(ch, ch) / 128.0
    c["c_cs"] = bf(np.concatenate([np.cos(ang), -np.sin(ang)], axis=1) / np.sqrt(128.0))
    t = np.arange(L, dtype=np.int64)
    m = np.outer(t, t) % L
    a = (2 * np.pi / L) * m
    c["c_dft"] = np.stack([bf(np.cos(a) / 64.0), bf(np.sin(a) / 64.0)])
    t2 = np.arange(CT, dtype=np.int64)
    a2 = (2 * np.pi / CT) * (np.outer(t2, t2) % CT)
    c["c_dft256"] = np.stack([bf(np.cos(a2) / 16.0), bf(np.sin(a2) / 16.0)])
    s_ = np.arange(128)[:, None]
    t_ = np.arange(128)[None, :]
    same = (s_ // CH) == (t_ // CH)
    mk = np.stack([(same & (s_ <= t_)), (same & (s_ >= t_))], axis=1).astype(np.int32)
    c["c_mask"] = np.ascontiguousarray(mk)
    sm = np.ones((128, 512), np.float32)
    sm[:, ::CH] = 0.0
    c["c_smask"] = sm
    return c


def build_full():
    K = KB()
    K.setup_consts()
    K.din('xT', [D, T])
    K.din('convwT', [DEPTH, 128, NFF, 9]); K.din('convbT', [DEPTH, 128, NFF])
    K.din('qkgT', [2, 128, 2]); K.din('c_perm', [128, 128], BF16); K.din('cosT', [128, L]); K.din('sinT', [128, L])
    K.din('c_cs', [128, 256], BF16); K.din('c_dft', [2, L, L], BF16); K.din('c_dft256', [2, CT, CT], BF16)
    K.din('c_mask', [128, 2, 128], I32); K.din('c_smask', [128, 512]); K.din('hglT', [128, 2, 2, 4]); K.din('gnT', [2, 128, 4])
    K.dscratch('xa', [D, T], F32); K.dscratch('xb', [D, T], F32); K.dscratch('mixT', [D, T], BF16)
    K.dscratch('qT', [1280, T], BF16); K.dscratch('vtok', [T, 256], BF16)
    K.dscratch('PQ', [T, 4, 256], BF16); K.dscratch('gT', [512, T], BF16); K.dscratch('vtok2', [T, 512], BF16)
    K.dscratch('qtT', [2, 512, T], BF16); K.dscratch('khat', [2, T, 512], BF16)
    K.dscratch('dT', [2, 512, NCH], F32); K.dscratch('qiT', [2, 512, T], BF16); K.dscratch('ointra', [512, T], F32)
    def worder(l):
        o = [('w_in_ab', l // 2), ('w_out_ab', l // 2)] if l % 2 == 0 else [('w_qkv', l // 2), ('w_out_att', l // 2)]
        return o + [('w_up', l), ('w_down', l)]
    K.wcast_setup([('w_in_ab', [2, D, 3072]), ('w_out_ab', [2, D, D]), ('w_up', [DEPTH, D, 2 * DFF]), ('w_down', [DEPTH, DFF, D]),
                   ('w_qkv', [2, D, 1536]), ('w_out_att', [2, D, D])])
    K.wcast_issue(worder(0)[0:2])
    K.mod_setup()
    K.phase_mod0()
    K.wcast_issue(worder(0)[2:4])
    for l in range(DEPTH):
        xin = 'xT' if l == 0 else 'xa'
        if l % 2 == 0:
            K.phase_ab_in(l, l // 2, xin)
            K.phase_fourier(l + 1)
            K.phase_hgrn_scan(l // 2)
            K.phase_out(l, 'w_out_ab', l // 2, xin, 'xb')
        else:
            K.phase_qkv(l, l // 2, xin)
            K.phase_attn(l + 1, skip_ctx=(l == DEPTH - 1))
            K.phase_out(l, 'w_out_att', l // 2, xin, 'xb', skip_ctx=(l == DEPTH - 1))
        K.phase_ffn(l, 'xb', 'xa', bg=(K.wcast_tasks(worder(l + 1)) if l + 1 < DEPTH else None), skip_ctx=(l == DEPTH - 1))
    K.phase_final('xa')
    nc = K.finish()
    return nc, K


def kernel(**inputs):
    inp = {k: np.asarray(v) for k, v in inputs.items()}
    nc, K = build_full()
    shared = dict(host_consts())
    shared.update(prep_weights(inp))
    shared.update(ab_consts())
    cos, sin, perm = rope_tables()
    shared['cosT'] = cos; shared['sinT'] = sin; shared['c_perm'] = perm
    shared['qkgT'] = np.ascontiguousarray(np.stack([inp['q_norm_g'], inp['k_norm_g']], axis=2).astype(np.float32))
    shared['hglT'] = np.ascontiguousarray(inp['hg_lb_logits'].reshape(2, 2, 4, 128).transpose(3, 0, 1, 2).astype(np.float32))
    shared['gnT'] = np.ascontiguousarray(inp['hg_norm_g'].reshape(2, 4, 128).transpose(0, 2, 1).astype(np.float32))
    in_maps = []
    for b in range(8):
        m = dict(shared)
        m.update(prep_core(b, inp, {}))
        in_maps.append({k: v for k, v in m.items() if k in K.dram})
    res = run_bass_kernel_spmd(nc, in_maps, core_ids=list(range(8)))
    out = np.stack([np.ascontiguousarray(r['outT'].T) for r in res.results], axis=0)
    return out.astype(np.float32)
```
